# Optimizing a Trainium2 kernel written in Bass

```python
import math
import jax
import jax.numpy as jnp
from jax import lax
import numpy as np

D_MODEL = 2048
BATCH = 1
SEQ = 8192
DEPTH = 2

CTX_LEN = 256
GRID_W = 64
N_MIXERS = 2
N_RWKV = (DEPTH + N_MIXERS - 1) // N_MIXERS
N_HYENA = DEPTH // N_MIXERS
RWKV_HEAD = 64
RWKV_HEADS = D_MODEL // RWKV_HEAD
DECAY_LORA = 96
ICLR_LORA = 96
GATE_LORA = 256
GN_EPS = 64e-5
L2_EPS = 1e-12
HYENA_EMB = 33
HYENA_WIDTH = 64
HYENA_TARGET = 1e-2
HYENA_PCT_SHORT = 0.3
HYENA_PCT_LONG = 1.5
HYENA_MAX_DECAY = math.log(HYENA_TARGET) / HYENA_PCT_SHORT
HYENA_MIN_DECAY = math.log(HYENA_TARGET) / HYENA_PCT_LONG
FFN_HIDDEN = int(math.ceil(8 * D_MODEL / 3 / 256)) * 256
RMS_EPS = 1e-6
F32 = jnp.float32

kernel_name = "hybrid_rwkv7_hyena_dit_prefix"


def rms_norm(x, g):
    x32 = x.astype(F32)
    y = x32 * lax.rsqrt(jnp.mean(x32 * x32, axis=-1, keepdims=True) + RMS_EPS)
    return y.astype(x.dtype) * g


def modulate(h, shift, scale):
    return h * (1 + scale) + shift


def swiglu(h, w1, w3, w2):
    return (jax.nn.silu(h @ w1) * (h @ w3)) @ w2


def seq_shift(h):
    half = h.shape[-1] // 2
    prev = jnp.pad(h[:, :-1, :half], ((0, 0), (1, 0), (0, 0)))
    nxt = jnp.pad(h[:, 1:, half:], ((0, 0), (0, 1), (0, 0)))
    return jnp.concatenate([prev, nxt], axis=-1)


def grid_shift(h):
    B, L, D = h.shape
    rows = L // GRID_W
    g = h.reshape(B, rows, GRID_W, D)
    q = D // 4
    left = jnp.pad(g[:, :, :-1, :q], ((0, 0), (0, 0), (1, 0), (0, 0)))
    right = jnp.pad(g[:, :, 1:, q:2 * q], ((0, 0), (0, 0), (0, 1), (0, 0)))
    up = jnp.pad(g[:, :-1, :, 2 * q:3 * q], ((0, 0), (1, 0), (0, 0), (0, 0)))
    down = jnp.pad(g[:, 1:, :, 3 * q:], ((0, 0), (0, 1), (0, 0), (0, 0)))
    return jnp.concatenate([left, right, up, down], axis=-1).reshape(B, L, D)


def rwkv7_inputs(h, hs, p):
    B, L, D = h.shape
    H, N = RWKV_HEADS, RWKV_HEAD
    xx = hs - h
    xr, xw, xk, xv, xa, xg = [h + xx * p["mu"][m] for m in range(6)]
    r = (xr @ p["w_r"]).reshape(B, L, H, N).astype(F32)
    k = (xk @ p["w_k"]).reshape(B, L, H, N).astype(F32)
    v = (xv @ p["w_v"]).reshape(B, L, H, N).astype(F32)
    kk = k * p["k_k"].reshape(H, N)
    kk = kk / jnp.maximum(jnp.linalg.norm(kk, axis=-1, keepdims=True), L2_EPS)
    dirs = []
    for d in range(2):
        w_log = -jax.nn.softplus(-(p["w0"][d] + jnp.tanh(xw @ p["w1"][d]) @ p["w2"][d]).astype(F32)) - 0.5
        decay = jnp.exp(-jnp.exp(w_log)).reshape(B, L, H, N)
        a = jax.nn.sigmoid((p["a0"][d] + (xa @ p["a1"][d]) @ p["a2"][d]).astype(F32)).reshape(B, L, H, N)
        k_d = k * (1 + (a - 1) * p["k_a"].reshape(H, N))
        dirs.append((decay, k_d, kk * a))
    return {"r": r, "v": v, "kk": kk, "xg": xg, "dirs": dirs}


def wkv7_scan(r, decay, k, v, kk, b, s0, reverse):
    def step(s, inp):
        r_t, w_t, k_t, v_t, kk_t, b_t = inp
        sa = jnp.einsum("bhvk,bhk->bhv", s, -kk_t)
        s = s * w_t[:, :, None, :] + sa[..., None] * b_t[:, :, None, :] + v_t[..., None] * k_t[:, :, None, :]
        return s, jnp.einsum("bhvk,bhk->bhv", s, r_t)
    xs = tuple(jnp.moveaxis(t, 1, 0) for t in (r, decay, k, v, kk, b))
    s_fin, ys = lax.scan(step, s0, xs, reverse=reverse)
    return jnp.moveaxis(ys, 0, 1), s_fin


def rwkv7_readout(y, q, p, dtype):
    B, L, H, N = y.shape
    mean = jnp.mean(y, axis=-1, keepdims=True)
    var = jnp.mean(jnp.square(y - mean), axis=-1, keepdims=True)
    yn = ((y - mean) * lax.rsqrt(var + GN_EPS)).reshape(B, L, H * N) * p["lnx_w"] + p["lnx_b"]
    k_both = q["dirs"][0][1] + q["dirs"][1][1]
    bonus = jnp.sum(q["r"] * k_both * p["r_k"], axis=-1, keepdims=True) * q["v"]
    g = jax.nn.sigmoid(q["xg"] @ p["g1"]) @ p["g2"]
    return ((yn + bonus.reshape(B, L, H * N)) * g).astype(dtype) @ p["w_o"]


def rwkv7_mixer(hc, hl, p, ctx_out):
    B = hl.shape[0]
    qc = rwkv7_inputs(hc, seq_shift(hc), p)
    ql = rwkv7_inputs(hl, grid_shift(hl), p)
    s_zero = jnp.zeros((B, RWKV_HEADS, RWKV_HEAD, RWKV_HEAD), F32)
    yc_sum = 0.0
    yl_sum = 0.0
    for d, reverse in enumerate((False, True)):
        decay_c, k_c, b_c = qc["dirs"][d]
        yc, s_c = wkv7_scan(qc["r"], decay_c, k_c, qc["v"], qc["kk"], b_c, s_zero, reverse)
        decay_l, k_l, b_l = ql["dirs"][d]
        yl, _ = wkv7_scan(ql["r"], decay_l, k_l, ql["v"], ql["kk"], b_l, s_c, reverse)
        yc_sum = yc_sum + yc
        yl_sum = yl_sum + yl
    out_l = rwkv7_readout(yl_sum, ql, p, hl.dtype)
    out_c = rwkv7_readout(yc_sum, qc, p, hc.dtype) if ctx_out else None
    return out_l, out_c


def centred_conv3(u, w, b):
    up = jnp.pad(u, ((0, 0), (1, 1), (0, 0)))
    return up[:, :-2] * w[0] + up[:, 1:-1] * w[1] + up[:, 2:] * w[2] + b


def hyena_filter(L, p):
    t = jnp.linspace(0.0, 1.0, L, dtype=F32)[:, None]
    bands = (HYENA_EMB - 1) // 2
    w = 2 * math.pi * jnp.arange(L, dtype=F32)[:, None] / L
    f = jnp.linspace(1e-4, bands - 1, bands, dtype=F32)[None, :]
    z = jnp.concatenate([t, jnp.cos(f * w), -jnp.sin(f * w)], axis=-1)
    act = lambda u: jnp.sin(p["freq"] * u)
    hid = act(z @ p["f_w0"] + p["f_b0"])
    hid = act(hid @ p["f_w1"] + p["f_b1"])
    hid = act(hid @ p["f_w2"] + p["f_b2"])
    h = (hid @ p["f_wout"]).astype(F32)
    deltas = jnp.abs(jnp.linspace(HYENA_MIN_DECAY, HYENA_MAX_DECAY, D_MODEL, dtype=F32))
    window = jnp.exp(-t * deltas)
    h_fwd = h[:, :D_MODEL] * window
    h_bwd = h[:, D_MODEL:] * window
    return jnp.concatenate([h_fwd, jnp.zeros((1, D_MODEL), F32), h_bwd[:0:-1]], axis=0)


def long_conv(u, filt):
    L = u.shape[1]
    n = 2 * L
    u_f = jnp.fft.rfft(u.astype(F32), n=n, axis=1)
    f_f = jnp.fft.rfft(filt, n=n, axis=0)
    return jnp.fft.irfft(u_f * f_f[None], n=n, axis=1)[:, :L].astype(u.dtype)


def hyena_mixer(h, p):
    L = h.shape[1]
    u = centred_conv3(h @ p["in_w"] + p["in_b"], p["short_w"], p["short_b"])
    x0, x1, v = jnp.split(u, 3, axis=-1)
    v = v * x1
    v = long_conv(v, hyena_filter(L, p)) + v * p["bias"]
    return (v * x0) @ p["out_w"] + p["out_b"]


def setup_inputs(seed: int = 0) -> dict:
    key = jax.random.key(seed)
    ks = iter(jax.random.split(key, 64))
    nrm = lambda shape, std: jax.random.normal(next(ks), shape, F32) * std
    uni = lambda shape, lo, hi: jax.random.uniform(next(ks), shape, F32, lo, hi)
    D, F, H, N = D_MODEL, FFN_HIDDEN, RWKV_HEADS, RWKV_HEAD
    A, Bh, W = N_RWKV, N_HYENA, HYENA_WIDTH
    return {
        "x": nrm((BATCH, SEQ, D), 1.0),
        "c": nrm((BATCH, D), 1.0),
        "ctx": nrm((BATCH, CTX_LEN, D), 1.0),
        "c_ctx": nrm((D,), 1.0),
        "ada_w": nrm((DEPTH, D, 6 * D), D ** -0.5),
        "ada_b": nrm((DEPTH, 6 * D), 0.02),
        "norm1_g": 1.0 + nrm((DEPTH, D), 0.02),
        "norm2_g": 1.0 + nrm((DEPTH, D), 0.02),
        "ffn_w1": nrm((DEPTH, D, F), D ** -0.5),
        "ffn_w3": nrm((DEPTH, D, F), D ** -0.5),
        "ffn_w2": nrm((DEPTH, F, D), F ** -0.5),
        "rw_mu": uni((A, 6, D), 0.0, 1.0),
        "rw_w_r": nrm((A, D, D), D ** -0.5),
        "rw_w_k": nrm((A, D, D), D ** -0.5),
        "rw_w_v": nrm((A, D, D), D ** -0.5),
        "rw_w_o": nrm((A, D, D), D ** -0.5),
        "rw_w0": uni((A, 2, D), -6.0, -1.0),
        "rw_w1": nrm((A, 2, D, DECAY_LORA), D ** -0.5),
        "rw_w2": nrm((A, 2, DECAY_LORA, D), 0.1 * DECAY_LORA ** -0.5),
        "rw_a0": nrm((A, 2, D), 0.1),
        "rw_a1": nrm((A, 2, D, ICLR_LORA), D ** -0.5),
        "rw_a2": nrm((A, 2, ICLR_LORA, D), 0.1 * ICLR_LORA ** -0.5),
        "rw_g1": nrm((A, D, GATE_LORA), D ** -0.5),
        "rw_g2": nrm((A, GATE_LORA, D), GATE_LORA ** -0.5),
        "rw_k_k": 0.85 + nrm((A, D), 0.02),
        "rw_k_a": 1.0 + nrm((A, D), 0.02),
        "rw_r_k": nrm((A, H, N), 0.1),
        "rw_lnx_w": 1.0 + nrm((A, D), 0.02),
        "rw_lnx_b": nrm((A, D), 0.02),
        "hy_in_w": nrm((Bh, D, 3 * D), D ** -0.5),
        "hy_in_b": nrm((Bh, 3 * D), 0.02),
        "hy_short_w": nrm((Bh, 3, 3 * D), 3 ** -0.5),
        "hy_short_b": nrm((Bh, 3 * D), 0.02),
        "hy_f_w0": nrm((Bh, HYENA_EMB, W), HYENA_EMB ** -0.5),
        "hy_f_b0": nrm((Bh, W), 0.1),
        "hy_f_w1": nrm((Bh, W, W), W ** -0.5),
        "hy_f_b1": nrm((Bh, W), 0.1),
        "hy_f_w2": nrm((Bh, W, W), W ** -0.5),
        "hy_f_b2": nrm((Bh, W), 0.1),
        "hy_f_freq": 1.0 + nrm((Bh, W), 0.02),
        "hy_f_wout": nrm((Bh, W, 2 * D), 0.005),
        "hy_bias": nrm((Bh, D), 1.0),
        "hy_out_w": nrm((Bh, D, D), D ** -0.5),
        "hy_out_b": nrm((Bh, D), 0.02),
        "final_g": 1.0 + nrm((D,), 0.02),
    }


def reference(x, c, ctx, c_ctx, ada_w, ada_b, norm1_g, norm2_g, ffn_w1, ffn_w3, ffn_w2,
              rw_mu, rw_w_r, rw_w_k, rw_w_v, rw_w_o, rw_w0, rw_w1, rw_w2, rw_a0, rw_a1, rw_a2,
              rw_g1, rw_g2, rw_k_k, rw_k_a, rw_r_k, rw_lnx_w, rw_lnx_b,
              hy_in_w, hy_in_b, hy_short_w, hy_short_b, hy_f_w0, hy_f_b0, hy_f_w1, hy_f_b1,
              hy_f_w2, hy_f_b2, hy_f_freq, hy_f_wout, hy_bias, hy_out_w, hy_out_b, final_g):
    s_lat = jax.nn.silu(c)
    s_ctx = jax.nn.silu(c_ctx)[None]
    xl, xc = x, ctx
    for i in range(DEPTH):
        is_rwkv = i % N_MIXERS == 0
        j = i // N_MIXERS
        ctx_later = any(k % N_MIXERS == 0 for k in range(i + 1, DEPTH))
        need_ctx = is_rwkv or ctx_later
        sh1, sc1, gt1, sh2, sc2, gt2 = jnp.split((s_lat @ ada_w[i] + ada_b[i])[:, None, :], 6, axis=-1)
        hl = modulate(rms_norm(xl, norm1_g[i]), sh1, sc1)
        if need_ctx:
            csh1, csc1, cgt1, csh2, csc2, cgt2 = jnp.split((s_ctx @ ada_w[i] + ada_b[i])[:, None, :], 6, axis=-1)
            hc = modulate(rms_norm(xc, norm1_g[i]), csh1, csc1)
        if is_rwkv:
            p = {"mu": rw_mu[j], "w_r": rw_w_r[j], "w_k": rw_w_k[j], "w_v": rw_w_v[j], "w_o": rw_w_o[j],
                 "w0": rw_w0[j], "w1": rw_w1[j], "w2": rw_w2[j], "a0": rw_a0[j], "a1": rw_a1[j],
                 "a2": rw_a2[j], "g1": rw_g1[j], "g2": rw_g2[j], "k_k": rw_k_k[j], "k_a": rw_k_a[j],
                 "r_k": rw_r_k[j], "lnx_w": rw_lnx_w[j], "lnx_b": rw_lnx_b[j]}
            yl, yc = rwkv7_mixer(hc, hl, p, ctx_later)
        else:
            p = {"in_w": hy_in_w[j], "in_b": hy_in_b[j], "short_w": hy_short_w[j], "short_b": hy_short_b[j],
                 "f_w0": hy_f_w0[j], "f_b0": hy_f_b0[j], "f_w1": hy_f_w1[j], "f_b1": hy_f_b1[j],
                 "f_w2": hy_f_w2[j], "f_b2": hy_f_b2[j], "freq": hy_f_freq[j], "f_wout": hy_f_wout[j],
                 "bias": hy_bias[j], "out_w": hy_out_w[j], "out_b": hy_out_b[j]}
            yl = hyena_mixer(hl, p)
            yc = hyena_mixer(hc, p) if ctx_later else None
        xl = xl + gt1 * yl
        xl = xl + gt2 * swiglu(modulate(rms_norm(xl, norm2_g[i]), sh2, sc2), ffn_w1[i], ffn_w3[i], ffn_w2[i])
        if ctx_later:
            xc = xc + cgt1 * yc
            xc = xc + cgt2 * swiglu(modulate(rms_norm(xc, norm2_g[i]), csh2, csc2), ffn_w1[i], ffn_w3[i], ffn_w2[i])
    return rms_norm(xl, final_g)
```

```python
import numpy as np
import concourse.bass as bass
import concourse.mybir as mybir
from concourse.bass_utils import run_bass_kernel_spmd

F32 = mybir.dt.float32
ALU = mybir.AluOpType
AF = mybir.ActivationFunctionType
AX = mybir.AxisListType

NCORES = 8
D = 2048
SEQ = 8192
CTX = 256
FF = 5632
ND = D // 128
NF = FF // 128
RMS_EPS = 1e-6
import os
OUTQ = os.environ.get("OUTQ", "act")
SAME_ENG_DIST = int(os.environ.get("SAME_ENG_DIST", "1000000000"))


class _Op:
    __slots__ = ("eng", "fn", "deps", "idx", "dma", "inc", "semi", "semval", "cnt")

    def __init__(self, eng, fn, dma):
        self.eng = eng
        self.fn = fn
        self.dma = dma
        self.deps = []
        self.inc = dma
        self.semi = None
        self.semval = None
        self.cnt = None


class Prog:
    ENGS = ("pe", "act", "dve", "pool", "sp")
    NDMASEM = {"sp": 16, "act": 6, "pool": 6, "pe": 2, "dve": 2}

    def __init__(self, nc):
        self.nc = nc
        self.ops = {e: [] for e in self.ENGS}
        self.lastw = {}
        self.readers = {}
        self.ndma = {e: 0 for e in self.ENGS}
        self.dma_prev = {}

    def add(self, eng, fn, r=(), w=(), dma=False):
        op = _Op(eng, fn, dma)
        lst = self.ops[eng]
        op.idx = len(lst)
        deps = []
        for k in r:
            lw = self.lastw.get(k)
            if lw is not None:
                deps.append(lw)
        for k in w:
            lw = self.lastw.get(k)
            if lw is not None:
                deps.append(lw)
            for rd in self.readers.get(k, {}).values():
                deps.append(rd)
        if dma:
            q = self.ndma[eng]
            self.ndma[eng] += 1
            op.semi = (eng, q % self.NDMASEM[eng])
            prev = self.dma_prev.get(op.semi)
            if prev is not None:
                deps.append(prev)
                op.semval = prev.semval + 16
            else:
                op.semval = 16
            self.dma_prev[op.semi] = op
        seen = set()
        for dp in deps:
            if dp is op or id(dp) in seen:
                continue
            seen.add(id(dp))
            if dp.eng == eng and not dp.dma and not dma:
                if eng == "pe" or op.idx - dp.idx > SAME_ENG_DIST:
                    continue
            dp.inc = True
            op.deps.append(dp)
        for k in r:
            self.readers.setdefault(k, {})[eng if not dma else (eng, "dma", op.idx)] = op
        for k in w:
            self.lastw[k] = op
            self.readers[k] = {}
        lst.append(op)
        return op

    def dma(self, out, in_, r=(), w=(), q="sp", **kw):
        return self.add(q, lambda e: e.dma_start(out=out, in_=in_, **kw), r=r, w=w, dma=True)

    def mm(self, out, lhsT, rhs, start, stop, r=(), w=()):
        return self.add("pe", lambda e: e.matmul(out, lhsT, rhs, start=start, stop=stop), r=r, w=w)

    def emit(self):
        nc = self.nc
        import contextlib
        with contextlib.ExitStack() as st:
            csem = {}
            for e in ("pe", "act", "dve", "pool"):
                csem[e] = st.enter_context(nc.semaphore("c_" + e))
            dsem = {}
            for e in self.ENGS:
                for i in range(min(self.NDMASEM[e], self.ndma[e])):
                    dsem[(e, i)] = st.enter_context(nc.semaphore("d_%s_%d" % (e, i)))
            for e in ("pe", "act", "dve", "pool"):
                c = 0
                for op in self.ops[e]:
                    if not op.dma and op.inc:
                        c += 1
                        op.cnt = c
            final_dma = list(self.dma_prev.values())
            block = st.enter_context(nc.Block())

            def section(ename):
                def body(eng):
                    known = {}
                    for op in self.ops[ename]:
                        for dp in op.deps:
                            if dp.dma:
                                sem, val = dsem[dp.semi], dp.semval
                            else:
                                sem, val = csem[dp.eng], dp.cnt
                            kk = id(sem)
                            if known.get(kk, 0) >= val:
                                continue
                            known[kk] = val
                            eng.wait_ge(sem, val)
                        ins = op.fn(eng)
                        if op.dma:
                            ins.then_inc(dsem[op.semi], 16)
                        elif op.inc:
                            ins.then_inc(csem[ename], 1)
                    if ename == "sp":
                        for dp in final_dma:
                            eng.wait_ge(dsem[dp.semi], dp.semval)
                return body

            block.tensor(section("pe"))
            block.scalar(section("act"))
            block.vector(section("dve"))
            block.gpsimd(section("pool"))
            block.sync(section("sp"))


def _run(nc, in_maps):
    res = run_bass_kernel_spmd(nc, in_maps, core_ids=list(range(len(in_maps))))
    return res.results


def vec_layout(v):
    v = np.asarray(v, np.float32).reshape(-1, v.shape[-1])
    n, L = v.shape
    return np.ascontiguousarray(v.reshape(n, L // 128, 128).transpose(2, 0, 1).reshape(128, n * (L // 128)))


def build_k3(NT, final):
    TB = 256
    NB = NT // TB
    nc = bass.Bass("TRN2", target_bir_lowering=False)
    xT = nc.dram_tensor("xT", [D, NT], F32, kind="ExternalInput").ap()
    zT = nc.dram_tensor("zT", [D, NT], F32, kind="ExternalInput").ap()
    wo = nc.dram_tensor("wo", [D, D], F32, kind="ExternalInput").ap()
    w1 = nc.dram_tensor("w1", [D, FF], F32, kind="ExternalInput").ap()
    w3 = nc.dram_tensor("w3", [D, FF], F32, kind="ExternalInput").ap()
    w2 = nc.dram_tensor("w2", [FF, D], F32, kind="ExternalInput").ap()
    vecs = nc.dram_tensor("vecs", [128, 7 * ND], F32, kind="ExternalInput").ap()
    oT = nc.dram_tensor("oT", [D, NT], F32, kind="ExternalOutput").ap()
    xTv = xT.rearrange("(t p) n -> p t n", p=128)
    zTv = zT.rearrange("(t p) n -> p t n", p=128)
    oTv = oT.rearrange("(t p) n -> p t n", p=128)
    wov = wo.rearrange("(t p) f -> p t f", p=128)
    w1v = w1.rearrange("(t p) f -> p t f", p=128)
    w3v = w3.rearrange("(t p) f -> p t f", p=128)
    w2v = w2.rearrange("(t p) f -> p t f", p=128)
    P = Prog(nc)
    import contextlib
    with contextlib.ExitStack() as st:
        sb = lambda name, shape: st.enter_context(nc.sbuf_tensor(name, shape, F32))
        ps = lambda name: st.enter_context(nc.psum_tensor(name, [128, 512], F32))
        vt = sb("vt", [128, 7 * ND])
        va = sb("va", [128, 4 * ND])
        ones = sb("ones", [128, 128])
        epsb = sb("epsb", [128, 1])
        xs = [sb("xs%d" % i, [128, ND, TB]) for i in range(2)]
        zs = [sb("zs%d" % i, [128, ND, TB]) for i in range(2)]
        gs = sb("gs", [128, NF, TB])
        wa = [sb("wa%d" % i, [128, ND, 128]) for i in range(2)]
        wb = [sb("wb%d" % i, [128, ND, 128]) for i in range(2)]
        wc = [sb("wc%d" % i, [128, NF, 128]) for i in range(2)]
        sq = [sb("sq%d" % i, [128, TB]) for i in range(2)]
        rstd = sb("rstd", [128, TB])
        tmp = [sb("tmp%d" % i, [128, TB]) for i in range(2)]
        pacc = [ps("pacc%d" % i) for i in range(2)]
        pa = [ps("pa%d" % i) for i in range(2)]
        pb = [ps("pb%d" % i) for i in range(2)]
        pss = ps("pss")

        P.dma(vt[:], vecs[:, :], w=["vt"])
        P.add("pool", lambda e: e.memset(ones[:], 1.0), w=["ones"])
        P.add("pool", lambda e: e.memset(epsb[:], RMS_EPS), w=["epsb"])
        V = lambda i, t: vt[:, i * ND + t:i * ND + t + 1]
        VA = lambda i, t: va[:, i * ND + t:i * ND + t + 1]
        P.add("dve", lambda e: e.tensor_tensor(out=va[:, 0:ND], in0=vt[:, 0:ND], in1=vt[:, ND:2 * ND], op=ALU.mult),
              r=["vt"], w=["va0"])
        P.add("dve", lambda e: e.tensor_scalar(out=va[:, ND:2 * ND], in0=vt[:, 3 * ND:4 * ND], scalar1=1.0,
                                               scalar2=None, op0=ALU.add),
              r=["vt"], w=["va1"])
        P.add("dve", lambda e: e.tensor_tensor(out=va[:, ND:2 * ND], in0=va[:, ND:2 * ND], in1=vt[:, 2 * ND:3 * ND],
                                               op=ALU.mult), r=["vt", "va1"], w=["va1"])
        P.add("dve", lambda e: e.tensor_scalar(out=va[:, 2 * ND:3 * ND], in0=vt[:, 6 * ND:7 * ND],
                                               scalar1=1.0, scalar2=None, op0=ALU.mult),
              r=["vt"], w=["va2"])
        VK = ["vt", "va0", "va1", "va2"]

        wcount = [0, 0, 0]

        def rmsnorm_mod(src, srck, dst, dstk, acol, bcol):
            for t in range(ND):
                s = sq[t % 2]
                P.add("act", lambda e, s=s, t=t: e.activation(out=s[:], in_=src[:, t, :], func=AF.Square),
                      r=[srck], w=[("sq", t % 2)])
                P.mm(pss[:, 0:TB], ones[:], s[:], start=(t == 0), stop=(t == ND - 1),
                     r=[("sq", t % 2), "ones"], w=["pss"])
            P.add("act", lambda e: e.activation(out=rstd[:], in_=pss[:, 0:TB], func=AF.Sqrt, scale=1.0 / D,
                                                bias=epsb[:, 0:1]),
                  r=["pss", "epsb"], w=["rstd"])
            P.add("dve", lambda e: e.reciprocal(out=rstd[:], in_=rstd[:]), r=["rstd"], w=["rstd"])
            for t in range(ND):
                tt = tmp[t % 2]
                P.add("dve", lambda e, tt=tt, t=t: e.tensor_tensor(out=tt[:], in0=src[:, t, :], in1=rstd[:],
                                                                   op=ALU.mult),
                      r=[srck, "rstd"], w=[("tmp", t % 2)])
                if bcol is None:
                    P.add("act", lambda e, tt=tt, t=t: e.activation(out=dst[:, t, :], in_=tt[:], func=AF.Copy,
                                                                    scale=VA(acol, t)),
                          r=[("tmp", t % 2)] + VK, w=[dstk])
                else:
                    P.add("act", lambda e, tt=tt, t=t: e.activation(out=dst[:, t, :], in_=tt[:], func=AF.Identity,
                                                                    scale=VA(acol, t), bias=V(bcol, t)),
                          r=[("tmp", t % 2)] + VK, w=[dstk])

        for b in range(NB):
            x = xs[b % 2]
            z = zs[b % 2]
            xk, zk = ("x", b % 2), ("z", b % 2)
            tok = slice(b * TB, (b + 1) * TB)
            P.dma(z[:], zTv[:, :, tok], w=[zk])
            P.dma(x[:], xTv[:, :, tok], w=[xk])
            for fo in range(ND):
                i = wcount[0] % 2
                wcount[0] += 1
                P.dma(wa[i][:], wov[:, :, fo * 128:(fo + 1) * 128], w=[("wa", i)])
                acc = pacc[fo % 2]
                for fi in range(ND):
                    P.mm(acc[:, 0:TB], wa[i][:, fi, :], z[:, fi, :], start=(fi == 0), stop=(fi == ND - 1),
                         r=[("wa", i), zk], w=[("pacc", fo % 2)])
                P.add("dve", lambda e, acc=acc, fo=fo, x=x: e.scalar_tensor_tensor(
                    out=x[:, fo, :], in0=acc[:, 0:TB], scalar=V(1, fo), in1=x[:, fo, :], op0=ALU.mult, op1=ALU.add),
                    r=[("pacc", fo % 2), xk] + VK, w=[xk])
                P.add("pool", lambda e, fo=fo, x=x: e.tensor_scalar(
                    out=x[:, fo, :], in0=x[:, fo, :], scalar1=VA(0, fo), scalar2=None, op0=ALU.add),
                    r=[xk] + VK, w=[xk])
            rmsnorm_mod(x, xk, z, zk, 1, 4)
            for j in range(NF):
                i = wcount[1] % 2
                wcount[1] += 1
                P.dma(wa[i][:], w1v[:, :, j * 128:(j + 1) * 128], w=[("wa", i)])
                P.dma(wb[i][:], w3v[:, :, j * 128:(j + 1) * 128], w=[("wb", i)])
                A = pa[j % 2]
                B = pb[j % 2]
                for fi in range(ND):
                    P.mm(A[:, 0:TB], wa[i][:, fi, :], z[:, fi, :], start=(fi == 0), stop=(fi == ND - 1),
                         r=[("wa", i), zk], w=[("pa", j % 2)])
                for fi in range(ND):
                    P.mm(B[:, 0:TB], wb[i][:, fi, :], z[:, fi, :], start=(fi == 0), stop=(fi == ND - 1),
                         r=[("wb", i), zk], w=[("pb", j % 2)])
                tt = tmp[j % 2]
                P.add("act", lambda e, A=A, tt=tt: e.activation(out=tt[:], in_=A[:, 0:TB], func=AF.Silu),
                      r=[("pa", j % 2)], w=[("tmp", j % 2)])
                P.add("dve", lambda e, B=B, tt=tt, j=j: e.tensor_tensor(out=gs[:, j, :], in0=tt[:], in1=B[:, 0:TB],
                                                                        op=ALU.mult),
                      r=[("pb", j % 2), ("tmp", j % 2)], w=["gs"])
            for fo in range(ND):
                i = wcount[2] % 2
                wcount[2] += 1
                P.dma(wc[i][:], w2v[:, :, fo * 128:(fo + 1) * 128], w=[("wc", i)])
                acc = pacc[fo % 2]
                for j in range(NF):
                    P.mm(acc[:, 0:TB], wc[i][:, j, :], gs[:, j, :], start=(j == 0), stop=(j == NF - 1),
                         r=[("wc", i), "gs"], w=[("pacc", fo % 2)])
                P.add("dve", lambda e, acc=acc, fo=fo, x=x: e.scalar_tensor_tensor(
                    out=x[:, fo, :], in0=acc[:, 0:TB], scalar=V(5, fo), in1=x[:, fo, :], op0=ALU.mult, op1=ALU.add),
                    r=[("pacc", fo % 2), xk] + VK, w=[xk])
            if final:
                rmsnorm_mod(x, xk, z, zk, 2, None)
                P.dma(oTv[:, :, tok], z[:], r=[zk], q=OUTQ)
            else:
                P.dma(oTv[:, :, tok], x[:], r=[xk], q=OUTQ)
        P.emit()
    return nc


def build_k0():
    NCOL = 6 * D // NCORES
    nc = bass.Bass("TRN2", target_bir_lowering=False)
    cv = nc.dram_tensor("cv", [128, ND, 2], F32, kind="ExternalInput").ap()
    aw = nc.dram_tensor("aw", [2, D, NCOL], F32, kind="ExternalInput").ap()
    ab = nc.dram_tensor("ab", [2, NCOL], F32, kind="ExternalInput").ap()
    mods = nc.dram_tensor("mods", [4, NCOL], F32, kind="ExternalOutput").ap()
    P = Prog(nc)
    import contextlib
    with contextlib.ExitStack() as st:
        sb = lambda name, shape: st.enter_context(nc.sbuf_tensor(name, shape, F32))
        s = sb("s", [128, ND, 2])
        wb = [sb("wb%d" % i, [128, ND, 512]) for i in range(2)]
        bs = [sb("bs%d" % i, [2, NCOL]) for i in range(2)]
        res = [sb("res%d" % i, [2, NCOL]) for i in range(2)]
        pp = [st.enter_context(nc.psum_tensor("pp%d" % i, [128, 512], F32)) for i in range(2)]
        P.dma(s[:], cv[:, :, :], w=["s"])
        P.add("act", lambda e: e.activation(out=s[:], in_=s[:], func=AF.Silu), r=["s"], w=["s"])
        n = 0
        for l in range(2):
            P.dma(bs[l][0:1, :], ab[l:l + 1, :], w=[("bs", l, 0)])
            P.dma(bs[l][1:2, :], ab[l:l + 1, :], w=[("bs", l, 1)])
            awv = aw[l].rearrange("(t p) f -> p t f", p=128)
            for cb in range(NCOL // 512):
                i = n % 2
                n += 1
                cs = slice(cb * 512, (cb + 1) * 512)
                P.dma(wb[i][:], awv[:, :, cs], w=[("wb", i)])
                for t in range(ND):
                    P.mm(pp[i][0:2, :], s[:, t, :], wb[i][:, t, :], start=(t == 0), stop=(t == ND - 1),
                         r=["s", ("wb", i)], w=[("pp", i)])
                P.add("dve", lambda e, i=i, l=l, cs=cs: e.tensor_tensor(out=res[l][:, cs], in0=pp[i][0:2, :],
                                                                        in1=bs[l][:, cs], op=ALU.add),
                      r=[("pp", i), ("bs", l, 0), ("bs", l, 1)], w=[("res", l)])
            P.dma(mods[2 * l:2 * l + 2, :], res[l][:], r=[("res", l)], q="act")
        P.emit()
    return nc


def run_k0(inp):
    cv = np.stack([vec_layout(inp["c"].reshape(1, D)), vec_layout(inp["c_ctx"].reshape(1, D))], axis=-1)
    cv = np.ascontiguousarray(cv.astype(np.float32))
    NCOL = 6 * D // NCORES
    nc = build_k0()
    maps = []
    for c in range(NCORES):
        cs = slice(c * NCOL, (c + 1) * NCOL)
        maps.append({"cv": cv, "aw": np.ascontiguousarray(inp["ada_w"][:, :, cs]),
                     "ab": np.ascontiguousarray(inp["ada_b"][:, cs])})
    r = _run(nc, maps)
    mods = np.concatenate([r[c]["mods"] for c in range(NCORES)], axis=1)
    return mods


TT = CTX + SEQ
HL = 96


def build_k1(nblocks=None):
    TBK = 256
    NBUF = TBK + 128
    NBLK = 1 + SEQ // TBK
    if nblocks is None:
        nblocks = NBLK
    nc = bass.Bass("TRN2", target_bir_lowering=False)
    din = lambda n, s: nc.dram_tensor(n, s, F32, kind="ExternalInput").ap()
    dout = lambda n, s: nc.dram_tensor(n, s, F32, kind="ExternalOutput").ap()
    xcT = din("xcT", [D, CTX]).rearrange("(t p) n -> p t n", p=128)
    xlT = din("xlT", [D, SEQ]).rearrange("(t p) n -> p t n", p=128)
    vecs = din("vecs", [128, 11 * ND])
    wr = din("wr", [D, 256]).rearrange("(t p) f -> p t f", p=128)
    wk = din("wk", [D, 256]).rearrange("(t p) f -> p t f", p=128)
    wv = din("wv", [D, 256]).rearrange("(t p) f -> p t f", p=128)
    g1 = din("g1", [D, 256]).rearrange("(t p) f -> p t f", p=128)
    w1 = din("w1", [2, D, HL])
    a1 = din("a1", [2, D, HL])
    w2 = din("w2", [2, HL, 256])
    a2 = din("a2", [2, HL, 256])
    g2 = din("g2", [256, 256]).rearrange("(t p) f -> p t f", p=128)
    tabs = din("tabs", [128, 7 * 256])
    ident = din("ident", [128, 128])
    tab = dout("tab", [TT, 4 * 2 * 5 * 64])
    vT = dout("vT", [256, TT]).rearrange("(h p) t -> p h t", p=128)
    gT = dout("gT", [256, SEQ]).rearrange("(h p) t -> p h t", p=128)
    rkT = dout("rkT", [4, TT])
    P = Prog(nc)
    import contextlib
    with contextlib.ExitStack() as st:
        sb = lambda name, shape: st.enter_context(nc.sbuf_tensor(name, shape, F32))
        vt = sb("vt", [128, 11 * ND])
        va = sb("va", [128, 2 * ND])
        ones = sb("ones", [128, 128])
        epsb = sb("epsb", [128, 1])
        idt = sb("idt", [128, 128])
        tb = sb("tb", [128, 8, 256])
        wr_s = sb("wr_s", [128, ND, 256])
        wk_s = sb("wk_s", [128, ND, 256])
        wv_s = sb("wv_s", [128, ND, 256])
        g1_s = sb("g1_s", [128, ND, 256])
        w1_s = sb("w1_s", [128, ND, 2, HL])
        a1_s = sb("a1_s", [128, ND, 2, HL])
        w2_s = sb("w2_s", [HL, 2, 256])
        a2_s = sb("a2_s", [HL, 2, 256])
        g2_s = sb("g2_s", [128, 2, 256])
        xb = sb("xb", [128, ND, NBUF])
        xx = sb("xx", [128, ND, TBK])
        xm = sb("xm", [128, ND, TBK])
        sq = [sb("sq%d" % i, [128, NBUF]) for i in range(2)]
        rstd = sb("rstd", [128, NBUF])
        tmp = [sb("tmp%d" % i, [128, NBUF]) for i in range(2)]
        hw_s = sb("hw_s", [HL, 2, TBK])
        ha_s = sb("ha_s", [HL, 2, TBK])
        hg_s = sb("hg_s", [128, 2, TBK])
        vo_s = sb("vo_s", [128, 2, TBK])
        go_s = sb("go_s", [128, 2, TBK])
        rkst = sb("rkst", [4, TBK])
        stg = [sb("stg%d" % i, [128, 4, 2, 5, 64]) for i in range(2)]
        k_s = sb("k_s", [128, 256])
        kk_s = sb("kk_s", [128, 256])
        t1_s = sb("t1_s", [128, 256])
        t2_s = sb("t2_s", [128, 256])
        zw_s = sb("zw_s", [128, 256])
        za_s = sb("za_s", [128, 256])
        a_s = sb("a_s", [128, 256])
        ss_s = sb("ss_s", [128, 4])
        rk_s = sb("rk_s", [128, 4])
        pb = [st.enter_context(nc.psum_tensor("pb%d" % i, [128, 512], F32)) for i in range(8)]
        PSS, PRK, PV, PHW, PHA, PHG, PG, PT = range(8)
        pk = lambda i: ("pb", i)

        P.dma(vt[:], vecs[:, :], w=["vt"])
        P.dma(tb[:, 0:7, :], tabs.rearrange("p (a f) -> p a f", f=256), w=["tb"])
        P.dma(idt[:], ident[:, :], w=["idt"])
        P.dma(wr_s[:], wr, w=["wr"])
        P.dma(wk_s[:], wk, w=["wk"])
        P.dma(wv_s[:], wv, w=["wv"])
        P.dma(g1_s[:], g1, w=["g1"])
        for d in range(2):
            P.dma(w1_s[:, :, d, :], w1[d].rearrange("(t p) f -> p t f", p=128), w=[("w1", d)])
            P.dma(a1_s[:, :, d, :], a1[d].rearrange("(t p) f -> p t f", p=128), w=[("a1", d)])
            P.dma(w2_s[:, d, :], w2[d], w=[("w2", d)])
            P.dma(a2_s[:, d, :], a2[d], w=[("a2", d)])
        P.dma(g2_s[:], g2, w=["g2"])
        WK = ["wr", "wk", "wv", "g1", ("w1", 0), ("w1", 1), ("a1", 0), ("a1", 1), ("w2", 0), ("w2", 1),
              ("a2", 0), ("a2", 1), "g2", "idt", "tb", "tb7"]
        P.add("pool", lambda e: e.memset(ones[:], 1.0), w=["ones"])
        P.add("pool", lambda e: e.memset(epsb[:], RMS_EPS), w=["epsb"])
        P.add("dve", lambda e: e.tensor_scalar(out=tb[:, 7, :], in0=tb[:, 5, :], scalar1=-1.0, scalar2=1.0,
                                               op0=ALU.mult, op1=ALU.add), r=["tb"], w=["tb7"])
        for j, c in enumerate((1, 3)):
            P.add("dve", lambda e, j=j, c=c: e.tensor_scalar(out=va[:, j * ND:(j + 1) * ND],
                                                             in0=vt[:, c * ND:(c + 1) * ND], scalar1=1.0,
                                                             scalar2=None, op0=ALU.add), r=["vt"], w=[("va", j)])
            P.add("dve", lambda e, j=j: e.tensor_tensor(out=va[:, j * ND:(j + 1) * ND], in0=va[:, j * ND:(j + 1) * ND],
                                                        in1=vt[:, 0:ND], op=ALU.mult), r=["vt", ("va", j)],
                  w=[("va", j)])
        VK = ["vt", ("va", 0), ("va", 1)]
        V = lambda i, t: vt[:, i * ND + t:i * ND + t + 1]
        VA = lambda i, t: va[:, i * ND + t:i * ND + t + 1]
        rows = lambda ap: ap.rearrange("p (r c) -> p r c", c=64)
        hv = lambda ap: ap.rearrange("p (h k) -> p h k", k=64)

        for bi in range(nblocks):
            is_ctx = bi == 0
            if is_ctx:
                t0 = 0
                P.add("pool", lambda e: e.memset(xb[:, :, 0:64], 0.0), w=["xb"])
                P.add("pool", lambda e: e.memset(xb[:, :, 64 + TBK:NBUF], 0.0), w=["xb"])
                P.dma(xb[:, :, 64:64 + TBK], xcT[:, :, :], w=["xb"])
                zero_top = zero_bot = True
            else:
                lb = bi - 1
                lt0 = lb * TBK
                t0 = CTX + lt0
                a = max(lt0 - 64, 0)
                b = min(lt0 + TBK + 64, SEQ)
                zero_top = lt0 - 64 < 0
                zero_bot = lt0 + TBK + 64 > SEQ
                if zero_top:
                    P.add("pool", lambda e: e.memset(xb[:, :, 0:64], 0.0), w=["xb"])
                if zero_bot:
                    P.add("pool", lambda e: e.memset(xb[:, :, 64 + TBK:NBUF], 0.0), w=["xb"])
                P.dma(xb[:, :, 64 + (a - lt0):64 + (b - lt0)], xlT[:, :, a:b], w=["xb"])
            for t in range(ND):
                s = sq[t % 2]
                P.add("act", lambda e, s=s, t=t: e.activation(out=s[:], in_=xb[:, t, :], func=AF.Square),
                      r=["xb"], w=[("sq", t % 2)])
                P.mm(pb[PSS][:, 0:NBUF], ones[:], s[:], start=(t == 0), stop=(t == ND - 1),
                     r=[("sq", t % 2), "ones"], w=[pk(PSS)])
            P.add("act", lambda e: e.activation(out=rstd[:], in_=pb[PSS][:, 0:NBUF], func=AF.Sqrt, scale=1.0 / D,
                                                bias=epsb[:, 0:1]), r=[pk(PSS), "epsb"], w=["rstd"])
            P.add("dve", lambda e: e.reciprocal(out=rstd[:], in_=rstd[:]), r=["rstd"], w=["rstd"])
            ac, bc = (1, 4) if is_ctx else (0, 2)
            for t in range(ND):
                tt = tmp[t % 2]
                P.add("dve", lambda e, tt=tt, t=t: e.tensor_tensor(out=tt[:], in0=xb[:, t, :], in1=rstd[:],
                                                                   op=ALU.mult), r=["xb", "rstd"], w=[("tmp", t % 2)])
                P.add("act", lambda e, tt=tt, t=t, ac=ac, bc=bc: e.activation(
                    out=xb[:, t, :], in_=tt[:], func=AF.Identity, scale=VA(ac, t), bias=V(bc, t)),
                    r=[("tmp", t % 2)] + VK, w=["xb"])
            if zero_top:
                P.add("pool", lambda e: e.memset(xb[:, :, 0:64], 0.0), w=["xb"])
            if zero_bot:
                P.add("pool", lambda e: e.memset(xb[:, :, 64 + TBK:NBUF], 0.0), w=["xb"])
            H0 = 64
            for t in range(ND):
                eng = "dve" if t % 2 == 0 else "pool"
                cur = xb[:, t, H0:H0 + TBK]
                if is_ctx:
                    off = -1 if t < ND // 2 else 1
                    P.add(eng, lambda e, t=t, off=off, cur=cur: e.tensor_tensor(
                        out=xx[:, t, :], in0=xb[:, t, H0 + off:H0 + off + TBK], in1=cur, op=ALU.subtract),
                        r=["xb"], w=["xx"])
                elif t >= 8:
                    off = -64 if t < 12 else 64
                    P.add(eng, lambda e, t=t, off=off, cur=cur: e.tensor_tensor(
                        out=xx[:, t, :], in0=xb[:, t, H0 + off:H0 + off + TBK], in1=cur, op=ALU.subtract),
                        r=["xb"], w=["xx"])
                else:
                    cr = rows(cur)
                    xr_ = rows(xx[:, t, :])
                    if t < 4:
                        P.add(eng, lambda e, cr=cr, xr_=xr_: e.tensor_tensor(
                            out=xr_[:, :, 1:64], in0=cr[:, :, 0:63], in1=cr[:, :, 1:64], op=ALU.subtract),
                            r=["xb"], w=["xx"])
                        P.add(eng, lambda e, cr=cr, xr_=xr_: e.tensor_scalar(
                            out=xr_[:, :, 0:1], in0=cr[:, :, 0:1], scalar1=-1.0, scalar2=None, op0=ALU.mult),
                            r=["xb"], w=["xx"])
                    else:
                        P.add(eng, lambda e, cr=cr, xr_=xr_: e.tensor_tensor(
                            out=xr_[:, :, 0:63], in0=cr[:, :, 1:64], in1=cr[:, :, 0:63], op=ALU.subtract),
                            r=["xb"], w=["xx"])
                        P.add(eng, lambda e, cr=cr, xr_=xr_: e.tensor_scalar(
                            out=xr_[:, :, 63:64], in0=cr[:, :, 63:64], scalar1=-1.0, scalar2=None, op0=ALU.mult),
                            r=["xb"], w=["xx"])

            def mix(m):
                for t in range(ND):
                    eng = "dve"
                    P.add(eng, lambda e, t=t, m=m: e.scalar_tensor_tensor(
                        out=xm[:, t, :], in0=xx[:, t, :], scalar=V(5 + m, t), in1=xb[:, t, H0:H0 + TBK],
                        op0=ALU.mult, op1=ALU.add), r=["xx", "xb"] + VK, w=["xm"])

            mix(0)
            for tt in range(2):
                for t in range(ND):
                    P.mm(pb[PRK][:, tt * 256:(tt + 1) * 256], xm[:, t, tt * 128:(tt + 1) * 128], wr_s[:, t, :],
                         start=(t == 0), stop=(t == ND - 1), r=["xm", "wr"], w=[pk(PRK)])
            sg = stg[bi % 2]
            sgk = [("stg", bi % 2, 0), ("stg", bi % 2, 1)]
            stgs = [stg[(2 * bi + tt) % 2] for tt in range(2)]
            for tt in range(2):
                for d in range(2):
                    P.add("act", lambda e, tt=tt, d=d: e.activation(
                        out=stgs[tt][:, :, d, 4, :], in_=hv(pb[PRK][:, tt * 256:(tt + 1) * 256]), func=AF.Copy),
                        r=[pk(PRK)], w=[("stg", tt)])
            mix(1)
            for d in range(2):
                for t in range(ND):
                    P.mm(pb[PHW][0:HL, d * TBK:(d + 1) * TBK], w1_s[:, t, d, :], xm[:, t, :],
                         start=(t == 0), stop=(t == ND - 1), r=["xm", ("w1", d)], w=[pk(PHW)])
            P.add("act", lambda e: e.activation(out=hw_s[:].rearrange("p d n -> p (d n)"), in_=pb[PHW][0:HL, :],
                                                func=AF.Tanh), r=[pk(PHW)], w=["hw"])
            mix(2)
            for tt in range(2):
                for t in range(ND):
                    P.mm(pb[PT][:, tt * 256:(tt + 1) * 256], xm[:, t, tt * 128:(tt + 1) * 128], wk_s[:, t, :],
                         start=(t == 0), stop=(t == ND - 1), r=["xm", "wk"], w=[pk(PT)])
            mix(3)
            for h in range(2):
                for t in range(ND):
                    P.mm(pb[PV][:, h * TBK:(h + 1) * TBK], wv_s[:, t, h * 128:(h + 1) * 128], xm[:, t, :],
                         start=(t == 0), stop=(t == ND - 1), r=["xm", "wv"], w=[pk(PV)])
            P.add("act", lambda e: e.activation(out=vo_s[:].rearrange("p h n -> p (h n)"), in_=pb[PV][:, :],
                                                func=AF.Copy), r=[pk(PV)], w=["vo"])
            P.dma(vT[:, :, t0:t0 + TBK], vo_s[:], r=["vo"], q="act")
            mix(4)
            for d in range(2):
                for t in range(ND):
                    P.mm(pb[PHA][0:HL, d * TBK:(d + 1) * TBK], a1_s[:, t, d, :], xm[:, t, :],
                         start=(t == 0), stop=(t == ND - 1), r=["xm", ("a1", d)], w=[pk(PHA)])
            P.add("dve", lambda e: e.tensor_copy(out=ha_s[:].rearrange("p d n -> p (d n)"), in_=pb[PHA][0:HL, :]),
                  r=[pk(PHA)], w=["ha"])
            if not is_ctx:
                mix(5)
                for h in range(2):
                    for t in range(ND):
                        P.mm(pb[PHG][:, h * TBK:(h + 1) * TBK], g1_s[:, t, h * 128:(h + 1) * 128], xm[:, t, :],
                             start=(t == 0), stop=(t == ND - 1), r=["xm", "g1"], w=[pk(PHG)])
                P.add("act", lambda e: e.activation(out=hg_s[:].rearrange("p h n -> p (h n)"), in_=pb[PHG][:, :],
                                                    func=AF.Sigmoid), r=[pk(PHG)], w=["hg"])
                for h in range(2):
                    for hh in range(2):
                        P.mm(pb[PG][:, h * TBK:(h + 1) * TBK], g2_s[:, hh, h * 128:(h + 1) * 128], hg_s[:, hh, :],
                             start=(hh == 0), stop=(hh == 1), r=["hg", "g2"], w=[pk(PG)])
                P.add("act", lambda e: e.activation(out=go_s[:].rearrange("p h n -> p (h n)"), in_=pb[PG][:, :],
                                                    func=AF.Copy), r=[pk(PG)], w=["go"])
                P.dma(gT[:, :, t0 - CTX:t0 - CTX + TBK], go_s[:], r=["go"], q="act")
            for tt in range(2):
                S = stgs[tt]
                sk = ("stg", tt)
                tsl = slice(tt * 128, (tt + 1) * 128)
                for d in range(2):
                    P.mm(pb[PSS][:, d * 256:(d + 1) * 256], hw_s[:, d, tsl], w2_s[:, d, :], start=True, stop=True,
                         r=["hw", ("w2", d)], w=[pk(PSS)])
                for d in range(2):
                    P.mm(pb[PG][:, d * 256:(d + 1) * 256], ha_s[:, d, tsl], a2_s[:, d, :], start=True, stop=True,
                         r=["ha", ("a2", d)], w=[pk(PG)])
                P.add("act", lambda e, tt=tt: e.activation(out=k_s[:], in_=pb[PT][:, tt * 256:(tt + 1) * 256],
                                                           func=AF.Copy), r=[pk(PT)], w=["k_s"])
                P.add("dve", lambda e: e.tensor_tensor(out=kk_s[:], in0=k_s[:], in1=tb[:, 4, :], op=ALU.mult),
                      r=["k_s", "tb"], w=["kk_s"])
                P.add("pool", lambda e: e.tensor_tensor(out=t1_s[:], in0=kk_s[:], in1=kk_s[:], op=ALU.mult),
                      r=["kk_s"], w=["t1_s"])
                P.add("dve", lambda e: e.tensor_reduce(out=ss_s[:], in_=hv(t1_s[:]), axis=AX.X, op=ALU.add),
                      r=["t1_s"], w=["ss_s"])
                P.add("act", lambda e: e.activation(out=ss_s[:], in_=ss_s[:], func=AF.Sqrt), r=["ss_s"], w=["ss_s"])
                P.add("dve", lambda e: e.tensor_scalar(out=ss_s[:], in0=ss_s[:], scalar1=1e-12, scalar2=-1.0,
                                                       op0=ALU.max, op1=ALU.mult), r=["ss_s"], w=["ss_s"])
                P.add("dve", lambda e: e.reciprocal(out=ss_s[:], in_=ss_s[:]), r=["ss_s"], w=["ss_s"])
                for h in range(4):
                    P.add("dve", lambda e, h=h, S=S: e.tensor_scalar(
                        out=S[:, h, 0, 0, :], in0=kk_s[:, h * 64:(h + 1) * 64], scalar1=ss_s[:, h:h + 1], scalar2=None,
                        op0=ALU.mult), r=["kk_s", "ss_s"], w=[sk])
                P.add("pool", lambda e, S=S: e.tensor_copy(out=S[:, :, 1, 0, :], in_=S[:, :, 0, 0, :]), r=[sk], w=[sk])
                for d in range(2):
                    P.add("dve", lambda e, d=d: e.tensor_tensor(out=zw_s[:], in0=pb[PSS][:, d * 256:(d + 1) * 256],
                                                                in1=tb[:, d, :], op=ALU.add),
                          r=[pk(PSS), "tb"], w=["zw_s"])
                    P.add("act", lambda e: e.activation(out=zw_s[:], in_=zw_s[:], func=AF.Sigmoid),
                          r=["zw_s"], w=["zw_s"])
                    P.add("act", lambda e, d=d, S=S: e.activation(out=S[:, :, d, 1, :], in_=hv(zw_s[:]), func=AF.Exp,
                                                                  scale=-float(np.exp(-0.5))), r=["zw_s"], w=[sk])
                    P.add("dve", lambda e, d=d: e.tensor_tensor(out=za_s[:], in0=pb[PG][:, d * 256:(d + 1) * 256],
                                                                in1=tb[:, 2 + d, :], op=ALU.add),
                          r=[pk(PG), "tb"], w=["za_s"])
                    P.add("act", lambda e: e.activation(out=a_s[:], in_=za_s[:], func=AF.Sigmoid),
                          r=["za_s"], w=["a_s"])
                    P.add("pool", lambda e: e.tensor_tensor(out=t1_s[:], in0=a_s[:], in1=tb[:, 5, :], op=ALU.mult),
                          r=["a_s", "tb"], w=["t1_s"])
                    P.add("pool", lambda e: e.tensor_tensor(out=t1_s[:], in0=t1_s[:], in1=tb[:, 7, :], op=ALU.add),
                          r=["t1_s", "tb7"], w=["t1_s"])
                    P.add("dve", lambda e, d=d, S=S: e.tensor_tensor(out=S[:, :, d, 3, :], in0=hv(k_s[:]),
                                                                     in1=hv(t1_s[:]), op=ALU.mult),
                          r=["k_s", "t1_s"], w=[sk])
                    P.add("dve", lambda e, d=d, S=S: e.scalar_tensor_tensor(
                        out=S[:, :, d, 2, :], in0=S[:, :, 0, 0, :], scalar=-1.0, in1=hv(a_s[:]), op0=ALU.mult,
                        op1=ALU.mult), r=[sk, "a_s"], w=[sk])
                P.add("pool", lambda e, S=S: e.tensor_tensor(out=hv(t2_s[:]), in0=S[:, :, 0, 3, :], in1=S[:, :, 1, 3, :],
                                                             op=ALU.add), r=[sk], w=["t2_s"])
                P.add("pool", lambda e, S=S: e.tensor_tensor(out=hv(t2_s[:]), in0=hv(t2_s[:]), in1=S[:, :, 0, 4, :],
                                                             op=ALU.mult), r=[sk, "t2_s"], w=["t2_s"])
                P.add("pool", lambda e: e.tensor_tensor(out=t2_s[:], in0=t2_s[:], in1=tb[:, 6, :], op=ALU.mult),
                      r=["t2_s", "tb"], w=["t2_s"])
                P.add("dve", lambda e: e.tensor_reduce(out=rk_s[:], in_=hv(t2_s[:]), axis=AX.X, op=ALU.add),
                      r=["t2_s"], w=["rk_s"])
                P.mm(pb[PHG][0:4, 0:128], rk_s[:], idt[:], start=True, stop=True, r=["rk_s", "idt"], w=[pk(PHG)])
                P.add("act", lambda e, tsl=tsl: e.activation(out=rkst[:, tsl], in_=pb[PHG][0:4, 0:128], func=AF.Copy),
                      r=[pk(PHG)], w=["rkst"])
                P.dma(tab[t0 + tt * 128:t0 + (tt + 1) * 128, :], S[:].rearrange("p a b c d -> p (a b c d)"),
                      r=[sk], q="act")
            P.dma(rkT[:, t0:t0 + TBK], rkst[:], r=["rkst"], q="act")
        P.emit()
    return nc


def k1_inputs(inp, mods, core):
    F = slice(core * 256, (core + 1) * 256)
    ml, mc = mods[0], mods[1]
    sp = lambda m, i: m[i * D:(i + 1) * D]
    vl = np.stack([inp["norm1_g"][0], sp(ml, 1), sp(ml, 0), sp(mc, 1), sp(mc, 0)] + [inp["rw_mu"][0, m] for m in range(6)])
    tabs = np.stack([inp["rw_w0"][0, 0, F], inp["rw_w0"][0, 1, F], inp["rw_a0"][0, 0, F], inp["rw_a0"][0, 1, F],
                     inp["rw_k_k"][0, F], inp["rw_k_a"][0, F], inp["rw_r_k"][0].reshape(-1)[F]]).reshape(1, -1)
    c = np.ascontiguousarray
    return {
        "xcT": c(inp["ctx"][0].T), "xlT": c(inp["x"][0].T), "vecs": vec_layout(vl),
        "wr": c(inp["rw_w_r"][0][:, F]), "wk": c(inp["rw_w_k"][0][:, F]), "wv": c(inp["rw_w_v"][0][:, F]),
        "g1": c(inp["rw_g1"][0]), "w1": c(inp["rw_w1"][0]), "a1": c(inp["rw_a1"][0]),
        "w2": c(inp["rw_w2"][0][:, :, F]), "a2": c(inp["rw_a2"][0][:, :, F]), "g2": c(inp["rw_g2"][0][:, F]),
        "tabs": c(np.broadcast_to(tabs, (128, tabs.shape[1]))).astype(np.float32),
        "ident": np.eye(128, dtype=np.float32),
    }


GN_EPS = 64e-5


def build_k2(seq=SEQ):
    T = CTX + seq
    CH = 8
    NSLOT = 2
    YC = 512 if seq >= 512 else seq
    nc = bass.Bass("TRN2", target_bir_lowering=False)
    din = lambda n, s: nc.dram_tensor(n, s, F32, kind="ExternalInput").ap()
    dout = lambda n, s: nc.dram_tensor(n, s, F32, kind="ExternalOutput").ap()
    tab = din("tab", [TT, 4, 2, 5 * 64])
    vT = din("vT", [256, TT])
    gT = din("gT", [256, SEQ])
    rkT = din("rkT", [4, TT])
    lnx = din("lnx", [128, 4])
    blk = din("blk", [128, 128])
    esel = din("esel", [4, 2 * 128])
    yT = dout("yT", [2, 256, seq])
    zT = dout("zT", [256, seq])
    P = Prog(nc)
    import contextlib
    with contextlib.ExitStack() as st:
        sb = lambda name, shape: st.enter_context(nc.sbuf_tensor(name, shape, F32))
        chains = [(p, d) for d in range(2) for p in range(2)]
        tbuf = {(p, d, s): sb("tb_%d_%d_%d" % (p, d, s), [128, CH, 5, 64]) for (p, d) in chains for s in range(NSLOT)}
        vt = [sb("vt%d" % p, [128, T]) for p in range(2)]
        S = {c: sb("S_%d_%d" % c, [128, 64]) for c in chains}
        junk = {c: sb("junk_%d_%d" % c, [128, 64]) for c in chains}
        sa = {c: sb("sa_%d_%d" % c, [128, 1]) for c in chains}
        yb = {(p, d, s): sb("yb_%d_%d_%d" % (p, d, s), [128, YC]) for (p, d) in chains for s in range(2)}
        for p in range(2):
            P.dma(vt[p][:], vT[p * 128:(p + 1) * 128, 0:T] if T == TT else vT[p * 128:(p + 1) * 128, 0:T], w=[("vt", p)])
        for c in chains:
            P.add("dve", lambda e, c=c: e.memset(S[c][:], 0.0), w=[("S", c)])
        order = {0: list(range(T)), 1: list(range(CTX - 1, -1, -1)) + list(range(T - 1, CTX - 1, -1))}
        nsteps = T
        nchunks = nsteps // CH
        ycount = {c: 0 for c in chains}

        def load_chunk(ci):
            s = ci % NSLOT
            for (p, d) in chains:
                toks = order[d][ci * CH:(ci + 1) * CH]
                lo = min(toks)
                for h in range(2):
                    src = tab[lo:lo + CH, 2 * p + h, d, :].partition_broadcast(64)
                    P.dma(tbuf[(p, d, s)][h * 64:(h + 1) * 64, :, :, :].rearrange("p t a k -> p t (a k)"), src,
                          w=[("tb", p, d, s, h)])

        load_chunk(0)
        for ci in range(nchunks):
            if ci + 1 < nchunks:
                load_chunk(ci + 1)
            s = ci % NSLOT
            for j in range(CH):
                info = {}
                for c in chains:
                    p, d = c
                    toks = order[d][ci * CH:(ci + 1) * CH]
                    lo = min(toks)
                    t = toks[j]
                    info[c] = (t, t - lo, tbuf[(p, d, s)], [("tb", p, d, s, 0), ("tb", p, d, s, 1)])
                for c in chains:
                    t, jj, tbf, rk = info[c]
                    P.add("dve", lambda e, c=c, jj=jj, tbf=tbf: e.scalar_tensor_tensor(
                        out=junk[c][:], in0=S[c][:], scalar=1.0, in1=tbf[:, jj, 0, :], op0=ALU.mult, op1=ALU.mult,
                        accum_out=sa[c][:]), r=[("S", c)] + rk, w=[("sa", c), ("junk", c)])
                for c in chains:
                    t, jj, tbf, rk = info[c]
                    P.add("dve", lambda e, c=c, jj=jj, tbf=tbf: e.tensor_tensor(
                        out=S[c][:], in0=S[c][:], in1=tbf[:, jj, 1, :], op=ALU.mult), r=[("S", c)] + rk, w=[("S", c)])
                for c in chains:
                    t, jj, tbf, rk = info[c]
                    P.add("dve", lambda e, c=c, jj=jj, tbf=tbf: e.scalar_tensor_tensor(
                        out=S[c][:], in0=tbf[:, jj, 2, :], scalar=sa[c][:, 0:1], in1=S[c][:], op0=ALU.mult, op1=ALU.add),
                        r=[("S", c), ("sa", c)] + rk, w=[("S", c)])
                for c in chains:
                    t, jj, tbf, rk = info[c]
                    P.add("dve", lambda e, c=c, jj=jj, tbf=tbf, t=t: e.scalar_tensor_tensor(
                        out=S[c][:], in0=tbf[:, jj, 3, :], scalar=vt[c[0]][:, t:t + 1], in1=S[c][:], op0=ALU.mult,
                        op1=ALU.add), r=[("S", c), ("vt", c[0])] + rk, w=[("S", c)])
                for c in chains:
                    t, jj, tbf, rk = info[c]
                    if t < CTX:
                        continue
                    p, d = c
                    lt = t - CTX
                    ys = (lt // YC) % 2
                    P.add("dve", lambda e, c=c, jj=jj, tbf=tbf, lt=lt, ys=ys: e.scalar_tensor_tensor(
                        out=junk[c][:], in0=S[c][:], scalar=1.0, in1=tbf[:, jj, 4, :], op0=ALU.mult, op1=ALU.mult,
                        accum_out=yb[(c[0], c[1], ys)][:, lt % YC:lt % YC + 1]),
                        r=[("S", c)] + rk, w=[("yb", p, d, ys), ("junk", c)])
                    ycount[c] += 1
                    if ycount[c] % YC == 0:
                        b0 = (lt // YC) * YC
                        P.dma(yT[d, p * 128:(p + 1) * 128, b0:b0 + YC], yb[(p, d, ys)][:], r=[("yb", p, d, ys)],
                              w=[("yT", d, p, b0)], q="act")
        RB = YC
        lx = sb("lx", [128, 4])
        bk = sb("bk", [128, 128])
        es = sb("es", [4, 2 * 128])
        geps = sb("geps", [128, 1])
        P.dma(lx[:], lnx[:, :], w=["lx"])
        P.dma(bk[:], blk[:, :], w=["bk"])
        P.dma(es[:], esel[:, :], w=["es"])
        P.add("pool", lambda e: e.memset(geps[:], GN_EPS), w=["geps"])
        y0 = [sb("y0_%d" % i, [128, RB]) for i in range(2)]
        y1 = [sb("y1_%d" % i, [128, RB]) for i in range(2)]
        gg = [sb("gg_%d" % i, [128, RB]) for i in range(2)]
        rks = [sb("rks_%d" % i, [4, RB]) for i in range(2)]
        cen = sb("cen", [128, RB])
        sqb = sb("sqb", [128, RB])
        rsd = sb("rsd", [128, RB])
        bon = sb("bon", [128, RB])
        zz = [sb("zz_%d" % i, [128, RB]) for i in range(2)]
        pm = st.enter_context(nc.psum_tensor("pm", [128, 512], F32))
        pv = st.enter_context(nc.psum_tensor("pv", [128, 512], F32))
        pr = st.enter_context(nc.psum_tensor("pr", [128, 512], F32))
        n = 0
        for p in range(2):
            rowsl = slice(p * 128, (p + 1) * 128)
            for b in range(seq // RB):
                i = n % 2
                n += 1
                cs = slice(b * RB, (b + 1) * RB)
                P.dma(y0[i][:], yT[0, rowsl, cs], r=[("yT", 0, p, b * RB)], w=[("y0", i)])
                P.dma(y1[i][:], yT[1, rowsl, cs], r=[("yT", 1, p, b * RB)], w=[("y1", i)])
                P.dma(gg[i][:], gT[rowsl, cs], w=[("gg", i)])
                P.dma(rks[i][:], rkT[:, CTX + b * RB:CTX + (b + 1) * RB], w=[("rks", i)])
                P.add("pool", lambda e, i=i: e.tensor_tensor(out=y0[i][:], in0=y0[i][:], in1=y1[i][:], op=ALU.add),
                      r=[("y0", i), ("y1", i)], w=[("y0", i)])
                P.mm(pm[:, 0:RB], bk[:], y0[i][:], start=True, stop=True, r=["bk", ("y0", i)], w=["pm"])
                P.add("dve", lambda e, i=i: e.tensor_tensor(out=cen[:], in0=y0[i][:], in1=pm[:, 0:RB], op=ALU.subtract),
                      r=[("y0", i), "pm"], w=["cen"])
                P.add("pool", lambda e: e.tensor_tensor(out=sqb[:], in0=cen[:], in1=cen[:], op=ALU.mult),
                      r=["cen"], w=["sqb"])
                P.mm(pv[:, 0:RB], bk[:], sqb[:], start=True, stop=True, r=["bk", "sqb"], w=["pv"])
                P.add("act", lambda e: e.activation(out=rsd[:], in_=pv[:, 0:RB], func=AF.Sqrt, bias=geps[:, 0:1]),
                      r=["pv", "geps"], w=["rsd"])
                P.add("dve", lambda e: e.reciprocal(out=rsd[:], in_=rsd[:]), r=["rsd"], w=["rsd"])
                P.add("dve", lambda e: e.tensor_tensor(out=cen[:], in0=cen[:], in1=rsd[:], op=ALU.mult),
                      r=["cen", "rsd"], w=["cen"])
                P.add("act", lambda e, p=p: e.activation(out=cen[:], in_=cen[:], func=AF.Identity,
                                                         scale=lx[:, 2 * p:2 * p + 1], bias=lx[:, 2 * p + 1:2 * p + 2]),
                      r=["cen", "lx"], w=["cen"])
                P.mm(pr[:, 0:RB], es[:, p * 128:(p + 1) * 128], rks[i][:], start=True, stop=True,
                     r=["es", ("rks", i)], w=["pr"])
                P.add("dve", lambda e, p=p, b=b: e.tensor_tensor(out=bon[:], in0=vt[p][:, CTX + b * RB:CTX + (b + 1) * RB],
                                                                 in1=pr[:, 0:RB], op=ALU.mult),
                      r=["pr", ("vt", p)], w=["bon"])
                P.add("pool", lambda e: e.tensor_tensor(out=bon[:], in0=bon[:], in1=cen[:], op=ALU.add),
                      r=["bon", "cen"], w=["bon"])
                P.add("pool", lambda e, i=i: e.tensor_tensor(out=zz[i][:], in0=bon[:], in1=gg[i][:], op=ALU.mult),
                      r=["bon", ("gg", i)], w=[("zz", i)])
                P.dma(zT[rowsl, cs], zz[i][:], r=[("zz", i)], q="act")
        P.emit()
    return nc


def k2_consts(inp, core):
    F0 = core * 256
    lnx = np.zeros((128, 4), np.float32)
    for p in range(2):
        lnx[:, 2 * p] = inp["rw_lnx_w"][0, F0 + p * 128:F0 + (p + 1) * 128]
        lnx[:, 2 * p + 1] = inp["rw_lnx_b"][0, F0 + p * 128:F0 + (p + 1) * 128]
    blk = np.zeros((128, 128), np.float32)
    blk[:64, :64] = 1.0 / 64
    blk[64:, 64:] = 1.0 / 64
    esel = np.zeros((4, 2, 128), np.float32)
    for p in range(2):
        for h in range(2):
            esel[2 * p + h, p, h * 64:(h + 1) * 64] = 1.0
    return {"lnx": lnx, "blk": blk, "esel": esel.reshape(4, 256)}


def build_k4a(nblocks=None):
    TBK = 256
    NB = TBK + 2
    NBLK = SEQ // TBK
    if nblocks is None:
        nblocks = NBLK
    nc = bass.Bass("TRN2", target_bir_lowering=False)
    din = lambda n, s: nc.dram_tensor(n, s, F32, kind="ExternalInput").ap()
    dout = lambda n, s: nc.dram_tensor(n, s, F32, kind="ExternalOutput").ap()
    xT = din("xT", [D, SEQ]).rearrange("(t p) n -> p t n", p=128)
    vecs = din("vecs", [128, 3 * ND])
    inw = din("inw", [D, 768]).rearrange("(t p) f -> p t f", p=128)
    cv6 = din("cv6", [128, 5 * 6])
    vvT = dout("vvT", [256, SEQ]).rearrange("(h p) t -> p h t", p=128)
    x0T = dout("x0T", [256, SEQ]).rearrange("(h p) t -> p h t", p=128)
    P = Prog(nc)
    import contextlib
    with contextlib.ExitStack() as st:
        sb = lambda name, shape: st.enter_context(nc.sbuf_tensor(name, shape, F32))
        vt = sb("vt", [128, 3 * ND])
        va = sb("va", [128, ND])
        c6 = sb("c6", [128, 5, 6])
        ones = sb("ones", [128, 128])
        epsb = sb("epsb", [128, 1])
        w_s = sb("w_s", [128, ND, 768])
        xb = [sb("xb%d" % i, [128, ND, NB]) for i in range(2)]
        sq = [sb("sq%d" % i, [128, NB]) for i in range(2)]
        tmp = [sb("tmp%d" % i, [128, NB]) for i in range(2)]
        rstd = sb("rstd", [128, NB])
        ub = sb("ub", [128, 6, NB])
        cb = [sb("cb%d" % i, [128, 6, TBK]) for i in range(2)]
        vv = [sb("vv%d" % i, [128, 2, TBK]) for i in range(2)]
        pss = st.enter_context(nc.psum_tensor("pss", [128, 512], F32))
        pu = [st.enter_context(nc.psum_tensor("pu%d" % i, [128, 512], F32)) for i in range(2)]
        P.dma(vt[:], vecs[:, :], w=["vt"])
        P.dma(c6[:].rearrange("p a b -> p (a b)"), cv6[:, :], w=["c6"])
        P.dma(w_s[:], inw, w=["w"])
        P.add("pool", lambda e: e.memset(ones[:], 1.0), w=["ones"])
        P.add("pool", lambda e: e.memset(epsb[:], RMS_EPS), w=["epsb"])
        P.add("dve", lambda e: e.tensor_scalar(out=va[:], in0=vt[:, ND:2 * ND], scalar1=1.0, scalar2=None, op0=ALU.add),
              r=["vt"], w=["va"])
        P.add("dve", lambda e: e.tensor_tensor(out=va[:], in0=va[:], in1=vt[:, 0:ND], op=ALU.mult), r=["vt", "va"],
              w=["va"])
        for bi in range(nblocks):
            x = xb[bi % 2]
            xk = ("xb", bi % 2)
            t0 = bi * TBK
            a = max(t0 - 1, 0)
            b = min(t0 + TBK + 1, SEQ)
            if t0 == 0:
                P.add("pool", lambda e, x=x: e.memset(x[:, :, 0:1], 0.0), w=[xk])
            if t0 + TBK == SEQ:
                P.add("pool", lambda e, x=x: e.memset(x[:, :, NB - 1:NB], 0.0), w=[xk])
            P.dma(x[:, :, 1 + (a - t0):1 + (b - t0)], xT[:, :, a:b], w=[xk])
            for t in range(ND):
                s = sq[t % 2]
                P.add("act", lambda e, s=s, t=t, x=x: e.activation(out=s[:], in_=x[:, t, :], func=AF.Square),
                      r=[xk], w=[("sq", t % 2)])
                P.mm(pss[:, 0:NB], ones[:], s[:], start=(t == 0), stop=(t == ND - 1), r=[("sq", t % 2), "ones"],
                     w=["pss"])
            P.add("act", lambda e: e.activation(out=rstd[:], in_=pss[:, 0:NB], func=AF.Sqrt, scale=1.0 / D,
                                                bias=epsb[:, 0:1]), r=["pss", "epsb"], w=["rstd"])
            P.add("dve", lambda e: e.reciprocal(out=rstd[:], in_=rstd[:]), r=["rstd"], w=["rstd"])
            for t in range(ND):
                tt = tmp[t % 2]
                P.add("dve", lambda e, tt=tt, t=t, x=x: e.tensor_tensor(out=tt[:], in0=x[:, t, :], in1=rstd[:],
                                                                        op=ALU.mult), r=[xk, "rstd"], w=[("tmp", t % 2)])
                P.add("act", lambda e, tt=tt, t=t, x=x: e.activation(
                    out=x[:, t, :], in_=tt[:], func=AF.Identity, scale=va[:, t:t + 1],
                    bias=vt[:, 2 * ND + t:2 * ND + t + 1]), r=[("tmp", t % 2), "va", "vt"], w=[xk])
            for j in range(6):
                pj = pu[j % 2]
                for t in range(ND):
                    P.mm(pj[:, 0:NB], w_s[:, t, j * 128:(j + 1) * 128], x[:, t, :], start=(t == 0), stop=(t == ND - 1),
                         r=["w", xk], w=[("pu", j % 2)])
                P.add("act", lambda e, j=j, pj=pj: e.activation(out=ub[:, j, :], in_=pj[:, 0:NB], func=AF.Identity,
                                                                bias=c6[:, 0, j:j + 1]), r=[("pu", j % 2), "c6"],
                      w=["ub"])
            if t0 == 0:
                P.add("pool", lambda e: e.memset(ub[:, :, 0:1], 0.0), w=["ub"])
            if t0 + TBK == SEQ:
                P.add("pool", lambda e: e.memset(ub[:, :, NB - 1:NB], 0.0), w=["ub"])
            c = cb[bi % 2]
            ck = ("cb", bi % 2)
            for j in range(6):
                P.add("dve", lambda e, j=j, c=c: e.tensor_scalar(out=c[:, j, :], in0=ub[:, j, 0:TBK],
                                                                 scalar1=c6[:, 1, j:j + 1], scalar2=c6[:, 4, j:j + 1],
                                                                 op0=ALU.mult, op1=ALU.add), r=["ub", "c6"], w=[ck])
                P.add("dve", lambda e, j=j, c=c: e.scalar_tensor_tensor(out=c[:, j, :], in0=ub[:, j, 1:TBK + 1],
                                                                        scalar=c6[:, 2, j:j + 1], in1=c[:, j, :],
                                                                        op0=ALU.mult, op1=ALU.add),
                      r=["ub", "c6", ck], w=[ck])
                P.add("dve", lambda e, j=j, c=c: e.scalar_tensor_tensor(out=c[:, j, :], in0=ub[:, j, 2:TBK + 2],
                                                                        scalar=c6[:, 3, j:j + 1], in1=c[:, j, :],
                                                                        op0=ALU.mult, op1=ALU.add),
                      r=["ub", "c6", ck], w=[ck])
            v_ = vv[bi % 2]
            P.add("pool", lambda e, c=c, v_=v_: e.tensor_tensor(out=v_[:], in0=c[:, 4:6, :], in1=c[:, 2:4, :],
                                                                op=ALU.mult), r=[ck], w=[("vv", bi % 2)])
            P.dma(vvT[:, :, t0:t0 + TBK], v_[:], r=[("vv", bi % 2)], q="act")
            P.dma(x0T[:, :, t0:t0 + TBK], c[:, 0:2, :], r=[ck], q="act")
        P.emit()
    return nc


def k4a_inputs(inp, mods, xT, core):
    C = np.arange(core * 256, (core + 1) * 256)
    cols = np.concatenate([C, D + C, 2 * D + C])
    m1 = mods[2]
    sp = lambda m, i: m[i * D:(i + 1) * D]
    vl = np.stack([inp["norm1_g"][1], sp(m1, 1), sp(m1, 0)])
    c6 = np.stack([inp["hy_in_b"][0][cols], inp["hy_short_w"][0][0][cols], inp["hy_short_w"][0][1][cols],
                   inp["hy_short_w"][0][2][cols], inp["hy_short_b"][0][cols]])
    c6 = np.ascontiguousarray(c6.reshape(5, 6, 128).transpose(2, 0, 1).reshape(128, 30)).astype(np.float32)
    return {"xT": xT, "vecs": vec_layout(vl), "inw": np.ascontiguousarray(inp["hy_in_w"][0][:, cols]), "cv6": c6}


NFFT = 2 * SEQ
MAGIC = 12582912.0


def hy_consts(core):
    import math
    L = SEQ
    a = np.arange(128)
    ang = 2 * np.pi * np.outer(a, a) / 128.0
    C = np.cos(ang)
    S = np.sin(ang)
    dftc = np.stack([-S, C, S, C, -S, C / NFFT, -S / NFFT], 1)
    ang2 = 2 * np.pi * np.outer(a, a) / NFFT
    tw = np.stack([np.cos(ang2), np.cos(ang2), np.sin(ang2), np.sin(ang2)], 1)
    t = np.linspace(0.0, 1.0, L, dtype=np.float32).astype(np.float64)
    w = (2 * math.pi * np.arange(L, dtype=np.float32) / L).astype(np.float64)[:, None]
    f = np.linspace(1e-4, 15, 16, dtype=np.float32).astype(np.float64)[None, :]
    z = np.concatenate([t[:, None], np.cos(f * w), -np.sin(f * w)], axis=-1)
    n = np.arange(NFFT)
    tidx = np.where(n < L, n, NFFT - n)
    tidx = np.minimum(tidx, L - 1)
    n1 = np.arange(128)[None, :]
    n2 = np.arange(128)[:, None]
    nq = (128 * n1 + n2).reshape(-1)
    z2T = np.ascontiguousarray(z[tidx[nq]].T)
    MAXD = math.log(1e-2) / 0.3
    MIND = math.log(1e-2) / 1.5
    deltas = np.abs(np.linspace(MIND, MAXD, D, dtype=np.float32).astype(np.float64))[core * 256:(core + 1) * 256]
    nn = (128 * np.arange(128)[:, None] + np.arange(128)[None, :])
    tt = t[np.minimum(np.where(nn < L, nn, NFFT - nn), L - 1)]
    win = np.exp(-tt[:, None, :] * deltas[None, :, None])
    win[nn[:, None, :].repeat(256, 1) == L] = 0.0
    f32 = lambda x: np.ascontiguousarray(x.astype(np.float32))
    return {"dftc": f32(dftc), "tw": f32(tw), "z2T": f32(z2T), "win": f32(win)}


def build_k4b(ngroups=None):
    G = 32
    NG = 256 // G
    if ngroups is None:
        ngroups = NG
    nc = bass.Bass("TRN2", target_bir_lowering=False)
    din = lambda n, s: nc.dram_tensor(n, s, F32, kind="ExternalInput").ap()
    dout = lambda n, s: nc.dram_tensor(n, s, F32, kind="ExternalOutput").ap()
    vvT = din("vvT", [256, SEQ])
    x0T = din("x0T", [256, SEQ])
    hb = din("hb", [128, 2])
    fw0 = din("fw0", [33, 64])
    fw12 = din("fw12", [128, 2, 64])
    fvec = din("fvec", [128, 4])
    fwout = din("fwout", [128, 256 // 32, 2, 32])
    dftc_d = din("dftc", [128, 7, 128])
    tw_d = din("tw", [128, 4, 128])
    z2T = din("z2T", [33, NFFT])
    win = din("win", [128, 256, 128])
    cvT = dout("cvT", [256, SEQ])
    zT = dout("zT", [256, SEQ])
    P = Prog(nc)
    import contextlib, math
    with contextlib.ExitStack() as st:
        sb = lambda name, shape: st.enter_context(nc.sbuf_tensor("s_" + name, shape, F32))
        dc = sb("dc", [128, 7, 128])
        tw = sb("tw", [128, 4, 128])
        w0_s = sb("w0_s", [33, 64])
        w12_s = sb("w12_s", [128, 2, 64])
        fv = sb("fv", [128, 4])
        fb = sb("fb", [128, 3])
        wo_s = sb("wo_s", [128, 256 // 32, 2, 32])
        hb_s = sb("hb_s", [128, 2])
        hid = sb("hid", [128, SEQ])
        zc = [sb("zc%d" % i, [33, 2, 512]) for i in range(2)]
        ha = sb("ha", [128, 512])
        hq = sb("hq", [128, 512])
        fil = sb("fil", [128, G, 128])
        wn = sb("wn", [128, G, 128])
        xin = sb("xin", [64, G, 128])
        Hre = sb("Hre", [128, G, 128])
        Him = sb("Him", [128, G, 128])
        Xre = sb("Xre", [128, G, 128])
        Xim = sb("Xim", [128, G, 128])
        Bre = [sb("Bre%d" % i, [128, 4, 128]) for i in range(2)]
        Bim = [sb("Bim%d" % i, [128, 4, 128]) for i in range(2)]
        ta = [sb("ta%d" % i, [128, 2, 128]) for i in range(2)]
        tbb = [sb("tbb%d" % i, [128, 2, 128]) for i in range(2)]
        yo = xin
        tq = fil
        pb = [st.enter_context(nc.psum_tensor("pb%d" % i, [128, 512], F32)) for i in range(8)]
        pk = lambda i: ("pb", i)
        P.dma(dc[:], dftc_d[:, :, :], w=["dc"])
        P.dma(tw[:], tw_d[:, :, :], w=["tw"])
        P.dma(w0_s[:], fw0[:, :], w=["w0"])
        P.dma(w12_s[:], fw12[:, :, :], w=["w12"])
        P.dma(fv[:], fvec[:, :], w=["fv"])
        P.dma(wo_s[:], fwout[:, :, :, :], w=["wo"])
        P.dma(hb_s[:], hb[:, :], w=["hb"])
        for l in range(3):
            P.add("dve", lambda e, l=l: e.tensor_tensor(out=fb[:, l:l + 1], in0=fv[:, 1 + l:2 + l], in1=fv[:, 0:1],
                                                        op=ALU.mult), r=["fv"], w=[("fb", l)])
        FBK = [("fb", 0), ("fb", 1), ("fb", 2), "fv"]

        def sin_layer(psrc, psk, l, dst, dstk):
            P.add("dve", lambda e: e.tensor_scalar(out=ha[:], in0=psrc, scalar1=fv[:, 0:1], scalar2=fb[:, l:l + 1],
                                                   op0=ALU.mult, op1=ALU.add), r=[psk] + FBK, w=["ha"])
            P.add("dve", lambda e: e.tensor_scalar(out=hq[:], in0=ha[:], scalar1=1.0 / (2 * math.pi), scalar2=MAGIC,
                                                   op0=ALU.mult, op1=ALU.add), r=["ha"], w=["hq"])
            P.add("dve", lambda e: e.tensor_scalar(out=hq[:], in0=hq[:], scalar1=-MAGIC, scalar2=-2 * math.pi,
                                                   op0=ALU.add, op1=ALU.mult), r=["hq"], w=["hq"])
            P.add("dve", lambda e: e.tensor_tensor(out=ha[:], in0=ha[:], in1=hq[:], op=ALU.add), r=["ha", "hq"],
                  w=["ha"])
            P.add("dve", lambda e: e.tensor_scalar(out=ha[:], in0=ha[:], scalar1=math.pi, scalar2=-math.pi,
                                                   op0=ALU.min, op1=ALU.max), r=["ha"], w=["ha"])
            P.add("act", lambda e: e.activation(out=dst, in_=ha[:], func=AF.Sin), r=["ha"], w=[dstk])

        hx = [sb("hx%d" % i, [128, 512]) for i in range(2)]
        for ch in range(SEQ // 512):
            i = ch % 2
            q0 = ch * 512
            for hf in range(2):
                P.dma(zc[i][:, hf, :], z2T[:, hf * SEQ + q0:hf * SEQ + q0 + 512], w=[("zc", i, hf)])
            for hf in range(2):
                P.mm(pb[6][hf * 64:(hf + 1) * 64, :], w0_s[:, :], zc[i][:, hf, :], start=True, stop=True,
                     r=["w0", ("zc", i, hf)], w=[pk(6)])
            sin_layer(pb[6][:, :], pk(6), 0, hx[0][:], "hx0")
            for hf in range(2):
                rs = slice(hf * 64, (hf + 1) * 64)
                P.mm(pb[7][rs, :], w12_s[rs, 0, :], hx[0][rs, :], start=True, stop=True, r=["w12", "hx0"], w=[pk(7)])
            sin_layer(pb[7][:, :], pk(7), 1, hx[1][:], "hx1")
            for hf in range(2):
                rs = slice(hf * 64, (hf + 1) * 64)
                P.mm(pb[6][rs, :], w12_s[rs, 1, :], hx[1][rs, :], start=True, stop=True, r=["w12", "hx1"], w=[pk(6)])
            sin_layer(pb[6][:, :], pk(6), 2, hid[:, q0:q0 + 512], "hid")

        TC2 = tw[:, 0:2, :]
        TS2 = tw[:, 2:4, :]

        def fwd_fft(src, srck, K, dre, dim, dk):
            for b4 in range(G // 4):
                i = b4 % 2
                for j2 in range(2):
                    bank = pb[j2]
                    for cc in range(2):
                        c = b4 * 4 + j2 * 2 + cc
                        P.mm(bank[:, cc * 256:(cc + 1) * 256], src[0:K, c, :], dc[0:K, 3:5, :], start=True, stop=True,
                             r=[srck, "dc"], w=[pk(j2)])
                    A = bank[:, :].rearrange("p (c r k) -> p c r k", c=2, r=2)
                    cs = slice(j2 * 2, j2 * 2 + 2)
                    bk = ("B", i)
                    P.add("dve", lambda e, A=A, cs=cs, i=i: e.tensor_tensor(out=Bre[i][:, cs, :], in0=A[:, :, 0, :],
                                                                            in1=TC2, op=ALU.mult),
                          r=[pk(j2), "tw"], w=[("Bre", i, j2)])
                    P.add("dve", lambda e, A=A, j2=j2: e.tensor_tensor(out=ta[j2][:], in0=A[:, :, 1, :], in1=TS2,
                                                                        op=ALU.mult), r=[pk(j2), "tw"], w=[("ta", j2)])
                    P.add("pool", lambda e, cs=cs, i=i, j2=j2: e.tensor_tensor(out=Bre[i][:, cs, :], in0=Bre[i][:, cs, :],
                                                                               in1=ta[j2][:], op=ALU.add),
                          r=[("Bre", i, j2), ("ta", j2)], w=[("Bre", i, j2)])
                    P.add("dve", lambda e, A=A, cs=cs, i=i: e.tensor_tensor(out=Bim[i][:, cs, :], in0=A[:, :, 1, :],
                                                                            in1=TC2, op=ALU.mult),
                          r=[pk(j2), "tw"], w=[("Bim", i, j2)])
                    P.add("dve", lambda e, A=A, j2=j2: e.tensor_tensor(out=tbb[j2][:], in0=A[:, :, 0, :], in1=TS2,
                                                                        op=ALU.mult), r=[pk(j2), "tw"], w=[("tbb", j2)])
                    P.add("pool", lambda e, cs=cs, i=i, j2=j2: e.tensor_tensor(out=Bim[i][:, cs, :], in0=Bim[i][:, cs, :],
                                                                               in1=tbb[j2][:], op=ALU.subtract),
                          r=[("Bim", i, j2), ("tbb", j2)], w=[("Bim", i, j2)])
                br = Bre[i][:].rearrange("p c k -> p (c k)")
                bi_ = Bim[i][:].rearrange("p c k -> p (c k)")
                rk = [("Bre", i, 0), ("Bre", i, 1), ("Bim", i, 0), ("Bim", i, 1), "dc"]
                P.mm(pb[2 + i][:, :], dc[:, 1, :], br, start=True, stop=False, r=rk, w=[pk(2 + i)])
                P.mm(pb[2 + i][:, :], dc[:, 2, :], bi_, start=False, stop=True, r=rk, w=[pk(2 + i)])
                P.mm(pb[4 + i][:, :], dc[:, 1, :], bi_, start=True, stop=False, r=rk, w=[pk(4 + i)])
                P.mm(pb[4 + i][:, :], dc[:, 0, :], br, start=False, stop=True, r=rk, w=[pk(4 + i)])
                P.add("act", lambda e, b4=b4, i=i: e.activation(
                    out=dre[:, b4 * 4:(b4 + 1) * 4, :].rearrange("p c k -> p (c k)"), in_=pb[2 + i][:, :], func=AF.Copy),
                    r=[pk(2 + i)], w=[dk + "re"])
                P.add("act", lambda e, b4=b4, i=i: e.activation(
                    out=dim[:, b4 * 4:(b4 + 1) * 4, :].rearrange("p c k -> p (c k)"), in_=pb[4 + i][:, :], func=AF.Copy),
                    r=[pk(4 + i)], w=[dk + "im"])

        for g in range(ngroups):
            cg = slice(g * G, (g + 1) * G)
            P.dma(wn[:], win[:, cg, :], w=["wn"])
            for nb in range(16):
                bank = pb[6 + nb % 2]
                for j in range(8):
                    n2 = nb * 8 + j
                    hf = n2 // 64
                    rs = slice(hf * 64, (hf + 1) * 64)
                    P.mm(bank[:, j * 64:(j + 1) * 64], hid[rs, (n2 % 64) * 128:(n2 % 64 + 1) * 128],
                         wo_s[rs, g, :, :].rearrange("p a c -> p (a c)"), start=True, stop=True,
                         r=["hid", "wo"], w=[pk(6 + nb % 2)])
                bv = bank[:, :].rearrange("p (n a c) -> p a c n", n=8, a=2)
                ns = slice(nb * 8, (nb + 1) * 8)
                P.add("dve", lambda e, bv=bv, ns=ns: e.tensor_tensor(out=fil[0:64, :, ns], in0=bv[0:64, 0, :, :],
                                                                     in1=wn[0:64, :, ns], op=ALU.mult),
                      r=[pk(6 + nb % 2), "wn"], w=["fil"])
                P.add("dve", lambda e, bv=bv, ns=ns: e.tensor_tensor(out=fil[64:128, :, ns], in0=bv[64:128, 1, :, :],
                                                                     in1=wn[64:128, :, ns], op=ALU.mult),
                      r=[pk(6 + nb % 2), "wn"], w=["fil"])
            fwd_fft(fil, "fil", 128, Hre, Him, "H")
            P.dma(xin[:], vvT[cg, :].rearrange("c (a b) -> a c b", b=128), w=["xin"])
            fwd_fft(xin, "xin", 64, Xre, Xim, "X")
            f2 = lambda t: t[:].rearrange("p c k -> p (c k)")
            P.add("dve", lambda e: e.tensor_tensor(out=f2(tq), in0=f2(Xim), in1=f2(Him), op=ALU.mult),
                  r=["Xim", "Him"], w=["fil"])
            P.add("pool", lambda e: e.tensor_tensor(out=f2(Xim), in0=f2(Xim), in1=f2(Hre), op=ALU.mult),
                  r=["Xim", "Hre"], w=["Xim"])
            P.add("dve", lambda e: e.tensor_tensor(out=f2(Him), in0=f2(Xre), in1=f2(Him), op=ALU.mult),
                  r=["Xre", "Him"], w=["Him"])
            P.add("pool", lambda e: e.tensor_tensor(out=f2(Xim), in0=f2(Xim), in1=f2(Him), op=ALU.add),
                  r=["Xim", "Him"], w=["Xim"])
            P.add("dve", lambda e: e.tensor_tensor(out=f2(Xre), in0=f2(Xre), in1=f2(Hre), op=ALU.mult),
                  r=["Xre", "Hre"], w=["Xre"])
            P.add("pool", lambda e: e.tensor_tensor(out=f2(Xre), in0=f2(Xre), in1=f2(tq), op=ALU.subtract),
                  r=["Xre", "fil"], w=["Xre"])
            for b4 in range(G // 4):
                i = b4 % 2
                for j2 in range(2):
                    bank = pb[j2]
                    for cc in range(2):
                        c = b4 * 4 + j2 * 2 + cc
                        P.mm(bank[:, cc * 256:(cc + 1) * 256], Xre[:, c, :], dc[:, 1:3, :], start=True, stop=False,
                             r=["Xre", "Xim", "dc"], w=[pk(j2)])
                        P.mm(bank[:, cc * 256:(cc + 1) * 256], Xim[:, c, :], dc[:, 0:2, :], start=False, stop=True,
                             r=["Xre", "Xim", "dc"], w=[pk(j2)])
                    A = bank[:, :].rearrange("p (c r k) -> p c r k", c=2, r=2)
                    cs = slice(j2 * 2, j2 * 2 + 2)
                    P.add("dve", lambda e, A=A, cs=cs, i=i: e.tensor_tensor(out=Bre[i][:, cs, :], in0=A[:, :, 0, :],
                                                                            in1=TC2, op=ALU.mult),
                          r=[pk(j2), "tw"], w=[("Bre", i, j2)])
                    P.add("dve", lambda e, A=A, j2=j2: e.tensor_tensor(out=ta[j2][:], in0=A[:, :, 1, :], in1=TS2,
                                                                        op=ALU.mult), r=[pk(j2), "tw"], w=[("ta", j2)])
                    P.add("pool", lambda e, cs=cs, i=i, j2=j2: e.tensor_tensor(out=Bre[i][:, cs, :], in0=Bre[i][:, cs, :],
                                                                               in1=ta[j2][:], op=ALU.subtract),
                          r=[("Bre", i, j2), ("ta", j2)], w=[("Bre", i, j2)])
                    P.add("dve", lambda e, A=A, cs=cs, i=i: e.tensor_tensor(out=Bim[i][:, cs, :], in0=A[:, :, 1, :],
                                                                            in1=TC2, op=ALU.mult),
                          r=[pk(j2), "tw"], w=[("Bim", i, j2)])
                    P.add("dve", lambda e, A=A, j2=j2: e.tensor_tensor(out=tbb[j2][:], in0=A[:, :, 0, :], in1=TS2,
                                                                        op=ALU.mult), r=[pk(j2), "tw"], w=[("tbb", j2)])
                    P.add("pool", lambda e, cs=cs, i=i, j2=j2: e.tensor_tensor(out=Bim[i][:, cs, :], in0=Bim[i][:, cs, :],
                                                                               in1=tbb[j2][:], op=ALU.add),
                          r=[("Bim", i, j2), ("tbb", j2)], w=[("Bim", i, j2)])
                br = Bre[i][:].rearrange("p c k -> p (c k)")
                bi_ = Bim[i][:].rearrange("p c k -> p (c k)")
                rk = [("Bre", i, 0), ("Bre", i, 1), ("Bim", i, 0), ("Bim", i, 1), "dc"]
                P.mm(pb[2 + i][0:64, :], dc[:, 5, 0:64], br, start=True, stop=False, r=rk, w=[pk(2 + i)])
                P.mm(pb[2 + i][0:64, :], dc[:, 6, 0:64], bi_, start=False, stop=True, r=rk, w=[pk(2 + i)])
                P.add("act", lambda e, b4=b4, i=i: e.activation(
                    out=yo[:, b4 * 4:(b4 + 1) * 4, :].rearrange("p c k -> p (c k)"), in_=pb[2 + i][0:64, :], func=AF.Copy),
                    r=[pk(2 + i)], w=["xin"])
            P.dma(cvT[cg, :].rearrange("c (a b) -> a c b", b=128), yo[:], r=["xin"], w=[("cvT", g)], q="act")
        RB = 512
        cvb = [sb("cvb%d" % i, [128, RB]) for i in range(2)]
        vvb = [sb("vvb%d" % i, [128, RB]) for i in range(2)]
        x0b = [sb("x0b%d" % i, [128, RB]) for i in range(2)]
        n = 0
        for h in range(2):
            if (h + 1) * 128 > ngroups * G:
                break
            rs = slice(h * 128, (h + 1) * 128)
            for b in range(SEQ // RB):
                i = n % 2
                n += 1
                cs = slice(b * RB, (b + 1) * RB)
                P.dma(cvb[i][:], cvT[rs, cs], r=[("cvT", g) for g in range(h * 4, h * 4 + 4)], w=[("cvb", i)])
                P.dma(vvb[i][:], vvT[rs, cs], w=[("vvb", i)])
                P.dma(x0b[i][:], x0T[rs, cs], w=[("x0b", i)])
                P.add("dve", lambda e, i=i, h=h: e.scalar_tensor_tensor(
                    out=cvb[i][:], in0=vvb[i][:], scalar=hb_s[:, h:h + 1], in1=cvb[i][:], op0=ALU.mult, op1=ALU.add),
                    r=[("cvb", i), ("vvb", i), "hb"], w=[("cvb", i)])
                P.add("pool", lambda e, i=i: e.tensor_tensor(out=cvb[i][:], in0=cvb[i][:], in1=x0b[i][:], op=ALU.mult),
                      r=[("cvb", i), ("x0b", i)], w=[("cvb", i)])
                P.dma(zT[rs, cs], cvb[i][:], r=[("cvb", i)], q="act")
        P.emit()
    return nc


def k4b_inputs(inp, core, consts, vvT, x0T):
    C = slice(core * 256, (core + 1) * 256)
    dup = lambda a: np.concatenate([a, a], 0)
    fw12 = dup(np.stack([inp["hy_f_w1"][0], inp["hy_f_w2"][0]], 1))
    fvec = dup(np.stack([inp["hy_f_freq"][0], inp["hy_f_b0"][0], inp["hy_f_b1"][0], inp["hy_f_b2"][0]], 1))
    wo = inp["hy_f_wout"][0]
    wof = wo[:, :D][:, C].reshape(64, 8, 32)
    wob = wo[:, D:][:, C].reshape(64, 8, 32)
    fwout = dup(np.stack([wof, wob], 2))
    hbv = inp["hy_bias"][0][C].reshape(2, 128).T
    c = lambda x: np.ascontiguousarray(x.astype(np.float32))
    m = {"vvT": vvT, "x0T": x0T, "hb": c(hbv), "fw0": c(inp["hy_f_w0"][0]), "fw12": c(fw12), "fvec": c(fvec),
         "fwout": c(fwout)}
    m["dftc"] = consts["dftc"]
    m["tw"] = consts["tw"]
    m["z2T"] = consts["z2T"]
    m["win"] = consts["win"]
    return m


def k3_inputs(xT_c, zT_c, wo, ob, mods_l, norm2_g, w1, w3, w2, final_g):
    sp = lambda i: mods_l[i * D:(i + 1) * D]
    vs = np.stack([ob, sp(2), norm2_g, sp(4), sp(3), sp(5), final_g])
    return {"xT": xT_c, "zT": zT_c, "wo": wo, "w1": w1, "w3": w3, "w2": w2, "vecs": vec_layout(vs)}


def kernel(**inp):
    inp = {k: np.asarray(v) for k, v in inp.items()}
    c_ = np.ascontiguousarray
    NTc = SEQ // NCORES
    mods = run_k0(inp)
    nc1 = build_k1()
    r1 = _run(nc1, [k1_inputs(inp, mods, c) for c in range(NCORES)])
    nc2 = build_k2()
    m2 = []
    for c in range(NCORES):
        m = {"tab": r1[c]["tab"].reshape(TT, 4, 2, 320), "vT": r1[c]["vT"], "gT": r1[c]["gT"], "rkT": r1[c]["rkT"]}
        m.update(k2_consts(inp, c))
        m2.append(m)
    r2 = _run(nc2, m2)
    zT = np.concatenate([r2[c]["zT"] for c in range(NCORES)], axis=0)
    xT = c_(inp["x"][0].T)
    nc3 = build_k3(NTc, False)
    zeros = np.zeros(D, np.float32)
    m3 = []
    for c in range(NCORES):
        ts = slice(c * NTc, (c + 1) * NTc)
        m3.append(k3_inputs(c_(xT[:, ts]), c_(zT[:, ts]), inp["rw_w_o"][0], zeros, mods[0], inp["norm2_g"][0],
                            inp["ffn_w1"][0], inp["ffn_w3"][0], inp["ffn_w2"][0], inp["final_g"]))
    r3 = _run(nc3, m3)
    x1T = c_(np.concatenate([r3[c]["oT"] for c in range(NCORES)], axis=1))
    nc4a = build_k4a()
    r4a = _run(nc4a, [k4a_inputs(inp, mods, x1T, c) for c in range(NCORES)])
    nc4b = build_k4b()
    r4b = _run(nc4b, [k4b_inputs(inp, c, hy_consts(c), r4a[c]["vvT"], r4a[c]["x0T"]) for c in range(NCORES)])
    z1T = np.concatenate([r4b[c]["zT"] for c in range(NCORES)], axis=0)
    nc5 = build_k3(NTc, True)
    m5 = []
    for c in range(NCORES):
        ts = slice(c * NTc, (c + 1) * NTc)
        m5.append(k3_inputs(c_(x1T[:, ts]), c_(z1T[:, ts]), inp["hy_out_w"][0], inp["hy_out_b"][0], mods[2],
                            inp["norm2_g"][1], inp["ffn_w1"][1], inp["ffn_w3"][1], inp["ffn_w2"][1], inp["final_g"]))
    r5 = _run(nc5, m5)
    oT = np.concatenate([r5[c]["oT"] for c in range(NCORES)], axis=1)
    return c_(oT.T).reshape(1, SEQ, D).astype(np.float32)
```

```python
import numpy as np
import concourse.bass as bass
import concourse.mybir as mybir
from concourse.bass_utils import run_bass_kernel_spmd

F32 = mybir.dt.float32
ALU = mybir.AluOpType
AF = mybir.ActivationFunctionType
AX = mybir.AxisListType

NCORES = 8
D = 2048
SEQ = 8192
CTX = 256
FF = 5632
ND = D // 128
NF = FF // 128
RMS_EPS = 1e-6
import os
OUTQ = os.environ.get("OUTQ", "act")
SAME_ENG_DIST = int(os.environ.get("SAME_ENG_DIST", "1000000000"))


class _Op:
    __slots__ = ("eng", "fn", "deps", "idx", "dma", "inc", "semi", "semval", "cnt", "incamt")

    def __init__(self, eng, fn, dma):
        self.eng = eng
        self.fn = fn
        self.dma = dma
        self.deps = []
        self.inc = dma
        self.semi = None
        self.semval = None
        self.cnt = None
        self.incamt = 16


class Prog:
    ENGS = ("pe", "act", "dve", "pool", "sp")
    NDMASEM = {"sp": 16, "act": 6, "pool": 6, "pe": 2, "dve": 2}

    def __init__(self, nc):
        self.nc = nc
        self.ops = {e: [] for e in self.ENGS}
        self.lastw = {}
        self.readers = {}
        self.ndma = {e: 0 for e in self.ENGS}
        self.dma_prev = {}

    def add(self, eng, fn, r=(), w=(), dma=False, cc=False):
        op = _Op(eng, fn, dma or cc)
        lst = self.ops[eng]
        op.idx = len(lst)
        deps = []
        for k in r:
            lw = self.lastw.get(k)
            if lw is not None:
                deps.append(lw)
            if isinstance(k, tuple) and k[0] == "ps":
                for rd in self.readers.get(k, {}).values():
                    if rd.eng != eng:
                        deps.append(rd)
        for k in w:
            lw = self.lastw.get(k)
            if lw is not None:
                deps.append(lw)
            for rd in self.readers.get(k, {}).values():
                deps.append(rd)
        if cc:
            self.ncc = getattr(self, "ncc", 0) + 1
            op.semi = ("cc", self.ncc)
            op.semval = 1
            op.incamt = 1
            self.dma_prev[op.semi] = op
            dma = True
        elif dma:
            q = self.ndma[eng]
            self.ndma[eng] += 1
            op.semi = (eng, q % self.NDMASEM[eng])
            prev = self.dma_prev.get(op.semi)
            if prev is not None:
                deps.append(prev)
                op.semval = prev.semval + 16
            else:
                op.semval = 16
            self.dma_prev[op.semi] = op
        seen = set()
        for dp in deps:
            if dp is op or id(dp) in seen:
                continue
            seen.add(id(dp))
            if dp.eng == eng and not dp.dma and not dma:
                if eng == "pe" or op.idx - dp.idx > SAME_ENG_DIST:
                    continue
            dp.inc = True
            op.deps.append(dp)
        for k in r:
            self.readers.setdefault(k, {})[eng if not dma else (eng, "dma", op.idx)] = op
        for k in w:
            self.lastw[k] = op
            self.readers[k] = {}
        lst.append(op)
        return op

    def dma(self, out, in_, r=(), w=(), q="sp", **kw):
        return self.add(q, lambda e: e.dma_start(out=out, in_=in_, **kw), r=r, w=w, dma=True)

    def mm(self, out, lhsT, rhs, start, stop, r=(), w=()):
        return self.add("pe", lambda e: e.matmul(out, lhsT, rhs, start=start, stop=stop), r=r, w=w)

    def emit(self):
        nc = self.nc
        import contextlib
        with contextlib.ExitStack() as st:
            csem = {}
            dsem = {}
            for i in range(1, getattr(self, "ncc", 0) + 1):
                dsem[("cc", i)] = st.enter_context(nc.semaphore("cc_%d" % i))
            for e in ("pe", "act", "dve", "pool"):
                csem[e] = st.enter_context(nc.semaphore("c_" + e))
            for e in self.ENGS:
                for i in range(min(self.NDMASEM[e], self.ndma[e])):
                    dsem[(e, i)] = st.enter_context(nc.semaphore("d_%s_%d" % (e, i)))
            for e in ("pe", "act", "dve", "pool"):
                c = 0
                for op in self.ops[e]:
                    if not op.dma and op.inc:
                        c += 1
                        op.cnt = c
            final_dma = list(self.dma_prev.values())
            block = st.enter_context(nc.Block())

            def section(ename):
                def body(eng):
                    known = {}
                    for op in self.ops[ename]:
                        for dp in op.deps:
                            if dp.dma:
                                sem, val = dsem[dp.semi], dp.semval
                            else:
                                sem, val = csem[dp.eng], dp.cnt
                            kk = id(sem)
                            if known.get(kk, 0) >= val:
                                continue
                            known[kk] = val
                            eng.wait_ge(sem, val)
                        ins = op.fn(eng)
                        if op.dma:
                            if op.incamt == 1:
                                ins.then_inc(dsem[op.semi])
                            else:
                                ins.then_inc(dsem[op.semi], op.incamt)
                        elif op.inc:
                            ins.then_inc(csem[ename], 1)
                    if ename == "sp":
                        for dp in final_dma:
                            eng.wait_ge(dsem[dp.semi], dp.semval)
                return body

            block.tensor(section("pe"))
            block.scalar(section("act"))
            block.vector(section("dve"))
            block.gpsimd(section("pool"))
            block.sync(section("sp"))


def _run(nc, in_maps):
    res = run_bass_kernel_spmd(nc, in_maps, core_ids=list(range(len(in_maps))))
    return res.results


def vec_layout(v):
    v = np.asarray(v, np.float32).reshape(-1, v.shape[-1])
    n, L = v.shape
    return np.ascontiguousarray(v.reshape(n, L // 128, 128).transpose(2, 0, 1).reshape(128, n * (L // 128)))


def build_k3(NT, final):
    TB = 256
    NB = NT // TB
    nc = bass.Bass("TRN2", target_bir_lowering=False)
    xT = nc.dram_tensor("xT", [D, NT], F32, kind="ExternalInput").ap()
    zT = nc.dram_tensor("zT", [D, NT], F32, kind="ExternalInput").ap()
    wo = nc.dram_tensor("wo", [D, D], F32, kind="ExternalInput").ap()
    w1 = nc.dram_tensor("w1", [D, FF], F32, kind="ExternalInput").ap()
    w3 = nc.dram_tensor("w3", [D, FF], F32, kind="ExternalInput").ap()
    w2 = nc.dram_tensor("w2", [FF, D], F32, kind="ExternalInput").ap()
    vecs = nc.dram_tensor("vecs", [128, 7 * ND], F32, kind="ExternalInput").ap()
    oT = nc.dram_tensor("oT", [D, NT], F32, kind="ExternalOutput").ap()
    xTv = xT.rearrange("(t p) n -> p t n", p=128)
    zTv = zT.rearrange("(t p) n -> p t n", p=128)
    oTv = oT.rearrange("(t p) n -> p t n", p=128)
    wov = wo.rearrange("(t p) f -> p t f", p=128)
    w1v = w1.rearrange("(t p) f -> p t f", p=128)
    w3v = w3.rearrange("(t p) f -> p t f", p=128)
    w2v = w2.rearrange("(t p) f -> p t f", p=128)
    P = Prog(nc)
    import contextlib
    with contextlib.ExitStack() as st:
        sb = lambda name, shape: st.enter_context(nc.sbuf_tensor(name, shape, F32))
        ps = lambda name: st.enter_context(nc.psum_tensor(name, [128, 512], F32))
        vt = sb("vt", [128, 7 * ND])
        va = sb("va", [128, 4 * ND])
        ones = sb("ones", [128, 128])
        epsb = sb("epsb", [128, 1])
        xs = [sb("xs%d" % i, [128, ND, TB]) for i in range(2)]
        zs = [sb("zs%d" % i, [128, ND, TB]) for i in range(2)]
        gs = sb("gs", [128, NF, TB])
        wa = [sb("wa%d" % i, [128, ND, 128]) for i in range(2)]
        wb = [sb("wb%d" % i, [128, ND, 128]) for i in range(2)]
        wc = [sb("wc%d" % i, [128, NF, 128]) for i in range(2)]
        sq = [sb("sq%d" % i, [128, TB]) for i in range(2)]
        rstd = sb("rstd", [128, TB])
        tmp = [sb("tmp%d" % i, [128, TB]) for i in range(2)]
        pacc = [ps("pacc%d" % i) for i in range(2)]
        pa = [ps("pa%d" % i) for i in range(2)]
        pb = [ps("pb%d" % i) for i in range(2)]
        pss = ps("pss")

        P.dma(vt[:], vecs[:, :], w=["vt"])
        P.add("pool", lambda e: e.memset(ones[:], 1.0), w=["ones"])
        P.add("pool", lambda e: e.memset(epsb[:], RMS_EPS), w=["epsb"])
        V = lambda i, t: vt[:, i * ND + t:i * ND + t + 1]
        VA = lambda i, t: va[:, i * ND + t:i * ND + t + 1]
        P.add("dve", lambda e: e.tensor_tensor(out=va[:, 0:ND], in0=vt[:, 0:ND], in1=vt[:, ND:2 * ND], op=ALU.mult),
              r=["vt"], w=["va0"])
        P.add("dve", lambda e: e.tensor_scalar(out=va[:, ND:2 * ND], in0=vt[:, 3 * ND:4 * ND], scalar1=1.0,
                                               scalar2=None, op0=ALU.add),
              r=["vt"], w=["va1"])
        P.add("dve", lambda e: e.tensor_tensor(out=va[:, ND:2 * ND], in0=va[:, ND:2 * ND], in1=vt[:, 2 * ND:3 * ND],
                                               op=ALU.mult), r=["vt", "va1"], w=["va1"])
        P.add("dve", lambda e: e.tensor_scalar(out=va[:, 2 * ND:3 * ND], in0=vt[:, 6 * ND:7 * ND],
                                               scalar1=1.0, scalar2=None, op0=ALU.mult),
              r=["vt"], w=["va2"])
        VK = ["vt", "va0", "va1", "va2"]

        wcount = [0, 0, 0]

        def rmsnorm_mod(src, srck, dst, dstk, acol, bcol):
            for t in range(ND):
                s = sq[t % 2]
                P.add("act", lambda e, s=s, t=t: e.activation(out=s[:], in_=src[:, t, :], func=AF.Square),
                      r=[srck], w=[("sq", t % 2)])
                P.mm(pss[:, 0:TB], ones[:], s[:], start=(t == 0), stop=(t == ND - 1),
                     r=[("sq", t % 2), "ones"], w=["pss"])
            P.add("act", lambda e: e.activation(out=rstd[:], in_=pss[:, 0:TB], func=AF.Sqrt, scale=1.0 / D,
                                                bias=epsb[:, 0:1]),
                  r=["pss", "epsb"], w=["rstd"])
            P.add("dve", lambda e: e.reciprocal(out=rstd[:], in_=rstd[:]), r=["rstd"], w=["rstd"])
            for t in range(ND):
                tt = tmp[t % 2]
                P.add("dve", lambda e, tt=tt, t=t: e.tensor_tensor(out=tt[:], in0=src[:, t, :], in1=rstd[:],
                                                                   op=ALU.mult),
                      r=[srck, "rstd"], w=[("tmp", t % 2)])
                if bcol is None:
                    P.add("act", lambda e, tt=tt, t=t: e.activation(out=dst[:, t, :], in_=tt[:], func=AF.Copy,
                                                                    scale=VA(acol, t)),
                          r=[("tmp", t % 2)] + VK, w=[dstk])
                else:
                    P.add("act", lambda e, tt=tt, t=t: e.activation(out=dst[:, t, :], in_=tt[:], func=AF.Identity,
                                                                    scale=VA(acol, t), bias=V(bcol, t)),
                          r=[("tmp", t % 2)] + VK, w=[dstk])

        for b in range(NB):
            x = xs[b % 2]
            z = zs[b % 2]
            xk, zk = ("x", b % 2), ("z", b % 2)
            tok = slice(b * TB, (b + 1) * TB)
            P.dma(z[:], zTv[:, :, tok], w=[zk])
            P.dma(x[:], xTv[:, :, tok], w=[xk])
            for fo in range(ND):
                i = wcount[0] % 2
                wcount[0] += 1
                P.dma(wa[i][:], wov[:, :, fo * 128:(fo + 1) * 128], w=[("wa", i)])
                acc = pacc[fo % 2]
                for fi in range(ND):
                    P.mm(acc[:, 0:TB], wa[i][:, fi, :], z[:, fi, :], start=(fi == 0), stop=(fi == ND - 1),
                         r=[("wa", i), zk], w=[("pacc", fo % 2)])
                P.add("dve", lambda e, acc=acc, fo=fo, x=x: e.scalar_tensor_tensor(
                    out=x[:, fo, :], in0=acc[:, 0:TB], scalar=V(1, fo), in1=x[:, fo, :], op0=ALU.mult, op1=ALU.add),
                    r=[("pacc", fo % 2), xk] + VK, w=[xk])
                P.add("pool", lambda e, fo=fo, x=x: e.tensor_scalar(
                    out=x[:, fo, :], in0=x[:, fo, :], scalar1=VA(0, fo), scalar2=None, op0=ALU.add),
                    r=[xk] + VK, w=[xk])
            rmsnorm_mod(x, xk, z, zk, 1, 4)
            for j in range(NF):
                i = wcount[1] % 2
                wcount[1] += 1
                P.dma(wa[i][:], w1v[:, :, j * 128:(j + 1) * 128], w=[("wa", i)])
                P.dma(wb[i][:], w3v[:, :, j * 128:(j + 1) * 128], w=[("wb", i)])
                A = pa[j % 2]
                B = pb[j % 2]
                for fi in range(ND):
                    P.mm(A[:, 0:TB], wa[i][:, fi, :], z[:, fi, :], start=(fi == 0), stop=(fi == ND - 1),
                         r=[("wa", i), zk], w=[("pa", j % 2)])
                for fi in range(ND):
                    P.mm(B[:, 0:TB], wb[i][:, fi, :], z[:, fi, :], start=(fi == 0), stop=(fi == ND - 1),
                         r=[("wb", i), zk], w=[("pb", j % 2)])
                tt = tmp[j % 2]
                P.add("act", lambda e, A=A, tt=tt: e.activation(out=tt[:], in_=A[:, 0:TB], func=AF.Silu),
                      r=[("pa", j % 2)], w=[("tmp", j % 2)])
                P.add("dve", lambda e, B=B, tt=tt, j=j: e.tensor_tensor(out=gs[:, j, :], in0=tt[:], in1=B[:, 0:TB],
                                                                        op=ALU.mult),
                      r=[("pb", j % 2), ("tmp", j % 2)], w=["gs"])
            for fo in range(ND):
                i = wcount[2] % 2
                wcount[2] += 1
                P.dma(wc[i][:], w2v[:, :, fo * 128:(fo + 1) * 128], w=[("wc", i)])
                acc = pacc[fo % 2]
                for j in range(NF):
                    P.mm(acc[:, 0:TB], wc[i][:, j, :], gs[:, j, :], start=(j == 0), stop=(j == NF - 1),
                         r=[("wc", i), "gs"], w=[("pacc", fo % 2)])
                P.add("dve", lambda e, acc=acc, fo=fo, x=x: e.scalar_tensor_tensor(
                    out=x[:, fo, :], in0=acc[:, 0:TB], scalar=V(5, fo), in1=x[:, fo, :], op0=ALU.mult, op1=ALU.add),
                    r=[("pacc", fo % 2), xk] + VK, w=[xk])
            if final:
                rmsnorm_mod(x, xk, z, zk, 2, None)
                P.dma(oTv[:, :, tok], z[:], r=[zk], q=OUTQ)
            else:
                P.dma(oTv[:, :, tok], x[:], r=[xk], q=OUTQ)
        P.emit()
    return nc


def build_k0():
    NCOL = 6 * D // NCORES
    nc = bass.Bass("TRN2", target_bir_lowering=False)
    cv = nc.dram_tensor("cv", [128, ND, 2], F32, kind="ExternalInput").ap()
    aw = nc.dram_tensor("aw", [2, D, NCOL], F32, kind="ExternalInput").ap()
    ab = nc.dram_tensor("ab", [2, NCOL], F32, kind="ExternalInput").ap()
    mods = nc.dram_tensor("mods", [4, NCOL], F32, kind="ExternalOutput").ap()
    P = Prog(nc)
    import contextlib
    with contextlib.ExitStack() as st:
        sb = lambda name, shape: st.enter_context(nc.sbuf_tensor(name, shape, F32))
        s = sb("s", [128, ND, 2])
        wb = [sb("wb%d" % i, [128, ND, 512]) for i in range(2)]
        bs = [sb("bs%d" % i, [2, NCOL]) for i in range(2)]
        res = [sb("res%d" % i, [2, NCOL]) for i in range(2)]
        pp = [st.enter_context(nc.psum_tensor("pp%d" % i, [128, 512], F32)) for i in range(2)]
        P.dma(s[:], cv[:, :, :], w=["s"])
        P.add("act", lambda e: e.activation(out=s[:], in_=s[:], func=AF.Silu), r=["s"], w=["s"])
        n = 0
        for l in range(2):
            P.dma(bs[l][0:1, :], ab[l:l + 1, :], w=[("bs", l, 0)])
            P.dma(bs[l][1:2, :], ab[l:l + 1, :], w=[("bs", l, 1)])
            awv = aw[l].rearrange("(t p) f -> p t f", p=128)
            for cb in range(NCOL // 512):
                i = n % 2
                n += 1
                cs = slice(cb * 512, (cb + 1) * 512)
                P.dma(wb[i][:], awv[:, :, cs], w=[("wb", i)])
                for t in range(ND):
                    P.mm(pp[i][0:2, :], s[:, t, :], wb[i][:, t, :], start=(t == 0), stop=(t == ND - 1),
                         r=["s", ("wb", i)], w=[("pp", i)])
                P.add("dve", lambda e, i=i, l=l, cs=cs: e.tensor_tensor(out=res[l][:, cs], in0=pp[i][0:2, :],
                                                                        in1=bs[l][:, cs], op=ALU.add),
                      r=[("pp", i), ("bs", l, 0), ("bs", l, 1)], w=[("res", l)])
            P.dma(mods[2 * l:2 * l + 2, :], res[l][:], r=[("res", l)], q="act")
        P.emit()
    return nc


def run_k0(inp):
    cv = np.stack([vec_layout(inp["c"].reshape(1, D)), vec_layout(inp["c_ctx"].reshape(1, D))], axis=-1)
    cv = np.ascontiguousarray(cv.astype(np.float32))
    NCOL = 6 * D // NCORES
    nc = build_k0()
    maps = []
    for c in range(NCORES):
        cs = slice(c * NCOL, (c + 1) * NCOL)
        maps.append({"cv": cv, "aw": np.ascontiguousarray(inp["ada_w"][:, :, cs]),
                     "ab": np.ascontiguousarray(inp["ada_b"][:, cs])})
    r = _run(nc, maps)
    mods = np.concatenate([r[c]["mods"] for c in range(NCORES)], axis=1)
    return mods


TT = CTX + SEQ
HL = 96
NTAB = 6


def build_k1(nblocks=None):
    TBK = 256
    NBUF = TBK + 128
    NBLK = 1 + SEQ // TBK
    if nblocks is None:
        nblocks = NBLK
    nc = bass.Bass("TRN2", target_bir_lowering=False)
    din = lambda n, s: nc.dram_tensor(n, s, F32, kind="ExternalInput").ap()
    dout = lambda n, s: nc.dram_tensor(n, s, F32, kind="ExternalOutput").ap()
    xcT = din("xcT", [D, CTX]).rearrange("(t p) n -> p t n", p=128)
    xlT = din("xlT", [D, SEQ]).rearrange("(t p) n -> p t n", p=128)
    vecs = din("vecs", [128, 11 * ND])
    wr = din("wr", [D, 256]).rearrange("(t p) f -> p t f", p=128)
    wk = din("wk", [D, 256]).rearrange("(t p) f -> p t f", p=128)
    wv = din("wv", [D, 256]).rearrange("(t p) f -> p t f", p=128)
    g1 = din("g1", [D, 256]).rearrange("(t p) f -> p t f", p=128)
    w1 = din("w1", [2, D, HL])
    a1 = din("a1", [2, D, HL])
    w2 = din("w2", [2, HL, 256])
    a2 = din("a2", [2, HL, 256])
    g2 = din("g2", [256, 256]).rearrange("(t p) f -> p t f", p=128)
    tabs = din("tabs", [128, 7 * 256])
    ident = din("ident", [128, 128])
    tab = dout("tab", [TT, 4 * 2 * NTAB * 64])
    vT = dout("vT", [256, TT]).rearrange("(h p) t -> p h t", p=128)
    gT = dout("gT", [256, SEQ]).rearrange("(h p) t -> p h t", p=128)
    rkT = dout("rkT", [4, TT])
    P = Prog(nc)
    import contextlib
    with contextlib.ExitStack() as st:
        sb = lambda name, shape: st.enter_context(nc.sbuf_tensor(name, shape, F32))
        vt = sb("vt", [128, 11 * ND])
        va = sb("va", [128, 2 * ND])
        ones = sb("ones", [128, 128])
        epsb = sb("epsb", [128, 1])
        idt = sb("idt", [128, 128])
        tb = sb("tb", [128, 8, 256])
        wr_s = sb("wr_s", [128, ND, 256])
        wk_s = sb("wk_s", [128, ND, 256])
        wv_s = sb("wv_s", [128, ND, 256])
        g1_s = sb("g1_s", [128, ND, 256])
        w1_s = sb("w1_s", [128, ND, 2, HL])
        a1_s = sb("a1_s", [128, ND, 2, HL])
        w2_s = sb("w2_s", [HL, 2, 256])
        a2_s = sb("a2_s", [HL, 2, 256])
        g2_s = sb("g2_s", [128, 2, 256])
        xb = sb("xb", [128, ND, NBUF])
        xx = sb("xx", [128, ND, TBK])
        xm = sb("xm", [128, ND, TBK])
        rstd = sb("rstd", [128, NBUF])
        tmp = [sb("tmp%d" % i, [128, NBUF]) for i in range(2)]
        sq = tmp
        hw_s = sb("hw_s", [HL, 2, TBK])
        ha_s = sb("ha_s", [HL, 2, TBK])
        hg_s = sb("hg_s", [128, 2, TBK])
        vo_s = sb("vo_s", [128, 2, TBK])
        go_s = sb("go_s", [128, 2, TBK])
        rkst = sb("rkst", [4, TBK])
        stg = [sb("stg%d" % i, [128, 4, 2, NTAB, 64]) for i in range(2)]
        k_s = sb("k_s", [128, 256])
        kk_s = sb("kk_s", [128, 256])
        t1_s = sb("t1_s", [128, 256])
        t2_s = sb("t2_s", [128, 256])
        zw_s = sb("zw_s", [128, 256])
        za_s = sb("za_s", [128, 256])
        a_s = sb("a_s", [128, 256])
        ss_s = sb("ss_s", [128, 4])
        rk_s = sb("rk_s", [128, 4])
        pb = [st.enter_context(nc.psum_tensor("pb%d" % i, [128, 512], F32)) for i in range(8)]
        PSS, PRK, PV, PHW, PHA, PHG, PG, PT = range(8)
        pk = lambda i: ("pb", i)

        P.dma(vt[:], vecs[:, :], w=["vt"])
        P.dma(tb[:, 0:7, :], tabs.rearrange("p (a f) -> p a f", f=256), w=["tb"])
        P.dma(idt[:], ident[:, :], w=["idt"])
        P.dma(wr_s[:], wr, w=["wr"])
        P.dma(wk_s[:], wk, w=["wk"])
        P.dma(wv_s[:], wv, w=["wv"])
        P.dma(g1_s[:], g1, w=["g1"])
        for d in range(2):
            P.dma(w1_s[:, :, d, :], w1[d].rearrange("(t p) f -> p t f", p=128), w=[("w1", d)])
            P.dma(a1_s[:, :, d, :], a1[d].rearrange("(t p) f -> p t f", p=128), w=[("a1", d)])
            P.dma(w2_s[:, d, :], w2[d], w=[("w2", d)])
            P.dma(a2_s[:, d, :], a2[d], w=[("a2", d)])
        P.dma(g2_s[:], g2, w=["g2"])
        WK = ["wr", "wk", "wv", "g1", ("w1", 0), ("w1", 1), ("a1", 0), ("a1", 1), ("w2", 0), ("w2", 1),
              ("a2", 0), ("a2", 1), "g2", "idt", "tb", "tb7"]
        P.add("pool", lambda e: e.memset(ones[:], 1.0), w=["ones"])
        P.add("pool", lambda e: e.memset(epsb[:], RMS_EPS), w=["epsb"])
        P.add("dve", lambda e: e.tensor_scalar(out=tb[:, 7, :], in0=tb[:, 5, :], scalar1=-1.0, scalar2=1.0,
                                               op0=ALU.mult, op1=ALU.add), r=["tb"], w=["tb7"])
        for j, c in enumerate((1, 3)):
            P.add("dve", lambda e, j=j, c=c: e.tensor_scalar(out=va[:, j * ND:(j + 1) * ND],
                                                             in0=vt[:, c * ND:(c + 1) * ND], scalar1=1.0,
                                                             scalar2=None, op0=ALU.add), r=["vt"], w=[("va", j)])
            P.add("dve", lambda e, j=j: e.tensor_tensor(out=va[:, j * ND:(j + 1) * ND], in0=va[:, j * ND:(j + 1) * ND],
                                                        in1=vt[:, 0:ND], op=ALU.mult), r=["vt", ("va", j)],
                  w=[("va", j)])
        VK = ["vt", ("va", 0), ("va", 1)]
        V = lambda i, t: vt[:, i * ND + t:i * ND + t + 1]
        VA = lambda i, t: va[:, i * ND + t:i * ND + t + 1]
        rows = lambda ap: ap.rearrange("p (r c) -> p r c", c=64)
        hv = lambda ap: ap.rearrange("p (h k) -> p h k", k=64)

        for bi in range(nblocks):
            is_ctx = bi == 0
            if is_ctx:
                t0 = 0
                P.add("pool", lambda e: e.memset(xb[:, :, 0:64], 0.0), w=["xb"])
                P.add("pool", lambda e: e.memset(xb[:, :, 64 + TBK:NBUF], 0.0), w=["xb"])
                P.dma(xb[:, :, 64:64 + TBK], xcT[:, :, :], w=["xb"])
                zero_top = zero_bot = True
            else:
                lb = bi - 1
                lt0 = lb * TBK
                t0 = CTX + lt0
                a = max(lt0 - 64, 0)
                b = min(lt0 + TBK + 64, SEQ)
                zero_top = lt0 - 64 < 0
                zero_bot = lt0 + TBK + 64 > SEQ
                if zero_top:
                    P.add("pool", lambda e: e.memset(xb[:, :, 0:64], 0.0), w=["xb"])
                if zero_bot:
                    P.add("pool", lambda e: e.memset(xb[:, :, 64 + TBK:NBUF], 0.0), w=["xb"])
                P.dma(xb[:, :, 64 + (a - lt0):64 + (b - lt0)], xlT[:, :, a:b], w=["xb"])
            for t in range(ND):
                s = sq[t % 2]
                P.add("act", lambda e, s=s, t=t: e.activation(out=s[:], in_=xb[:, t, :], func=AF.Square),
                      r=["xb"], w=[("tmp", t % 2)])
                P.mm(pb[PSS][:, 0:NBUF], ones[:], s[:], start=(t == 0), stop=(t == ND - 1),
                     r=[("tmp", t % 2), "ones"], w=[pk(PSS)])
            P.add("act", lambda e: e.activation(out=rstd[:], in_=pb[PSS][:, 0:NBUF], func=AF.Sqrt, scale=1.0 / D,
                                                bias=epsb[:, 0:1]), r=[pk(PSS), "epsb"], w=["rstd"])
            P.add("dve", lambda e: e.reciprocal(out=rstd[:], in_=rstd[:]), r=["rstd"], w=["rstd"])
            ac, bc = (1, 4) if is_ctx else (0, 2)
            for t in range(ND):
                tt = tmp[t % 2]
                P.add("dve", lambda e, tt=tt, t=t: e.tensor_tensor(out=tt[:], in0=xb[:, t, :], in1=rstd[:],
                                                                   op=ALU.mult), r=["xb", "rstd"], w=[("tmp", t % 2)])
                P.add("act", lambda e, tt=tt, t=t, ac=ac, bc=bc: e.activation(
                    out=xb[:, t, :], in_=tt[:], func=AF.Identity, scale=VA(ac, t), bias=V(bc, t)),
                    r=[("tmp", t % 2)] + VK, w=["xb"])
            if zero_top:
                P.add("pool", lambda e: e.memset(xb[:, :, 0:64], 0.0), w=["xb"])
            if zero_bot:
                P.add("pool", lambda e: e.memset(xb[:, :, 64 + TBK:NBUF], 0.0), w=["xb"])
            H0 = 64
            for t in range(ND):
                eng = "dve" if t % 2 == 0 else "pool"
                cur = xb[:, t, H0:H0 + TBK]
                if is_ctx:
                    off = -1 if t < ND // 2 else 1
                    P.add(eng, lambda e, t=t, off=off, cur=cur: e.tensor_tensor(
                        out=xx[:, t, :], in0=xb[:, t, H0 + off:H0 + off + TBK], in1=cur, op=ALU.subtract),
                        r=["xb"], w=["xx"])
                elif t >= 8:
                    off = -64 if t < 12 else 64
                    P.add(eng, lambda e, t=t, off=off, cur=cur: e.tensor_tensor(
                        out=xx[:, t, :], in0=xb[:, t, H0 + off:H0 + off + TBK], in1=cur, op=ALU.subtract),
                        r=["xb"], w=["xx"])
                else:
                    cr = rows(cur)
                    xr_ = rows(xx[:, t, :])
                    if t < 4:
                        P.add(eng, lambda e, cr=cr, xr_=xr_: e.tensor_tensor(
                            out=xr_[:, :, 1:64], in0=cr[:, :, 0:63], in1=cr[:, :, 1:64], op=ALU.subtract),
                            r=["xb"], w=["xx"])
                        P.add(eng, lambda e, cr=cr, xr_=xr_: e.tensor_scalar(
                            out=xr_[:, :, 0:1], in0=cr[:, :, 0:1], scalar1=-1.0, scalar2=None, op0=ALU.mult),
                            r=["xb"], w=["xx"])
                    else:
                        P.add(eng, lambda e, cr=cr, xr_=xr_: e.tensor_tensor(
                            out=xr_[:, :, 0:63], in0=cr[:, :, 1:64], in1=cr[:, :, 0:63], op=ALU.subtract),
                            r=["xb"], w=["xx"])
                        P.add(eng, lambda e, cr=cr, xr_=xr_: e.tensor_scalar(
                            out=xr_[:, :, 63:64], in0=cr[:, :, 63:64], scalar1=-1.0, scalar2=None, op0=ALU.mult),
                            r=["xb"], w=["xx"])

            def mix(m):
                for t in range(ND):
                    eng = "dve"
                    P.add(eng, lambda e, t=t, m=m: e.scalar_tensor_tensor(
                        out=xm[:, t, :], in0=xx[:, t, :], scalar=V(5 + m, t), in1=xb[:, t, H0:H0 + TBK],
                        op0=ALU.mult, op1=ALU.add), r=["xx", "xb"] + VK, w=["xm"])

            mix(0)
            for tt in range(2):
                for t in range(ND):
                    P.mm(pb[PRK][:, tt * 256:(tt + 1) * 256], xm[:, t, tt * 128:(tt + 1) * 128], wr_s[:, t, :],
                         start=(t == 0), stop=(t == ND - 1), r=["xm", "wr"], w=[pk(PRK)])
            sg = stg[bi % 2]
            sgk = [("stg", bi % 2, 0), ("stg", bi % 2, 1)]
            stgs = [stg[(2 * bi + tt) % 2] for tt in range(2)]
            for tt in range(2):
                for d in range(2):
                    P.add("act", lambda e, tt=tt, d=d: e.activation(
                        out=stgs[tt][:, :, d, 4, :], in_=hv(pb[PRK][:, tt * 256:(tt + 1) * 256]), func=AF.Copy),
                        r=[pk(PRK)], w=[("stg", tt)])
            mix(1)
            for d in range(2):
                for t in range(ND):
                    P.mm(pb[PHW][0:HL, d * TBK:(d + 1) * TBK], w1_s[:, t, d, :], xm[:, t, :],
                         start=(t == 0), stop=(t == ND - 1), r=["xm", ("w1", d)], w=[pk(PHW)])
            P.add("act", lambda e: e.activation(out=hw_s[:].rearrange("p d n -> p (d n)"), in_=pb[PHW][0:HL, :],
                                                func=AF.Tanh), r=[pk(PHW)], w=["hw"])
            mix(2)
            for tt in range(2):
                for t in range(ND):
                    P.mm(pb[PT][:, tt * 256:(tt + 1) * 256], xm[:, t, tt * 128:(tt + 1) * 128], wk_s[:, t, :],
                         start=(t == 0), stop=(t == ND - 1), r=["xm", "wk"], w=[pk(PT)])
            mix(3)
            for tt in range(2):
                for t in range(ND):
                    P.mm(pb[PRK][:, tt * 256:(tt + 1) * 256], xm[:, t, tt * 128:(tt + 1) * 128], wv_s[:, t, :],
                         start=(t == 0), stop=(t == ND - 1), r=["xm", "wv"], w=[pk(PRK)])
            for tt in range(2):
                for d in range(2):
                    P.add("act", lambda e, tt=tt, d=d: e.activation(
                        out=stgs[tt][:, :, d, 5, :], in_=hv(pb[PRK][:, tt * 256:(tt + 1) * 256]), func=AF.Copy),
                        r=[pk(PRK)], w=[("stg", tt)])
            for h in range(2):
                for t in range(ND):
                    P.mm(pb[PV][:, h * TBK:(h + 1) * TBK], wv_s[:, t, h * 128:(h + 1) * 128], xm[:, t, :],
                         start=(t == 0), stop=(t == ND - 1), r=["xm", "wv"], w=[pk(PV)])
            P.add("act", lambda e: e.activation(out=vo_s[:].rearrange("p h n -> p (h n)"), in_=pb[PV][:, :],
                                                func=AF.Copy), r=[pk(PV)], w=["vo"])
            P.dma(vT[:, :, t0:t0 + TBK], vo_s[:], r=["vo"], q="act")
            mix(4)
            for d in range(2):
                for t in range(ND):
                    P.mm(pb[PHA][0:HL, d * TBK:(d + 1) * TBK], a1_s[:, t, d, :], xm[:, t, :],
                         start=(t == 0), stop=(t == ND - 1), r=["xm", ("a1", d)], w=[pk(PHA)])
            P.add("dve", lambda e: e.tensor_copy(out=ha_s[:].rearrange("p d n -> p (d n)"), in_=pb[PHA][0:HL, :]),
                  r=[pk(PHA)], w=["ha"])
            if not is_ctx:
                mix(5)
                for h in range(2):
                    for t in range(ND):
                        P.mm(pb[PHG][:, h * TBK:(h + 1) * TBK], g1_s[:, t, h * 128:(h + 1) * 128], xm[:, t, :],
                             start=(t == 0), stop=(t == ND - 1), r=["xm", "g1"], w=[pk(PHG)])
                P.add("act", lambda e: e.activation(out=hg_s[:].rearrange("p h n -> p (h n)"), in_=pb[PHG][:, :],
                                                    func=AF.Sigmoid), r=[pk(PHG)], w=["hg"])
                for h in range(2):
                    for hh in range(2):
                        P.mm(pb[PG][:, h * TBK:(h + 1) * TBK], g2_s[:, hh, h * 128:(h + 1) * 128], hg_s[:, hh, :],
                             start=(hh == 0), stop=(hh == 1), r=["hg", "g2"], w=[pk(PG)])
                P.add("act", lambda e: e.activation(out=go_s[:].rearrange("p h n -> p (h n)"), in_=pb[PG][:, :],
                                                    func=AF.Copy), r=[pk(PG)], w=["go"])
                P.dma(gT[:, :, t0 - CTX:t0 - CTX + TBK], go_s[:], r=["go"], q="act")
            for tt in range(2):
                S = stgs[tt]
                sk = ("stg", tt)
                tsl = slice(tt * 128, (tt + 1) * 128)
                for d in range(2):
                    P.mm(pb[PSS][:, d * 256:(d + 1) * 256], hw_s[:, d, tsl], w2_s[:, d, :], start=True, stop=True,
                         r=["hw", ("w2", d)], w=[pk(PSS)])
                for d in range(2):
                    P.mm(pb[PG][:, d * 256:(d + 1) * 256], ha_s[:, d, tsl], a2_s[:, d, :], start=True, stop=True,
                         r=["ha", ("a2", d)], w=[pk(PG)])
                P.add("act", lambda e, tt=tt: e.activation(out=k_s[:], in_=pb[PT][:, tt * 256:(tt + 1) * 256],
                                                           func=AF.Copy), r=[pk(PT)], w=["k_s"])
                P.add("dve", lambda e: e.tensor_tensor(out=kk_s[:], in0=k_s[:], in1=tb[:, 4, :], op=ALU.mult),
                      r=["k_s", "tb"], w=["kk_s"])
                P.add("pool", lambda e: e.tensor_tensor(out=t1_s[:], in0=kk_s[:], in1=kk_s[:], op=ALU.mult),
                      r=["kk_s"], w=["t1_s"])
                P.add("dve", lambda e: e.tensor_reduce(out=ss_s[:], in_=hv(t1_s[:]), axis=AX.X, op=ALU.add),
                      r=["t1_s"], w=["ss_s"])
                P.add("act", lambda e: e.activation(out=ss_s[:], in_=ss_s[:], func=AF.Sqrt), r=["ss_s"], w=["ss_s"])
                P.add("dve", lambda e: e.tensor_scalar(out=ss_s[:], in0=ss_s[:], scalar1=1e-12, scalar2=-1.0,
                                                       op0=ALU.max, op1=ALU.mult), r=["ss_s"], w=["ss_s"])
                P.add("dve", lambda e: e.reciprocal(out=ss_s[:], in_=ss_s[:]), r=["ss_s"], w=["ss_s"])
                for h in range(4):
                    P.add("dve", lambda e, h=h, S=S: e.tensor_scalar(
                        out=S[:, h, 0, 0, :], in0=kk_s[:, h * 64:(h + 1) * 64], scalar1=ss_s[:, h:h + 1], scalar2=None,
                        op0=ALU.mult), r=["kk_s", "ss_s"], w=[sk])
                P.add("pool", lambda e, S=S: e.tensor_copy(out=S[:, :, 1, 0, :], in_=S[:, :, 0, 0, :]), r=[sk], w=[sk])
                for d in range(2):
                    P.add("dve", lambda e, d=d: e.tensor_tensor(out=zw_s[:], in0=pb[PSS][:, d * 256:(d + 1) * 256],
                                                                in1=tb[:, d, :], op=ALU.add),
                          r=[pk(PSS), "tb"], w=["zw_s"])
                    P.add("act", lambda e: e.activation(out=zw_s[:], in_=zw_s[:], func=AF.Sigmoid),
                          r=["zw_s"], w=["zw_s"])
                    P.add("act", lambda e, d=d, S=S: e.activation(out=S[:, :, d, 1, :], in_=hv(zw_s[:]), func=AF.Exp,
                                                                  scale=-float(np.exp(-0.5))), r=["zw_s"], w=[sk])
                    P.add("dve", lambda e, d=d: e.tensor_tensor(out=za_s[:], in0=pb[PG][:, d * 256:(d + 1) * 256],
                                                                in1=tb[:, 2 + d, :], op=ALU.add),
                          r=[pk(PG), "tb"], w=["za_s"])
                    P.add("act", lambda e: e.activation(out=a_s[:], in_=za_s[:], func=AF.Sigmoid),
                          r=["za_s"], w=["a_s"])
                    P.add("pool", lambda e: e.tensor_tensor(out=t1_s[:], in0=a_s[:], in1=tb[:, 5, :], op=ALU.mult),
                          r=["a_s", "tb"], w=["t1_s"])
                    P.add("pool", lambda e: e.tensor_tensor(out=t1_s[:], in0=t1_s[:], in1=tb[:, 7, :], op=ALU.add),
                          r=["t1_s", "tb7"], w=["t1_s"])
                    P.add("dve", lambda e, d=d, S=S: e.tensor_tensor(out=S[:, :, d, 3, :], in0=hv(k_s[:]),
                                                                     in1=hv(t1_s[:]), op=ALU.mult),
                          r=["k_s", "t1_s"], w=[sk])
                    P.add("dve", lambda e, d=d, S=S: e.scalar_tensor_tensor(
                        out=S[:, :, d, 2, :], in0=S[:, :, 0, 0, :], scalar=-1.0, in1=hv(a_s[:]), op0=ALU.mult,
                        op1=ALU.mult), r=[sk, "a_s"], w=[sk])
                P.add("pool", lambda e, S=S: e.tensor_tensor(out=hv(t2_s[:]), in0=S[:, :, 0, 3, :], in1=S[:, :, 1, 3, :],
                                                             op=ALU.add), r=[sk], w=["t2_s"])
                P.add("pool", lambda e, S=S: e.tensor_tensor(out=hv(t2_s[:]), in0=hv(t2_s[:]), in1=S[:, :, 0, 4, :],
                                                             op=ALU.mult), r=[sk, "t2_s"], w=["t2_s"])
                P.add("pool", lambda e: e.tensor_tensor(out=t2_s[:], in0=t2_s[:], in1=tb[:, 6, :], op=ALU.mult),
                      r=["t2_s", "tb"], w=["t2_s"])
                P.add("dve", lambda e: e.tensor_reduce(out=rk_s[:], in_=hv(t2_s[:]), axis=AX.X, op=ALU.add),
                      r=["t2_s"], w=["rk_s"])
                P.mm(pb[PHG][0:4, 0:128], rk_s[:], idt[:], start=True, stop=True, r=["rk_s", "idt"], w=[pk(PHG)])
                P.add("act", lambda e, tsl=tsl: e.activation(out=rkst[:, tsl], in_=pb[PHG][0:4, 0:128], func=AF.Copy),
                      r=[pk(PHG)], w=["rkst"])
                P.dma(tab[t0 + tt * 128:t0 + (tt + 1) * 128, :], S[:].rearrange("p a b c d -> p (a b c d)"),
                      r=[sk], q="act")
            P.dma(rkT[:, t0:t0 + TBK], rkst[:], r=["rkst"], q="act")
        P.emit()
    return nc


def k1_inputs(inp, mods, core):
    F = slice(core * 256, (core + 1) * 256)
    ml, mc = mods[0], mods[1]
    sp = lambda m, i: m[i * D:(i + 1) * D]
    vl = np.stack([inp["norm1_g"][0], sp(ml, 1), sp(ml, 0), sp(mc, 1), sp(mc, 0)] + [inp["rw_mu"][0, m] for m in range(6)])
    tabs = np.stack([inp["rw_w0"][0, 0, F], inp["rw_w0"][0, 1, F], inp["rw_a0"][0, 0, F], inp["rw_a0"][0, 1, F],
                     inp["rw_k_k"][0, F], inp["rw_k_a"][0, F], inp["rw_r_k"][0].reshape(-1)[F]]).reshape(1, -1)
    c = np.ascontiguousarray
    return {
        "xcT": c(inp["ctx"][0].T), "xlT": c(inp["x"][0].T), "vecs": vec_layout(vl),
        "wr": c(inp["rw_w_r"][0][:, F]), "wk": c(inp["rw_w_k"][0][:, F]), "wv": c(inp["rw_w_v"][0][:, F]),
        "g1": c(inp["rw_g1"][0]), "w1": c(inp["rw_w1"][0]), "a1": c(inp["rw_a1"][0]),
        "w2": c(inp["rw_w2"][0][:, :, F]), "a2": c(inp["rw_a2"][0][:, :, F]), "g2": c(inp["rw_g2"][0][:, F]),
        "tabs": c(np.broadcast_to(tabs, (128, tabs.shape[1]))).astype(np.float32),
        "ident": np.eye(128, dtype=np.float32),
    }


GN_EPS = 64e-5


def build_k2(seq=SEQ):
    T = CTX + seq
    CH = 8
    NSLOT = 2
    YC = 512 if seq >= 512 else seq
    nc = bass.Bass("TRN2", target_bir_lowering=False)
    din = lambda n, s: nc.dram_tensor(n, s, F32, kind="ExternalInput").ap()
    dout = lambda n, s: nc.dram_tensor(n, s, F32, kind="ExternalOutput").ap()
    tab = din("tab", [TT, 4, 2, 5 * 64])
    vT = din("vT", [256, TT])
    gT = din("gT", [256, SEQ])
    rkT = din("rkT", [4, TT])
    lnx = din("lnx", [128, 4])
    blk = din("blk", [128, 128])
    esel = din("esel", [4, 2 * 128])
    yT = dout("yT", [2, 256, seq])
    zT = dout("zT", [256, seq])
    P = Prog(nc)
    import contextlib
    with contextlib.ExitStack() as st:
        sb = lambda name, shape: st.enter_context(nc.sbuf_tensor(name, shape, F32))
        chains = [(p, d) for d in range(2) for p in range(2)]
        tbuf = {(p, d, s): sb("tb_%d_%d_%d" % (p, d, s), [128, CH, 5, 64]) for (p, d) in chains for s in range(NSLOT)}
        vt = [sb("vt%d" % p, [128, T]) for p in range(2)]
        S = {c: sb("S_%d_%d" % c, [128, 64]) for c in chains}
        junk = {c: sb("junk_%d_%d" % c, [128, 64]) for c in chains}
        sa = {c: sb("sa_%d_%d" % c, [128, 1]) for c in chains}
        yb = {(p, d, s): sb("yb_%d_%d_%d" % (p, d, s), [128, YC]) for (p, d) in chains for s in range(2)}
        for p in range(2):
            P.dma(vt[p][:], vT[p * 128:(p + 1) * 128, 0:T] if T == TT else vT[p * 128:(p + 1) * 128, 0:T], w=[("vt", p)])
        for c in chains:
            P.add("dve", lambda e, c=c: e.memset(S[c][:], 0.0), w=[("S", c)])
        order = {0: list(range(T)), 1: list(range(CTX - 1, -1, -1)) + list(range(T - 1, CTX - 1, -1))}
        nsteps = T
        nchunks = nsteps // CH
        ycount = {c: 0 for c in chains}

        def load_chunk(ci):
            s = ci % NSLOT
            for (p, d) in chains:
                toks = order[d][ci * CH:(ci + 1) * CH]
                lo = min(toks)
                for h in range(2):
                    src = tab[lo:lo + CH, 2 * p + h, d, :].partition_broadcast(64)
                    P.dma(tbuf[(p, d, s)][h * 64:(h + 1) * 64, :, :, :].rearrange("p t a k -> p t (a k)"), src,
                          w=[("tb", p, d, s, h)])

        load_chunk(0)
        for ci in range(nchunks):
            if ci + 1 < nchunks:
                load_chunk(ci + 1)
            s = ci % NSLOT
            for j in range(CH):
                info = {}
                for c in chains:
                    p, d = c
                    toks = order[d][ci * CH:(ci + 1) * CH]
                    lo = min(toks)
                    t = toks[j]
                    info[c] = (t, t - lo, tbuf[(p, d, s)], [("tb", p, d, s, 0), ("tb", p, d, s, 1)])
                for c in chains:
                    t, jj, tbf, rk = info[c]
                    P.add("dve", lambda e, c=c, jj=jj, tbf=tbf: e.scalar_tensor_tensor(
                        out=junk[c][:], in0=S[c][:], scalar=1.0, in1=tbf[:, jj, 0, :], op0=ALU.mult, op1=ALU.mult,
                        accum_out=sa[c][:]), r=[("S", c)] + rk, w=[("sa", c), ("junk", c)])
                for c in chains:
                    t, jj, tbf, rk = info[c]
                    P.add("dve", lambda e, c=c, jj=jj, tbf=tbf: e.tensor_tensor(
                        out=S[c][:], in0=S[c][:], in1=tbf[:, jj, 1, :], op=ALU.mult), r=[("S", c)] + rk, w=[("S", c)])
                for c in chains:
                    t, jj, tbf, rk = info[c]
                    P.add("dve", lambda e, c=c, jj=jj, tbf=tbf: e.scalar_tensor_tensor(
                        out=S[c][:], in0=tbf[:, jj, 2, :], scalar=sa[c][:, 0:1], in1=S[c][:], op0=ALU.mult, op1=ALU.add),
                        r=[("S", c), ("sa", c)] + rk, w=[("S", c)])
                for c in chains:
                    t, jj, tbf, rk = info[c]
                    P.add("dve", lambda e, c=c, jj=jj, tbf=tbf, t=t: e.scalar_tensor_tensor(
                        out=S[c][:], in0=tbf[:, jj, 3, :], scalar=vt[c[0]][:, t:t + 1], in1=S[c][:], op0=ALU.mult,
                        op1=ALU.add), r=[("S", c), ("vt", c[0])] + rk, w=[("S", c)])
                for c in chains:
                    t, jj, tbf, rk = info[c]
                    if t < CTX:
                        continue
                    p, d = c
                    lt = t - CTX
                    ys = (lt // YC) % 2
                    P.add("dve", lambda e, c=c, jj=jj, tbf=tbf, lt=lt, ys=ys: e.scalar_tensor_tensor(
                        out=junk[c][:], in0=S[c][:], scalar=1.0, in1=tbf[:, jj, 4, :], op0=ALU.mult, op1=ALU.mult,
                        accum_out=yb[(c[0], c[1], ys)][:, lt % YC:lt % YC + 1]),
                        r=[("S", c)] + rk, w=[("yb", p, d, ys), ("junk", c)])
                    ycount[c] += 1
                    if ycount[c] % YC == 0:
                        b0 = (lt // YC) * YC
                        P.dma(yT[d, p * 128:(p + 1) * 128, b0:b0 + YC], yb[(p, d, ys)][:], r=[("yb", p, d, ys)],
                              w=[("yT", d, p, b0)], q="act")
        RB = YC
        lx = sb("lx", [128, 4])
        bk = sb("bk", [128, 128])
        es = sb("es", [4, 2 * 128])
        geps = sb("geps", [128, 1])
        P.dma(lx[:], lnx[:, :], w=["lx"])
        P.dma(bk[:], blk[:, :], w=["bk"])
        P.dma(es[:], esel[:, :], w=["es"])
        P.add("pool", lambda e: e.memset(geps[:], GN_EPS), w=["geps"])
        y0 = [sb("y0_%d" % i, [128, RB]) for i in range(2)]
        y1 = [sb("y1_%d" % i, [128, RB]) for i in range(2)]
        gg = [sb("gg_%d" % i, [128, RB]) for i in range(2)]
        rks = [sb("rks_%d" % i, [4, RB]) for i in range(2)]
        cen = sb("cen", [128, RB])
        sqb = sb("sqb", [128, RB])
        rsd = sb("rsd", [128, RB])
        bon = sb("bon", [128, RB])
        zz = [sb("zz_%d" % i, [128, RB]) for i in range(2)]
        pm = st.enter_context(nc.psum_tensor("pm", [128, 512], F32))
        pv = st.enter_context(nc.psum_tensor("pv", [128, 512], F32))
        pr = st.enter_context(nc.psum_tensor("pr", [128, 512], F32))
        n = 0
        for p in range(2):
            rowsl = slice(p * 128, (p + 1) * 128)
            for b in range(seq // RB):
                i = n % 2
                n += 1
                cs = slice(b * RB, (b + 1) * RB)
                P.dma(y0[i][:], yT[0, rowsl, cs], r=[("yT", 0, p, b * RB)], w=[("y0", i)])
                P.dma(y1[i][:], yT[1, rowsl, cs], r=[("yT", 1, p, b * RB)], w=[("y1", i)])
                P.dma(gg[i][:], gT[rowsl, cs], w=[("gg", i)])
                P.dma(rks[i][:], rkT[:, CTX + b * RB:CTX + (b + 1) * RB], w=[("rks", i)])
                P.add("pool", lambda e, i=i: e.tensor_tensor(out=y0[i][:], in0=y0[i][:], in1=y1[i][:], op=ALU.add),
                      r=[("y0", i), ("y1", i)], w=[("y0", i)])
                P.mm(pm[:, 0:RB], bk[:], y0[i][:], start=True, stop=True, r=["bk", ("y0", i)], w=["pm"])
                P.add("dve", lambda e, i=i: e.tensor_tensor(out=cen[:], in0=y0[i][:], in1=pm[:, 0:RB], op=ALU.subtract),
                      r=[("y0", i), "pm"], w=["cen"])
                P.add("pool", lambda e: e.tensor_tensor(out=sqb[:], in0=cen[:], in1=cen[:], op=ALU.mult),
                      r=["cen"], w=["sqb"])
                P.mm(pv[:, 0:RB], bk[:], sqb[:], start=True, stop=True, r=["bk", "sqb"], w=["pv"])
                P.add("act", lambda e: e.activation(out=rsd[:], in_=pv[:, 0:RB], func=AF.Sqrt, bias=geps[:, 0:1]),
                      r=["pv", "geps"], w=["rsd"])
                P.add("dve", lambda e: e.reciprocal(out=rsd[:], in_=rsd[:]), r=["rsd"], w=["rsd"])
                P.add("dve", lambda e: e.tensor_tensor(out=cen[:], in0=cen[:], in1=rsd[:], op=ALU.mult),
                      r=["cen", "rsd"], w=["cen"])
                P.add("act", lambda e, p=p: e.activation(out=cen[:], in_=cen[:], func=AF.Identity,
                                                         scale=lx[:, 2 * p:2 * p + 1], bias=lx[:, 2 * p + 1:2 * p + 2]),
                      r=["cen", "lx"], w=["cen"])
                P.mm(pr[:, 0:RB], es[:, p * 128:(p + 1) * 128], rks[i][:], start=True, stop=True,
                     r=["es", ("rks", i)], w=["pr"])
                P.add("dve", lambda e, p=p, b=b: e.tensor_tensor(out=bon[:], in0=vt[p][:, CTX + b * RB:CTX + (b + 1) * RB],
                                                                 in1=pr[:, 0:RB], op=ALU.mult),
                      r=["pr", ("vt", p)], w=["bon"])
                P.add("pool", lambda e: e.tensor_tensor(out=bon[:], in0=bon[:], in1=cen[:], op=ALU.add),
                      r=["bon", "cen"], w=["bon"])
                P.add("pool", lambda e, i=i: e.tensor_tensor(out=zz[i][:], in0=bon[:], in1=gg[i][:], op=ALU.mult),
                      r=["bon", ("gg", i)], w=[("zz", i)])
                P.dma(zT[rowsl, cs], zz[i][:], r=[("zz", i)], q="act")
        P.emit()
    return nc


def k2_consts(inp, core):
    F0 = core * 256
    lnx = np.zeros((128, 4), np.float32)
    for p in range(2):
        lnx[:, 2 * p] = inp["rw_lnx_w"][0, F0 + p * 128:F0 + (p + 1) * 128]
        lnx[:, 2 * p + 1] = inp["rw_lnx_b"][0, F0 + p * 128:F0 + (p + 1) * 128]
    blk = np.zeros((128, 128), np.float32)
    blk[:64, :64] = 1.0 / 64
    blk[64:, 64:] = 1.0 / 64
    esel = np.zeros((4, 2, 128), np.float32)
    for p in range(2):
        for h in range(2):
            esel[2 * p + h, p, h * 64:(h + 1) * 64] = 1.0
    return {"lnx": lnx, "blk": blk, "esel": esel.reshape(4, 256)}


def build_k4a(nblocks=None):
    TBK = 256
    NB = TBK + 2
    NBLK = SEQ // TBK
    if nblocks is None:
        nblocks = NBLK
    nc = bass.Bass("TRN2", target_bir_lowering=False)
    din = lambda n, s: nc.dram_tensor(n, s, F32, kind="ExternalInput").ap()
    dout = lambda n, s: nc.dram_tensor(n, s, F32, kind="ExternalOutput").ap()
    xT = din("xT", [D, SEQ]).rearrange("(t p) n -> p t n", p=128)
    vecs = din("vecs", [128, 3 * ND])
    inw = din("inw", [D, 768]).rearrange("(t p) f -> p t f", p=128)
    cv6 = din("cv6", [128, 5 * 6])
    vvT = dout("vvT", [256, SEQ]).rearrange("(h p) t -> p h t", p=128)
    x0T = dout("x0T", [256, SEQ]).rearrange("(h p) t -> p h t", p=128)
    P = Prog(nc)
    import contextlib
    with contextlib.ExitStack() as st:
        sb = lambda name, shape: st.enter_context(nc.sbuf_tensor(name, shape, F32))
        vt = sb("vt", [128, 3 * ND])
        va = sb("va", [128, ND])
        c6 = sb("c6", [128, 5, 6])
        ones = sb("ones", [128, 128])
        epsb = sb("epsb", [128, 1])
        w_s = sb("w_s", [128, ND, 768])
        xb = [sb("xb%d" % i, [128, ND, NB]) for i in range(2)]
        sq = [sb("sq%d" % i, [128, NB]) for i in range(2)]
        tmp = [sb("tmp%d" % i, [128, NB]) for i in range(2)]
        rstd = sb("rstd", [128, NB])
        ub = sb("ub", [128, 6, NB])
        cb = [sb("cb%d" % i, [128, 6, TBK]) for i in range(2)]
        vv = [sb("vv%d" % i, [128, 2, TBK]) for i in range(2)]
        pss = st.enter_context(nc.psum_tensor("pss", [128, 512], F32))
        pu = [st.enter_context(nc.psum_tensor("pu%d" % i, [128, 512], F32)) for i in range(2)]
        P.dma(vt[:], vecs[:, :], w=["vt"])
        P.dma(c6[:].rearrange("p a b -> p (a b)"), cv6[:, :], w=["c6"])
        P.dma(w_s[:], inw, w=["w"])
        P.add("pool", lambda e: e.memset(ones[:], 1.0), w=["ones"])
        P.add("pool", lambda e: e.memset(epsb[:], RMS_EPS), w=["epsb"])
        P.add("dve", lambda e: e.tensor_scalar(out=va[:], in0=vt[:, ND:2 * ND], scalar1=1.0, scalar2=None, op0=ALU.add),
              r=["vt"], w=["va"])
        P.add("dve", lambda e: e.tensor_tensor(out=va[:], in0=va[:], in1=vt[:, 0:ND], op=ALU.mult), r=["vt", "va"],
              w=["va"])
        for bi in range(nblocks):
            x = xb[bi % 2]
            xk = ("xb", bi % 2)
            t0 = bi * TBK
            a = max(t0 - 1, 0)
            b = min(t0 + TBK + 1, SEQ)
            if t0 == 0:
                P.add("pool", lambda e, x=x: e.memset(x[:, :, 0:1], 0.0), w=[xk])
            if t0 + TBK == SEQ:
                P.add("pool", lambda e, x=x: e.memset(x[:, :, NB - 1:NB], 0.0), w=[xk])
            P.dma(x[:, :, 1 + (a - t0):1 + (b - t0)], xT[:, :, a:b], w=[xk])
            for t in range(ND):
                s = sq[t % 2]
                P.add("act", lambda e, s=s, t=t, x=x: e.activation(out=s[:], in_=x[:, t, :], func=AF.Square),
                      r=[xk], w=[("sq", t % 2)])
                P.mm(pss[:, 0:NB], ones[:], s[:], start=(t == 0), stop=(t == ND - 1), r=[("sq", t % 2), "ones"],
                     w=["pss"])
            P.add("act", lambda e: e.activation(out=rstd[:], in_=pss[:, 0:NB], func=AF.Sqrt, scale=1.0 / D,
                                                bias=epsb[:, 0:1]), r=["pss", "epsb"], w=["rstd"])
            P.add("dve", lambda e: e.reciprocal(out=rstd[:], in_=rstd[:]), r=["rstd"], w=["rstd"])
            for t in range(ND):
                tt = tmp[t % 2]
                P.add("dve", lambda e, tt=tt, t=t, x=x: e.tensor_tensor(out=tt[:], in0=x[:, t, :], in1=rstd[:],
                                                                        op=ALU.mult), r=[xk, "rstd"], w=[("tmp", t % 2)])
                P.add("act", lambda e, tt=tt, t=t, x=x: e.activation(
                    out=x[:, t, :], in_=tt[:], func=AF.Identity, scale=va[:, t:t + 1],
                    bias=vt[:, 2 * ND + t:2 * ND + t + 1]), r=[("tmp", t % 2), "va", "vt"], w=[xk])
            for j in range(6):
                pj = pu[j % 2]
                for t in range(ND):
                    P.mm(pj[:, 0:NB], w_s[:, t, j * 128:(j + 1) * 128], x[:, t, :], start=(t == 0), stop=(t == ND - 1),
                         r=["w", xk], w=[("pu", j % 2)])
                P.add("act", lambda e, j=j, pj=pj: e.activation(out=ub[:, j, :], in_=pj[:, 0:NB], func=AF.Identity,
                                                                bias=c6[:, 0, j:j + 1]), r=[("pu", j % 2), "c6"],
                      w=["ub"])
            if t0 == 0:
                P.add("pool", lambda e: e.memset(ub[:, :, 0:1], 0.0), w=["ub"])
            if t0 + TBK == SEQ:
                P.add("pool", lambda e: e.memset(ub[:, :, NB - 1:NB], 0.0), w=["ub"])
            c = cb[bi % 2]
            ck = ("cb", bi % 2)
            for j in range(6):
                P.add("dve", lambda e, j=j, c=c: e.tensor_scalar(out=c[:, j, :], in0=ub[:, j, 0:TBK],
                                                                 scalar1=c6[:, 1, j:j + 1], scalar2=c6[:, 4, j:j + 1],
                                                                 op0=ALU.mult, op1=ALU.add), r=["ub", "c6"], w=[ck])
                P.add("dve", lambda e, j=j, c=c: e.scalar_tensor_tensor(out=c[:, j, :], in0=ub[:, j, 1:TBK + 1],
                                                                        scalar=c6[:, 2, j:j + 1], in1=c[:, j, :],
                                                                        op0=ALU.mult, op1=ALU.add),
                      r=["ub", "c6", ck], w=[ck])
                P.add("dve", lambda e, j=j, c=c: e.scalar_tensor_tensor(out=c[:, j, :], in0=ub[:, j, 2:TBK + 2],
                                                                        scalar=c6[:, 3, j:j + 1], in1=c[:, j, :],
                                                                        op0=ALU.mult, op1=ALU.add),
                      r=["ub", "c6", ck], w=[ck])
            v_ = vv[bi % 2]
            P.add("pool", lambda e, c=c, v_=v_: e.tensor_tensor(out=v_[:], in0=c[:, 4:6, :], in1=c[:, 2:4, :],
                                                                op=ALU.mult), r=[ck], w=[("vv", bi % 2)])
            P.dma(vvT[:, :, t0:t0 + TBK], v_[:], r=[("vv", bi % 2)], q="act")
            P.dma(x0T[:, :, t0:t0 + TBK], c[:, 0:2, :], r=[ck], q="act")
        P.emit()
    return nc


def k4a_inputs(inp, mods, xT, core):
    C = np.arange(core * 256, (core + 1) * 256)
    cols = np.concatenate([C, D + C, 2 * D + C])
    m1 = mods[2]
    sp = lambda m, i: m[i * D:(i + 1) * D]
    vl = np.stack([inp["norm1_g"][1], sp(m1, 1), sp(m1, 0)])
    c6 = np.stack([inp["hy_in_b"][0][cols], inp["hy_short_w"][0][0][cols], inp["hy_short_w"][0][1][cols],
                   inp["hy_short_w"][0][2][cols], inp["hy_short_b"][0][cols]])
    c6 = np.ascontiguousarray(c6.reshape(5, 6, 128).transpose(2, 0, 1).reshape(128, 30)).astype(np.float32)
    return {"xT": xT, "vecs": vec_layout(vl), "inw": np.ascontiguousarray(inp["hy_in_w"][0][:, cols]), "cv6": c6}


NFFT = 2 * SEQ
MAGIC = 12582912.0


def hy_consts(core):
    import math
    L = SEQ
    a = np.arange(128)
    ang = 2 * np.pi * np.outer(a, a) / 128.0
    C = np.cos(ang)
    S = np.sin(ang)
    dftc = np.stack([-S, C, S, C, -S, C / NFFT, -S / NFFT], 1)
    ang2 = 2 * np.pi * np.outer(a, a) / NFFT
    tw = np.stack([np.cos(ang2), np.cos(ang2), np.sin(ang2), np.sin(ang2)], 1)
    t = np.linspace(0.0, 1.0, L, dtype=np.float32).astype(np.float64)
    w = (2 * math.pi * np.arange(L, dtype=np.float32) / L).astype(np.float64)[:, None]
    f = np.linspace(1e-4, 15, 16, dtype=np.float32).astype(np.float64)[None, :]
    z = np.concatenate([t[:, None], np.cos(f * w), -np.sin(f * w)], axis=-1)
    n = np.arange(NFFT)
    tidx = np.where(n < L, n, NFFT - n)
    tidx = np.minimum(tidx, L - 1)
    n1 = np.arange(128)[None, :]
    n2 = np.arange(128)[:, None]
    nq = (128 * n1 + n2).reshape(-1)
    z2T = np.ascontiguousarray(z[tidx[nq]].T)
    MAXD = math.log(1e-2) / 0.3
    MIND = math.log(1e-2) / 1.5
    deltas = np.abs(np.linspace(MIND, MAXD, D, dtype=np.float32).astype(np.float64))[core * 256:(core + 1) * 256]
    nn = (128 * np.arange(128)[:, None] + np.arange(128)[None, :])
    tt = t[np.minimum(np.where(nn < L, nn, NFFT - nn), L - 1)]
    win = np.exp(-tt[:, None, :] * deltas[None, :, None])
    win[nn[:, None, :].repeat(256, 1) == L] = 0.0
    f32 = lambda x: np.ascontiguousarray(x.astype(np.float32))
    return {"dftc": f32(dftc), "tw": f32(tw), "z2T": f32(z2T), "win": f32(win)}


def build_k4b(ngroups=None):
    G = 32
    NG = 256 // G
    if ngroups is None:
        ngroups = NG
    nc = bass.Bass("TRN2", target_bir_lowering=False)
    din = lambda n, s: nc.dram_tensor(n, s, F32, kind="ExternalInput").ap()
    dout = lambda n, s: nc.dram_tensor(n, s, F32, kind="ExternalOutput").ap()
    vvT = din("vvT", [256, SEQ])
    x0T = din("x0T", [256, SEQ])
    hb = din("hb", [128, 2])
    fw0 = din("fw0", [33, 64])
    fw12 = din("fw12", [128, 2, 64])
    fvec = din("fvec", [128, 4])
    fwout = din("fwout", [128, 256 // 32, 2, 32])
    dftc_d = din("dftc", [128, 7, 128])
    tw_d = din("tw", [128, 4, 128])
    z2T = din("z2T", [33, NFFT])
    win = din("win", [128, 256, 128])
    cvT = dout("cvT", [256, SEQ])
    zT = dout("zT", [256, SEQ])
    P = Prog(nc)
    import contextlib, math
    with contextlib.ExitStack() as st:
        sb = lambda name, shape: st.enter_context(nc.sbuf_tensor("s_" + name, shape, F32))
        dc = sb("dc", [128, 7, 128])
        tw = sb("tw", [128, 4, 128])
        w0_s = sb("w0_s", [33, 64])
        w12_s = sb("w12_s", [128, 2, 64])
        fv = sb("fv", [128, 4])
        fb = sb("fb", [128, 3])
        wo_s = sb("wo_s", [128, 256 // 32, 2, 32])
        hb_s = sb("hb_s", [128, 2])
        hid = sb("hid", [128, SEQ])
        zc = [sb("zc%d" % i, [33, 2, 512]) for i in range(2)]
        ha = sb("ha", [128, 512])
        hq = sb("hq", [128, 512])
        fil = sb("fil", [128, G, 128])
        wn = sb("wn", [128, G, 128])
        xin = sb("xin", [64, G, 128])
        Hre = sb("Hre", [128, G, 128])
        Him = sb("Him", [128, G, 128])
        Xre = sb("Xre", [128, G, 128])
        Xim = sb("Xim", [128, G, 128])
        Bre = [sb("Bre%d" % i, [128, 4, 128]) for i in range(2)]
        Bim = [sb("Bim%d" % i, [128, 4, 128]) for i in range(2)]
        ta = [sb("ta%d" % i, [128, 2, 128]) for i in range(2)]
        tbb = [sb("tbb%d" % i, [128, 2, 128]) for i in range(2)]
        yo = xin
        tq = fil
        pb = [st.enter_context(nc.psum_tensor("pb%d" % i, [128, 512], F32)) for i in range(8)]
        pk = lambda i: ("pb", i)
        P.dma(dc[:], dftc_d[:, :, :], w=["dc"])
        P.dma(tw[:], tw_d[:, :, :], w=["tw"])
        P.dma(w0_s[:], fw0[:, :], w=["w0"])
        P.dma(w12_s[:], fw12[:, :, :], w=["w12"])
        P.dma(fv[:], fvec[:, :], w=["fv"])
        P.dma(wo_s[:], fwout[:, :, :, :], w=["wo"])
        P.dma(hb_s[:], hb[:, :], w=["hb"])
        for l in range(3):
            P.add("dve", lambda e, l=l: e.tensor_tensor(out=fb[:, l:l + 1], in0=fv[:, 1 + l:2 + l], in1=fv[:, 0:1],
                                                        op=ALU.mult), r=["fv"], w=[("fb", l)])
        FBK = [("fb", 0), ("fb", 1), ("fb", 2), "fv"]

        def sin_layer(psrc, psk, l, dst, dstk):
            P.add("dve", lambda e: e.tensor_scalar(out=ha[:], in0=psrc, scalar1=fv[:, 0:1], scalar2=fb[:, l:l + 1],
                                                   op0=ALU.mult, op1=ALU.add), r=[psk] + FBK, w=["ha"])
            P.add("dve", lambda e: e.tensor_scalar(out=hq[:], in0=ha[:], scalar1=1.0 / (2 * math.pi), scalar2=MAGIC,
                                                   op0=ALU.mult, op1=ALU.add), r=["ha"], w=["hq"])
            P.add("dve", lambda e: e.tensor_scalar(out=hq[:], in0=hq[:], scalar1=-MAGIC, scalar2=-2 * math.pi,
                                                   op0=ALU.add, op1=ALU.mult), r=["hq"], w=["hq"])
            P.add("dve", lambda e: e.tensor_tensor(out=ha[:], in0=ha[:], in1=hq[:], op=ALU.add), r=["ha", "hq"],
                  w=["ha"])
            P.add("dve", lambda e: e.tensor_scalar(out=ha[:], in0=ha[:], scalar1=math.pi, scalar2=-math.pi,
                                                   op0=ALU.min, op1=ALU.max), r=["ha"], w=["ha"])
            P.add("act", lambda e: e.activation(out=dst, in_=ha[:], func=AF.Sin), r=["ha"], w=[dstk])

        hx = [sb("hx%d" % i, [128, 512]) for i in range(2)]
        for ch in range(SEQ // 512):
            i = ch % 2
            q0 = ch * 512
            for hf in range(2):
                P.dma(zc[i][:, hf, :], z2T[:, hf * SEQ + q0:hf * SEQ + q0 + 512], w=[("zc", i, hf)])
            for hf in range(2):
                P.mm(pb[6][hf * 64:(hf + 1) * 64, :], w0_s[:, :], zc[i][:, hf, :], start=True, stop=True,
                     r=["w0", ("zc", i, hf)], w=[pk(6)])
            sin_layer(pb[6][:, :], pk(6), 0, hx[0][:], "hx0")
            for hf in range(2):
                rs = slice(hf * 64, (hf + 1) * 64)
                P.mm(pb[7][rs, :], w12_s[rs, 0, :], hx[0][rs, :], start=True, stop=True, r=["w12", "hx0"], w=[pk(7)])
            sin_layer(pb[7][:, :], pk(7), 1, hx[1][:], "hx1")
            for hf in range(2):
                rs = slice(hf * 64, (hf + 1) * 64)
                P.mm(pb[6][rs, :], w12_s[rs, 1, :], hx[1][rs, :], start=True, stop=True, r=["w12", "hx1"], w=[pk(6)])
            sin_layer(pb[6][:, :], pk(6), 2, hid[:, q0:q0 + 512], "hid")

        TC2 = tw[:, 0:2, :]
        TS2 = tw[:, 2:4, :]

        def fwd_fft(src, srck, K, dre, dim, dk):
            for b4 in range(G // 4):
                i = b4 % 2
                for j2 in range(2):
                    bank = pb[j2]
                    for cc in range(2):
                        c = b4 * 4 + j2 * 2 + cc
                        P.mm(bank[:, cc * 256:(cc + 1) * 256], src[0:K, c, :], dc[0:K, 3:5, :], start=True, stop=True,
                             r=[srck, "dc"], w=[pk(j2)])
                    A = bank[:, :].rearrange("p (c r k) -> p c r k", c=2, r=2)
                    cs = slice(j2 * 2, j2 * 2 + 2)
                    bk = ("B", i)
                    P.add("dve", lambda e, A=A, cs=cs, i=i: e.tensor_tensor(out=Bre[i][:, cs, :], in0=A[:, :, 0, :],
                                                                            in1=TC2, op=ALU.mult),
                          r=[pk(j2), "tw"], w=[("Bre", i, j2)])
                    P.add("dve", lambda e, A=A, j2=j2: e.tensor_tensor(out=ta[j2][:], in0=A[:, :, 1, :], in1=TS2,
                                                                        op=ALU.mult), r=[pk(j2), "tw"], w=[("ta", j2)])
                    P.add("pool", lambda e, cs=cs, i=i, j2=j2: e.tensor_tensor(out=Bre[i][:, cs, :], in0=Bre[i][:, cs, :],
                                                                               in1=ta[j2][:], op=ALU.add),
                          r=[("Bre", i, j2), ("ta", j2)], w=[("Bre", i, j2)])
                    P.add("dve", lambda e, A=A, cs=cs, i=i: e.tensor_tensor(out=Bim[i][:, cs, :], in0=A[:, :, 1, :],
                                                                            in1=TC2, op=ALU.mult),
                          r=[pk(j2), "tw"], w=[("Bim", i, j2)])
                    P.add("dve", lambda e, A=A, j2=j2: e.tensor_tensor(out=tbb[j2][:], in0=A[:, :, 0, :], in1=TS2,
                                                                        op=ALU.mult), r=[pk(j2), "tw"], w=[("tbb", j2)])
                    P.add("pool", lambda e, cs=cs, i=i, j2=j2: e.tensor_tensor(out=Bim[i][:, cs, :], in0=Bim[i][:, cs, :],
                                                                               in1=tbb[j2][:], op=ALU.subtract),
                          r=[("Bim", i, j2), ("tbb", j2)], w=[("Bim", i, j2)])
                br = Bre[i][:].rearrange("p c k -> p (c k)")
                bi_ = Bim[i][:].rearrange("p c k -> p (c k)")
                rk = [("Bre", i, 0), ("Bre", i, 1), ("Bim", i, 0), ("Bim", i, 1), "dc"]
                P.mm(pb[2 + i][:, :], dc[:, 1, :], br, start=True, stop=False, r=rk, w=[pk(2 + i)])
                P.mm(pb[2 + i][:, :], dc[:, 2, :], bi_, start=False, stop=True, r=rk, w=[pk(2 + i)])
                P.mm(pb[4 + i][:, :], dc[:, 1, :], bi_, start=True, stop=False, r=rk, w=[pk(4 + i)])
                P.mm(pb[4 + i][:, :], dc[:, 0, :], br, start=False, stop=True, r=rk, w=[pk(4 + i)])
                P.add("act", lambda e, b4=b4, i=i: e.activation(
                    out=dre[:, b4 * 4:(b4 + 1) * 4, :].rearrange("p c k -> p (c k)"), in_=pb[2 + i][:, :], func=AF.Copy),
                    r=[pk(2 + i)], w=[dk + "re"])
                P.add("act", lambda e, b4=b4, i=i: e.activation(
                    out=dim[:, b4 * 4:(b4 + 1) * 4, :].rearrange("p c k -> p (c k)"), in_=pb[4 + i][:, :], func=AF.Copy),
                    r=[pk(4 + i)], w=[dk + "im"])

        for g in range(ngroups):
            cg = slice(g * G, (g + 1) * G)
            P.dma(wn[:], win[:, cg, :], w=["wn"])
            for nb in range(16):
                bank = pb[6 + nb % 2]
                for j in range(8):
                    n2 = nb * 8 + j
                    hf = n2 // 64
                    rs = slice(hf * 64, (hf + 1) * 64)
                    P.mm(bank[:, j * 64:(j + 1) * 64], hid[rs, (n2 % 64) * 128:(n2 % 64 + 1) * 128],
                         wo_s[rs, g, :, :].rearrange("p a c -> p (a c)"), start=True, stop=True,
                         r=["hid", "wo"], w=[pk(6 + nb % 2)])
                bv = bank[:, :].rearrange("p (n a c) -> p a c n", n=8, a=2)
                ns = slice(nb * 8, (nb + 1) * 8)
                P.add("dve", lambda e, bv=bv, ns=ns: e.tensor_tensor(out=fil[0:64, :, ns], in0=bv[0:64, 0, :, :],
                                                                     in1=wn[0:64, :, ns], op=ALU.mult),
                      r=[pk(6 + nb % 2), "wn"], w=["fil"])
                P.add("dve", lambda e, bv=bv, ns=ns: e.tensor_tensor(out=fil[64:128, :, ns], in0=bv[64:128, 1, :, :],
                                                                     in1=wn[64:128, :, ns], op=ALU.mult),
                      r=[pk(6 + nb % 2), "wn"], w=["fil"])
            fwd_fft(fil, "fil", 128, Hre, Him, "H")
            P.dma(xin[:], vvT[cg, :].rearrange("c (a b) -> a c b", b=128), w=["xin"])
            fwd_fft(xin, "xin", 64, Xre, Xim, "X")
            f2 = lambda t: t[:].rearrange("p c k -> p (c k)")
            P.add("dve", lambda e: e.tensor_tensor(out=f2(tq), in0=f2(Xim), in1=f2(Him), op=ALU.mult),
                  r=["Xim", "Him"], w=["fil"])
            P.add("pool", lambda e: e.tensor_tensor(out=f2(Xim), in0=f2(Xim), in1=f2(Hre), op=ALU.mult),
                  r=["Xim", "Hre"], w=["Xim"])
            P.add("dve", lambda e: e.tensor_tensor(out=f2(Him), in0=f2(Xre), in1=f2(Him), op=ALU.mult),
                  r=["Xre", "Him"], w=["Him"])
            P.add("pool", lambda e: e.tensor_tensor(out=f2(Xim), in0=f2(Xim), in1=f2(Him), op=ALU.add),
                  r=["Xim", "Him"], w=["Xim"])
            P.add("dve", lambda e: e.tensor_tensor(out=f2(Xre), in0=f2(Xre), in1=f2(Hre), op=ALU.mult),
                  r=["Xre", "Hre"], w=["Xre"])
            P.add("pool", lambda e: e.tensor_tensor(out=f2(Xre), in0=f2(Xre), in1=f2(tq), op=ALU.subtract),
                  r=["Xre", "fil"], w=["Xre"])
            for b4 in range(G // 4):
                i = b4 % 2
                for j2 in range(2):
                    bank = pb[j2]
                    for cc in range(2):
                        c = b4 * 4 + j2 * 2 + cc
                        P.mm(bank[:, cc * 256:(cc + 1) * 256], Xre[:, c, :], dc[:, 1:3, :], start=True, stop=False,
                             r=["Xre", "Xim", "dc"], w=[pk(j2)])
                        P.mm(bank[:, cc * 256:(cc + 1) * 256], Xim[:, c, :], dc[:, 0:2, :], start=False, stop=True,
                             r=["Xre", "Xim", "dc"], w=[pk(j2)])
                    A = bank[:, :].rearrange("p (c r k) -> p c r k", c=2, r=2)
                    cs = slice(j2 * 2, j2 * 2 + 2)
                    P.add("dve", lambda e, A=A, cs=cs, i=i: e.tensor_tensor(out=Bre[i][:, cs, :], in0=A[:, :, 0, :],
                                                                            in1=TC2, op=ALU.mult),
                          r=[pk(j2), "tw"], w=[("Bre", i, j2)])
                    P.add("dve", lambda e, A=A, j2=j2: e.tensor_tensor(out=ta[j2][:], in0=A[:, :, 1, :], in1=TS2,
                                                                        op=ALU.mult), r=[pk(j2), "tw"], w=[("ta", j2)])
                    P.add("pool", lambda e, cs=cs, i=i, j2=j2: e.tensor_tensor(out=Bre[i][:, cs, :], in0=Bre[i][:, cs, :],
                                                                               in1=ta[j2][:], op=ALU.subtract),
                          r=[("Bre", i, j2), ("ta", j2)], w=[("Bre", i, j2)])
                    P.add("dve", lambda e, A=A, cs=cs, i=i: e.tensor_tensor(out=Bim[i][:, cs, :], in0=A[:, :, 1, :],
                                                                            in1=TC2, op=ALU.mult),
                          r=[pk(j2), "tw"], w=[("Bim", i, j2)])
                    P.add("dve", lambda e, A=A, j2=j2: e.tensor_tensor(out=tbb[j2][:], in0=A[:, :, 0, :], in1=TS2,
                                                                        op=ALU.mult), r=[pk(j2), "tw"], w=[("tbb", j2)])
                    P.add("pool", lambda e, cs=cs, i=i, j2=j2: e.tensor_tensor(out=Bim[i][:, cs, :], in0=Bim[i][:, cs, :],
                                                                               in1=tbb[j2][:], op=ALU.add),
                          r=[("Bim", i, j2), ("tbb", j2)], w=[("Bim", i, j2)])
                br = Bre[i][:].rearrange("p c k -> p (c k)")
                bi_ = Bim[i][:].rearrange("p c k -> p (c k)")
                rk = [("Bre", i, 0), ("Bre", i, 1), ("Bim", i, 0), ("Bim", i, 1), "dc"]
                P.mm(pb[2 + i][0:64, :], dc[:, 5, 0:64], br, start=True, stop=False, r=rk, w=[pk(2 + i)])
                P.mm(pb[2 + i][0:64, :], dc[:, 6, 0:64], bi_, start=False, stop=True, r=rk, w=[pk(2 + i)])
                P.add("act", lambda e, b4=b4, i=i: e.activation(
                    out=yo[:, b4 * 4:(b4 + 1) * 4, :].rearrange("p c k -> p (c k)"), in_=pb[2 + i][0:64, :], func=AF.Copy),
                    r=[pk(2 + i)], w=["xin"])
            P.dma(cvT[cg, :].rearrange("c (a b) -> a c b", b=128), yo[:], r=["xin"], w=[("cvT", g)], q="act")
        RB = 512
        cvb = [sb("cvb%d" % i, [128, RB]) for i in range(2)]
        vvb = [sb("vvb%d" % i, [128, RB]) for i in range(2)]
        x0b = [sb("x0b%d" % i, [128, RB]) for i in range(2)]
        n = 0
        for h in range(2):
            if (h + 1) * 128 > ngroups * G:
                break
            rs = slice(h * 128, (h + 1) * 128)
            for b in range(SEQ // RB):
                i = n % 2
                n += 1
                cs = slice(b * RB, (b + 1) * RB)
                P.dma(cvb[i][:], cvT[rs, cs], r=[("cvT", g) for g in range(h * 4, h * 4 + 4)], w=[("cvb", i)])
                P.dma(vvb[i][:], vvT[rs, cs], w=[("vvb", i)])
                P.dma(x0b[i][:], x0T[rs, cs], w=[("x0b", i)])
                P.add("dve", lambda e, i=i, h=h: e.scalar_tensor_tensor(
                    out=cvb[i][:], in0=vvb[i][:], scalar=hb_s[:, h:h + 1], in1=cvb[i][:], op0=ALU.mult, op1=ALU.add),
                    r=[("cvb", i), ("vvb", i), "hb"], w=[("cvb", i)])
                P.add("pool", lambda e, i=i: e.tensor_tensor(out=cvb[i][:], in0=cvb[i][:], in1=x0b[i][:], op=ALU.mult),
                      r=[("cvb", i), ("x0b", i)], w=[("cvb", i)])
                P.dma(zT[rs, cs], cvb[i][:], r=[("cvb", i)], q="act")
        P.emit()
    return nc


def k4b_inputs(inp, core, consts, vvT, x0T):
    C = slice(core * 256, (core + 1) * 256)
    dup = lambda a: np.concatenate([a, a], 0)
    fw12 = dup(np.stack([inp["hy_f_w1"][0], inp["hy_f_w2"][0]], 1))
    fvec = dup(np.stack([inp["hy_f_freq"][0], inp["hy_f_b0"][0], inp["hy_f_b1"][0], inp["hy_f_b2"][0]], 1))
    wo = inp["hy_f_wout"][0]
    wof = wo[:, :D][:, C].reshape(64, 8, 32)
    wob = wo[:, D:][:, C].reshape(64, 8, 32)
    fwout = dup(np.stack([wof, wob], 2))
    hbv = inp["hy_bias"][0][C].reshape(2, 128).T
    c = lambda x: np.ascontiguousarray(x.astype(np.float32))
    m = {"vvT": vvT, "x0T": x0T, "hb": c(hbv), "fw0": c(inp["hy_f_w0"][0]), "fw12": c(fw12), "fvec": c(fvec),
         "fwout": c(fwout)}
    m["dftc"] = consts["dftc"]
    m["tw"] = consts["tw"]
    m["z2T"] = consts["z2T"]
    m["win"] = consts["win"]
    return m


def k3_inputs(xT_c, zT_c, wo, ob, mods_l, norm2_g, w1, w3, w2, final_g):
    sp = lambda i: mods_l[i * D:(i + 1) * D]
    vs = np.stack([ob, sp(2), norm2_g, sp(4), sp(3), sp(5), final_g])
    return {"xT": xT_c, "zT": zT_c, "wo": wo, "w1": w1, "w3": w3, "w2": w2, "vecs": vec_layout(vs)}


def kernel(**inp):
    inp = {k: np.asarray(v) for k, v in inp.items()}
    c_ = np.ascontiguousarray
    NTc = SEQ // NCORES
    mods = run_k0(inp)
    nc1 = build_k1()
    r1 = _run(nc1, [k1_inputs(inp, mods, c) for c in range(NCORES)])
    nc2 = build_k2c()
    cc = k2c_consts()
    m2 = []
    for c in range(NCORES):
        m = {"tab": r1[c]["tab"].reshape(TT, 4, 2, NTAB * 64), "vT": r1[c]["vT"], "gT": r1[c]["gT"],
             "rkT": r1[c]["rkT"]}
        m.update(k2_consts(inp, c))
        m.update(cc)
        m2.append(m)
    r2 = _run(nc2, m2)
    zT = np.concatenate([r2[c]["zT"] for c in range(NCORES)], axis=0)
    xT = c_(inp["x"][0].T)
    nc3 = build_k3(NTc, False)
    zeros = np.zeros(D, np.float32)
    m3 = []
    for c in range(NCORES):
        ts = slice(c * NTc, (c + 1) * NTc)
        m3.append(k3_inputs(c_(xT[:, ts]), c_(zT[:, ts]), inp["rw_w_o"][0], zeros, mods[0], inp["norm2_g"][0],
                            inp["ffn_w1"][0], inp["ffn_w3"][0], inp["ffn_w2"][0], inp["final_g"]))
    r3 = _run(nc3, m3)
    x1T = c_(np.concatenate([r3[c]["oT"] for c in range(NCORES)], axis=1))
    nc4a = build_k4a()
    r4a = _run(nc4a, [k4a_inputs(inp, mods, x1T, c) for c in range(NCORES)])
    nc4b = build_k4b()
    r4b = _run(nc4b, [k4b_inputs(inp, c, hy_consts(c), r4a[c]["vvT"], r4a[c]["x0T"]) for c in range(NCORES)])
    z1T = np.concatenate([r4b[c]["zT"] for c in range(NCORES)], axis=0)
    nc5 = build_k3(NTc, True)
    m5 = []
    for c in range(NCORES):
        ts = slice(c * NTc, (c + 1) * NTc)
        m5.append(k3_inputs(c_(x1T[:, ts]), c_(z1T[:, ts]), inp["hy_out_w"][0], inp["hy_out_b"][0], mods[2],
                            inp["norm2_g"][1], inp["ffn_w1"][1], inp["ffn_w3"][1], inp["ffn_w2"][1], inp["final_g"]))
    r5 = _run(nc5, m5)
    oT = np.concatenate([r5[c]["oT"] for c in range(NCORES)], axis=1)
    return c_(oT.T).reshape(1, SEQ, D).astype(np.float32)


def k2c_consts():
    i = np.arange(64)
    iu = (i[:, None] <= i[None, :]).astype(np.float32)
    su = (i[:, None] < i[None, :]).astype(np.float32)
    z = np.zeros((64, 64), np.float32)
    bd = lambda m: np.block([[m, z], [z, m]])
    mats = [bd(iu), bd(su), bd(su.T), bd(iu.T), bd(su.T), bd(su), bd(np.ones((64, 64), np.float32)),
            np.eye(128, dtype=np.float32)]
    cm = np.stack(mats, 1).astype(np.float32)
    sel = np.zeros((128, 2, 64), np.float32)
    sel[:64, 0, :] = 1.0
    sel[64:, 1, :] = 1.0
    i2 = np.stack([np.eye(64, dtype=np.float32)] * 2, 1)
    return {"cmat": np.ascontiguousarray(cm), "csel": sel, "ci2": np.ascontiguousarray(i2)}


def build_k2c(seq=SEQ):
    T = CTX + seq
    C = 64
    STOP = int(os.environ.get('K2C_STOP', '9'))
    SKIP = os.environ.get('K2C_SKIP', '')
    SEQL = int(os.environ.get('K2C_SEQ', '9'))
    YC = 512 if seq >= 512 else seq
    nc = bass.Bass("TRN2", target_bir_lowering=False)
    din = lambda n, s: nc.dram_tensor(n, s, F32, kind="ExternalInput").ap()
    dout = lambda n, s: nc.dram_tensor(n, s, F32, kind="ExternalOutput").ap()
    tab = din("tab", [TT, 4, 2, NTAB * 64])
    vT = din("vT", [256, TT])
    gT = din("gT", [256, SEQ])
    rkT = din("rkT", [4, TT])
    lnx = din("lnx", [128, 4])
    blk = din("blk", [128, 128])
    esel = din("esel", [4, 2 * 128])
    cmat = din("cmat", [128, 8, 128])
    csel = din("csel", [128, 2, 64])
    ci2 = din("ci2", [64, 2, 64])
    yT = dout("yT", [2, 256, seq])
    zT = dout("zT", [256, seq])
    P = Prog(nc)
    import contextlib
    with contextlib.ExitStack() as st:
        sb = lambda name, shape: st.enter_context(nc.sbuf_tensor(name, shape, F32))
        chains = [(p, d) for d in range(2) for p in range(2)]
        cm = sb("cm", [128, 8, 128])
        sl = sb("sl", [128, 2, 64])
        i2 = sb("i2", [64, 2, 64])
        P.dma(cm[:], cmat[:, :, :], w=["cm"])
        P.dma(sl[:], csel[:, :, :], w=["sl"])
        P.dma(i2[:], ci2[:, :, :], w=["i2"])
        vt = [sb("vt%d" % p, [128, T]) for p in range(2)]
        for p in range(2):
            P.dma(vt[p][:], vT[p * 128:(p + 1) * 128, 0:T], w=[("vt", p)])
        IDENT = cm[:, 7, :]
        ONESBD = cm[:, 6, :]
        INC = lambda d: cm[:, 3 * d, :]
        STR = lambda d: cm[:, 3 * d + 1, :]
        STRT = lambda d: cm[:, 3 * d + 2, :]

        class Ring:
            def __init__(self, name, shape, n):
                self.name, self.n, self.i = name, n, 0
                self.t = [sb("%s_%d" % (name, j), shape) for j in range(n)]

            def get(self):
                j = self.i % self.n
                self.i += 1
                return self.t[j], (self.name, j)

        pbanks = [st.enter_context(nc.psum_tensor("pq%d" % i, [128, 512], F32)) for i in range(8)]
        pctr = [0]

        def pslot():
            j = pctr[0] % 32
            pctr[0] += 1
            b, q = j % 8, j // 8
            return pbanks[b][:, q * 128:(q + 1) * 128], ("ps", b)

        NR = 8
        R_tbc = Ring("tbc", [128, NTAB, 64], 8)
        R_tm = Ring("tm", [128, 64], 44)
        R_fm = {n: Ring(n, [64, 128], NR) for n in ("Af", "Bf", "Kf", "Rf")}
        R_tp = {n: Ring(n, [128, 64], NR) for n in ("BTp", "KTp")}
        R_bd = {n: Ring(n, [128, 128], NR) for n in ("Mka", "Nbr", "Nkr", "Tm")}
        R_D = Ring("Dm", [64, 2, 64], NR)
        R_A = Ring("An", [128, 128], 12)
        R_AT = Ring("AnT", [128, 128], 12)
        R_X = Ring("XT", [128, 64], 8)
        R_U = Ring("UT", [128, 64], 8)
        R_Y = Ring("Ysb", [128, 64], 8)
        Sbuf = {c: [sb("S_%d_%d_%d" % (c[0], c[1], j), [64, 2, 64]) for j in range(2)] for c in chains}
        yb = {(p, d, s): sb("yb_%d_%d_%d" % (p, d, s), [128, YC]) for (p, d) in chains for s in range(2)}
        for c in chains:
            P.add("pool", lambda e, c=c: e.memset(Sbuf[c][0][:], 0.0), w=[("S", c, 0)])
        order = {0: list(range(T)), 1: list(range(CTX - 1, -1, -1)) + list(range(T - 1, CTX - 1, -1))}
        nchunks = T // C
        CM = ["cm"]

        def evac(eng, out, in_, r, w):
            if eng == "act":
                P.add("act", lambda e: e.activation(out=out, in_=in_, func=AF.Copy), r=r, w=w)
            else:
                P.add(eng, lambda e: e.tensor_scalar(out=out, in0=in_, scalar1=1.0, scalar2=None, op0=ALU.mult), r=r, w=w)

        def tt_op(eng, out, in0, in1, op, r, w):
            P.add(eng, lambda e: e.tensor_tensor(out=out, in0=in0, in1=in1, op=op), r=r, w=w)

        pre = {}

        def precompute(ci):
            U = {}
            for c in chains:
                p, d = c
                toks = order[d][ci * C:(ci + 1) * C]
                lo = min(toks)
                tbc, tk = R_tbc.get()
                for h in range(2):
                    P.dma(tbc[h * 64:(h + 1) * 64, :, :].rearrange("p a k -> p (a k)"), tab[lo:lo + C, 2 * p + h, d, :],
                          w=[(tk, h)])
                U[c] = dict(tbc=tbc, tk=[(tk, 0), (tk, 1)], lo=lo)
            for c in chains:
                u = U[c]
                d = c[1]
                tbc, tk = u["tbc"], u["tk"]
                lw, lwk = R_tm.get()
                P.add("act", lambda e, lw=lw, tbc=tbc: e.activation(out=lw[:], in_=tbc[:, 1, :], func=AF.Ln), r=tk, w=[lwk])
                pcl, pclk = pslot()
                P.mm(pcl[:, 0:64], INC(d), lw[:], True, True, r=CM + [lwk], w=[pclk])
                ptot, ptotk = pslot()
                P.mm(ptot[:, 0:64], ONESBD, lw[:], True, True, r=CM + [lwk], w=[ptotk])
                ptx, ptxk = pslot()
                for h in range(2):
                    P.mm(ptx[0:64, h * 64:(h + 1) * 64], sl[:, h, :], lw[:], True, True, r=["sl", lwk], w=[ptxk])
                u.update(lw=lw, lwk=lwk, pcl=pcl, pclk=pclk, ptot=ptot, ptotk=ptotk, ptx=ptx, ptxk=ptxk)
            if STOP <= 1:
                pre[ci] = U
                return
            for c in chains:
                u = U[c]
                tbc, tk = u["tbc"], u["tk"]
                Pin, Pink = R_tm.get()
                iP, iPk = R_tm.get()
                e1, e1k = R_tm.get()
                PC, PCk = R_tm.get()
                cls, clsk = R_tm.get()
                evac("act", cls[:], u["pcl"][:, 0:64], [u["pclk"]], [clsk])
                P.add("act", lambda e, Pin=Pin, cls=cls: e.activation(out=Pin[:], in_=cls[:], func=AF.Exp),
                      r=[clsk], w=[Pink])
                P.add("act", lambda e, iP=iP, cls=cls: e.activation(out=iP[:], in_=cls[:], func=AF.Exp, scale=-1.0),
                      r=[clsk], w=[iPk])
                tt_op("dve", e1[:], cls[:], u["lw"][:], ALU.subtract, [clsk, u["lwk"]], [e1k])
                P.add("act", lambda e, e1=e1: e.activation(out=e1[:], in_=e1[:], func=AF.Exp), r=[e1k], w=[e1k])
                P.add("act", lambda e, PC=PC, u=u: e.activation(out=PC[:], in_=u["ptot"][:, 0:64], func=AF.Exp),
                      r=[u["ptotk"]], w=[PCk])
                Dm, Dk = R_D.get()
                if 'D' not in SKIP:
                    P.add("act", lambda e, Dm=Dm, u=u: e.activation(out=Dm[:].rearrange("p h k -> p (h k)"),
                                                                    in_=u["ptx"][0:64, :], func=AF.Exp),
                          r=[u["ptxk"]], w=[Dk])
                if 'E' not in SKIP:
                    tt_op("pool", Dm[:], Dm[:], i2[:], ALU.mult, [Dk, "i2"], [Dk])
                tt_op("pool", PC[:], PC[:], iP[:], ALU.mult, [PCk, iPk], [PCk])
                AT, ATk = R_tm.get()
                BT, BTk = R_tm.get()
                KT, KTk = R_tm.get()
                RT, RTk = R_tm.get()
                BTp, BTpk = R_tp["BTp"].get()
                KTp, KTpk = R_tp["KTp"].get()
                tt_op("dve", AT[:], tbc[:, 0, :], e1[:], ALU.mult, tk + [e1k], [ATk])
                tt_op("pool", BT[:], tbc[:, 2, :], iP[:], ALU.mult, tk + [iPk], [BTk])
                tt_op("dve", KT[:], tbc[:, 3, :], iP[:], ALU.mult, tk + [iPk], [KTk])
                tt_op("pool", RT[:], tbc[:, 4, :], Pin[:], ALU.mult, tk + [Pink], [RTk])
                tt_op("pool", BTp[:], tbc[:, 2, :], PC[:], ALU.mult, tk + [PCk], [BTpk])
                tt_op("pool", KTp[:], tbc[:, 3, :], PC[:], ALU.mult, tk + [PCk], [KTpk])
                u.update(Dm=Dm, Dk=Dk, BTp=BTp, BTpk=BTpk, KTp=KTp, KTpk=KTpk)
                for nm, src, srck, eng in (("Af", AT, ATk, "act"), ("Bf", BT, BTk, "dve"), ("Kf", KT, KTk, "act"),
                                           ("Rf", RT, RTk, "dve")):
                    ps, psk = pslot()
                    dst, dstk = R_fm[nm].get()
                    if 'T' not in SKIP:
                        P.mm(ps[0:64, :], src[:], IDENT, True, True, r=CM + [srck], w=[psk])
                        if 'V' not in SKIP:
                            evac(eng, dst[:], ps[0:64, :], [psk], [dstk])
                    u[nm] = dst
                    u[nm + "k"] = dstk
            if STOP <= 2:
                pre[ci] = U
                return
            for c in chains:
                u = U[c]
                d = c[1]
                grams = (("A0", "Bf", "Af", STR(d), R_A), ("A0T", "Af", "Bf", STRT(d), R_AT),
                         ("Mka", "Kf", "Af", STR(d), R_bd["Mka"]), ("Nbr", "Bf", "Rf", INC(d), R_bd["Nbr"]),
                         ("Nkr", "Kf", "Rf", INC(d), R_bd["Nkr"]))
                for nm, l, r_, mask, ring in grams:
                    ps, psk = pslot()
                    P.mm(ps[:, :], u[l][:], u[r_][:], True, True, r=[u[l + "k"], u[r_ + "k"]], w=[psk])
                    dst, dstk = ring.get()
                    tt_op("dve", dst[:], ps[:, :], mask, ALU.mult, [psk] + CM, [dstk])
                    u[nm] = dst
                    u[nm + "k"] = dstk
                Tm, Tmk = R_bd["Tm"].get()
                tt_op("pool", Tm[:], u["A0"][:], IDENT, ALU.add, [u["A0k"]] + CM, [Tmk])
                u.update(Tm=Tm, Tmk=Tmk, A=u["A0"], Ak=u["A0k"], AT=u["A0T"], ATk_=u["A0Tk"])
            if STOP <= 3:
                pre[ci] = U
                return
            for n in range(5):
                for c in chains:
                    u = U[c]
                    ps2, ps2k = pslot()
                    P.mm(ps2[:, :], u["A"][:], u["AT"][:], True, True, r=[u["Ak"], u["ATk_"]], w=[ps2k])
                    if n < 4:
                        ps1, ps1k = pslot()
                        P.mm(ps1[:, :], u["AT"][:], u["A"][:], True, True, r=[u["Ak"], u["ATk_"]], w=[ps1k])
                        An, Ank = R_A.get()
                        evac("act", An[:], ps1[:, :], [ps1k], [Ank])
                    AnT, AnTk = R_AT.get()
                    evac("dve", AnT[:], ps2[:, :], [ps2k], [AnTk])
                    if n < 4:
                        u.update(A=An, Ak=Ank)
                    u.update(AT=AnT, ATk_=AnTk)
                for c in chains:
                    u = U[c]
                    ps, psk = pslot()
                    P.mm(ps[:, :], u["AT"][:], u["Tm"][:], True, True, r=[u["ATk_"], u["Tmk"]], w=[psk])
                    tt_op("dve", u["Tm"][:], u["Tm"][:], ps[:, :], ALU.add, [psk, u["Tmk"]], [u["Tmk"]])
            pre[ci] = U

        ycount = {c: 0 for c in chains}

        def sequential(ci):
            U = pre.pop(ci)
            if STOP <= 4:
                return
            par = ci % 2
            for c in chains:
                u = U[c]
                S0 = Sbuf[c][par]
                u["S0"], u["S0k"] = S0, ("S", c, par)
                V = u["tbc"][:, 5, :]
                ps, psk = pslot()
                P.mm(ps[0:64, 0:64], u["Af"][:, 0:64], S0[:, 0, :], True, False, r=[u["Afk"], u["S0k"]], w=[psk])
                P.mm(ps[64:128, 0:64], u["Af"][:, 64:128], S0[:, 1, :], True, False, r=[u["Afk"], u["S0k"]], w=[psk])
                P.mm(ps[:, 0:64], u["Mka"][:], V, False, True, r=[u["Mkak"]] + u["tk"], w=[psk])
                XT, XTk = R_X.get()
                evac("act", XT[:], ps[:, 0:64], [psk], [XTk])
                u.update(XT=XT, XTk=XTk)
            if SEQL <= 1:
                return
            for c in chains:
                u = U[c]
                ps, psk = pslot()
                P.mm(ps[:, 0:64], u["Tm"][:], u["XT"][:], True, True, r=[u["Tmk"], u["XTk"]], w=[psk])
                UT, UTk = R_U.get()
                evac("dve", UT[:], ps[:, 0:64], [psk], [UTk])
                u.update(UT=UT, UTk=UTk)
            if SEQL <= 2:
                return
            for c in chains:
                u = U[c]
                p, d = c
                S0 = u["S0"]
                V = u["tbc"][:, 5, :]
                ps, psk = pslot()
                hs0, hs1 = slice(0, 64), slice(64, 128)
                P.mm(ps[0:64, hs0], u["Dm"][:, 0, :], S0[:, 0, :], True, False, r=[u["Dk"], u["S0k"]], w=[psk])
                P.mm(ps[0:64, hs0], u["BTp"][hs0, :], u["UT"][hs0, :], False, False, r=[u["BTpk"], u["UTk"]], w=[psk])
                P.mm(ps[0:64, hs0], u["KTp"][hs0, :], V[hs0, :], False, True, r=[u["KTpk"]] + u["tk"], w=[psk])
                P.mm(ps[0:64, hs1], u["Dm"][:, 1, :], S0[:, 1, :], True, True, r=[u["Dk"], u["S0k"]], w=[psk])
                ps2, ps2k = pslot()
                P.mm(ps2[0:64, 0:64], u["BTp"][hs1, :], u["UT"][hs1, :], True, False, r=[u["BTpk"], u["UTk"]], w=[ps2k])
                P.mm(ps2[0:64, 0:64], u["KTp"][hs1, :], V[hs1, :], False, True, r=[u["KTpk"]] + u["tk"], w=[ps2k])
                Sn = Sbuf[c][1 - (ci % 2)]
                snk = ("S", c, 1 - (ci % 2))
                evac("act", Sn[:].rearrange("p h v -> p (h v)"), ps[0:64, :], [psk], [snk])
                tt_op("dve", Sn[:, 1, :], Sn[:, 1, :], ps2[0:64, 0:64], ALU.add, [snk, ps2k], [snk])
                if u["lo"] < CTX or SEQL <= 3:
                    continue
                ps, psk = pslot()
                P.mm(ps[0:64, 0:64], u["Rf"][:, 0:64], S0[:, 0, :], True, False, r=[u["Rfk"], u["S0k"]], w=[psk])
                P.mm(ps[64:128, 0:64], u["Rf"][:, 64:128], S0[:, 1, :], True, False, r=[u["Rfk"], u["S0k"]], w=[psk])
                P.mm(ps[:, 0:64], u["Nbr"][:], u["UT"][:], False, False, r=[u["Nbrk"], u["UTk"]], w=[psk])
                P.mm(ps[:, 0:64], u["Nkr"][:], V, False, True, r=[u["Nkrk"]] + u["tk"], w=[psk])
                Ys, Ysk = R_Y.get()
                evac("dve", Ys[:], ps[:, 0:64], [psk], [Ysk])
                pt, ptk = pslot()
                for h in range(2):
                    hs = slice(h * 64, (h + 1) * 64)
                    P.mm(pt[hs, 0:64], Ys[hs, :], cm[hs, 7, hs], True, True, r=[Ysk] + CM, w=[ptk])
                lt0 = u["lo"] - CTX
                ys = (lt0 // YC) % 2
                evac("act", yb[(p, d, ys)][:, lt0 % YC:lt0 % YC + C], pt[:, 0:64], [ptk], [("yb", p, d, ys)])
                ycount[c] += C
                if ycount[c] % YC == 0:
                    b0 = (lt0 // YC) * YC
                    P.dma(yT[d, p * 128:(p + 1) * 128, b0:b0 + YC], yb[(p, d, ys)][:], r=[("yb", p, d, ys)],
                          w=[("yT", d, p, b0)], q="act")

        precompute(0)
        for ci in range(nchunks):
            if ci + 1 < nchunks:
                precompute(ci + 1)
            sequential(ci)
        RB = YC
        lx = sb("lx", [128, 4])
        bk = sb("bk", [128, 128])
        es = sb("es", [4, 2 * 128])
        geps = sb("geps", [128, 1])
        P.dma(lx[:], lnx[:, :], w=["lx"])
        P.dma(bk[:], blk[:, :], w=["bk"])
        P.dma(es[:], esel[:, :], w=["es"])
        P.add("pool", lambda e: e.memset(geps[:], GN_EPS), w=["geps"])
        y0 = [sb("y0_%d" % i, [128, RB]) for i in range(2)]
        y1 = [sb("y1_%d" % i, [128, RB]) for i in range(2)]
        gg = [sb("gg_%d" % i, [128, RB]) for i in range(2)]
        rks = [sb("rks_%d" % i, [4, RB]) for i in range(2)]
        cen = sb("cen", [128, RB])
        sqb = sb("sqb", [128, RB])
        rsd = sb("rsd", [128, RB])
        bon = sb("bon", [128, RB])
        zz = [sb("zz_%d" % i, [128, RB]) for i in range(2)]
        pm, pv, pr = pbanks[0], pbanks[1], pbanks[2]
        PSK = [("ps", 0)] * 4 + [("ps", 1)] * 4 + [("ps", 2)] * 4
        n = 0
        for p in range(2 if STOP > 5 else 0):
            rowsl = slice(p * 128, (p + 1) * 128)
            for b in range(seq // RB):
                i = n % 2
                n += 1
                cs = slice(b * RB, (b + 1) * RB)
                P.dma(y0[i][:], yT[0, rowsl, cs], r=[("yT", 0, p, b * RB)], w=[("y0", i)])
                P.dma(y1[i][:], yT[1, rowsl, cs], r=[("yT", 1, p, b * RB)], w=[("y1", i)])
                P.dma(gg[i][:], gT[rowsl, cs], w=[("gg", i)])
                P.dma(rks[i][:], rkT[:, CTX + b * RB:CTX + (b + 1) * RB], w=[("rks", i)])
                tt_op("pool", y0[i][:], y0[i][:], y1[i][:], ALU.add, [("y0", i), ("y1", i)], [("y0", i)])
                P.mm(pm[:, 0:RB], bk[:], y0[i][:], True, True, r=["bk", ("y0", i)], w=PSK[0:4])
                tt_op("dve", cen[:], y0[i][:], pm[:, 0:RB], ALU.subtract, [("y0", i)] + PSK[0:4], ["cen"])
                tt_op("pool", sqb[:], cen[:], cen[:], ALU.mult, ["cen"], ["sqb"])
                P.mm(pv[:, 0:RB], bk[:], sqb[:], True, True, r=["bk", "sqb"], w=PSK[4:8])
                P.add("act", lambda e: e.activation(out=rsd[:], in_=pv[:, 0:RB], func=AF.Sqrt, bias=geps[:, 0:1]),
                      r=PSK[4:8] + ["geps"], w=["rsd"])
                P.add("dve", lambda e: e.reciprocal(out=rsd[:], in_=rsd[:]), r=["rsd"], w=["rsd"])
                tt_op("dve", cen[:], cen[:], rsd[:], ALU.mult, ["cen", "rsd"], ["cen"])
                P.add("act", lambda e, p=p: e.activation(out=cen[:], in_=cen[:], func=AF.Identity,
                                                         scale=lx[:, 2 * p:2 * p + 1], bias=lx[:, 2 * p + 1:2 * p + 2]),
                      r=["cen", "lx"], w=["cen"])
                P.mm(pr[:, 0:RB], es[:, p * 128:(p + 1) * 128], rks[i][:], True, True, r=["es", ("rks", i)], w=PSK[8:12])
                tt_op("dve", bon[:], vt[p][:, CTX + b * RB:CTX + (b + 1) * RB], pr[:, 0:RB], ALU.mult,
                      PSK[8:12] + [("vt", p)], ["bon"])
                tt_op("pool", bon[:], bon[:], cen[:], ALU.add, ["bon", "cen"], ["bon"])
                tt_op("pool", zz[i][:], bon[:], gg[i][:], ALU.mult, ["bon", ("gg", i)], [("zz", i)])
                P.dma(zT[rowsl, cs], zz[i][:], r=[("zz", i)], q="act")
        P.emit()
    return nc
```
